# Optimizing a Trainium2 kernel written in Bass

```python
import math
import jax, jax.numpy as jnp
from jax import lax
import numpy as np

D_MODEL = 2048
BATCH = 4
SEQ = 8192
DEPTH = 1

CHUNK = 64
N_META = 16
Q_BLOCK = 128
N_DIFF_HEADS = 4
DIFF_HEAD_DIM = 128
DIFF_WIDTH = N_DIFF_HEADS * 2 * DIFF_HEAD_DIM
N_FOX_HEADS = 8
FOX_HEAD_DIM = 128
FOX_WIDTH = N_FOX_HEADS * FOX_HEAD_DIM
D_FF = -(-8 * D_MODEL // (3 * 256)) * 256
RMS_EPS = 1e-6
SUBLN_EPS = 1e-5
SPLIT_SIZES = [DIFF_WIDTH, DIFF_WIDTH, DIFF_WIDTH, FOX_WIDTH, FOX_WIDTH, FOX_WIDTH,
               N_FOX_HEADS, D_MODEL, D_MODEL]
SPLIT_OFFSETS = np.cumsum(SPLIT_SIZES)[:-1].tolist()
N_IN_COLS = int(sum(SPLIT_SIZES))

kernel_name = "hybrid_diffattn_fox_gated_block"


def _rms_norm(x, g, eps):
    xf = x.astype(jnp.float32)
    y = xf * lax.rsqrt(jnp.mean(xf * xf, axis=-1, keepdims=True) + eps)
    return (y * g.astype(jnp.float32)).astype(x.dtype)


def _chunk_id(pos):
    return jnp.where(pos < N_META, 0, 1 + (pos - N_META) // CHUNK)


def _alibi_slopes(n):
    return jnp.asarray([2.0 ** (-8.0 * (i + 1) / n) for i in range(n)], dtype=jnp.float32)


def _diff_attention(q, k, v, lam, slopes, pos):
    b, l = q.shape[0], q.shape[1]
    n_blk = l // Q_BLOCK
    key_chunk = _chunk_id(pos)
    q_blk = jnp.moveaxis(q.reshape(b, n_blk, Q_BLOCK, N_DIFF_HEADS, 2, DIFF_HEAD_DIM), 1, 0)
    pos_blk = pos.reshape(n_blk, Q_BLOCK)
    scale = DIFF_HEAD_DIM ** -0.5

    def one_block(args):
        qi, pi = args
        s = jnp.einsum("bqhcd,bkhcd->bhcqk", qi, k).astype(jnp.float32) * scale
        dist = jnp.abs(pi[:, None] - pos[None, :]).astype(jnp.float32)
        s = s - slopes[:, None, None, None] * dist
        visible = key_chunk[None, :] <= _chunk_id(pi)[:, None]
        p = jax.nn.softmax(jnp.where(visible, s, -jnp.inf), axis=-1)
        w = p[:, :, 0] - lam * p[:, :, 1]
        return jnp.einsum("bhqk,bkhe->bqhe", w.astype(v.dtype), v)

    o = lax.map(one_block, (q_blk, pos_blk))
    return jnp.moveaxis(o, 0, 1).reshape(b, l, N_DIFF_HEADS, 2 * DIFF_HEAD_DIM)


def _forgetting_attention(q, k, v, log_f, pos):
    b, l = q.shape[0], q.shape[1]
    n_blk = l // Q_BLOCK
    cum = jnp.moveaxis(jnp.cumsum(log_f, axis=1), 2, 1)
    q_blk = jnp.moveaxis(q.reshape(b, n_blk, Q_BLOCK, N_FOX_HEADS, FOX_HEAD_DIM), 1, 0)
    c_blk = jnp.moveaxis(cum.reshape(b, N_FOX_HEADS, n_blk, Q_BLOCK), 2, 0)
    pos_blk = pos.reshape(n_blk, Q_BLOCK)
    scale = FOX_HEAD_DIM ** -0.5

    def one_block(args):
        qi, ci, pi = args
        s = (jnp.einsum("bqhd,bkhd->bhqk", qi, k).astype(jnp.float32) * scale
             + (ci[:, :, :, None] - cum[:, :, None, :]))
        visible = pos[None, :] <= pi[:, None]
        p = jax.nn.softmax(jnp.where(visible, s, -jnp.inf), axis=-1)
        return jnp.einsum("bhqk,bkhd->bqhd", p.astype(v.dtype), v)

    o = lax.map(one_block, (q_blk, c_blk, pos_blk))
    return jnp.moveaxis(o, 0, 1).reshape(b, l, N_FOX_HEADS * FOX_HEAD_DIM)


def _hybrid_layer(h, pos, slopes, lambda_init, g_mix, w_in, lambda_q1, lambda_k1, lambda_q2,
                  lambda_k2, g_subln, b_f, w_branch_a, w_branch_f, w_out, g_ffn, w_gate,
                  w_up, w_down):
    b, l, _ = h.shape
    u = _rms_norm(h, g_mix, RMS_EPS)
    z = u @ w_in
    qa, ka, va, qf, kf, vf, zf, za_gate, zf_gate = jnp.split(z, SPLIT_OFFSETS, axis=-1)

    lam = (jnp.exp(jnp.sum(lambda_q1.astype(jnp.float32) * lambda_k1.astype(jnp.float32)))
           - jnp.exp(jnp.sum(lambda_q2.astype(jnp.float32) * lambda_k2.astype(jnp.float32)))
           + lambda_init)
    oa = _diff_attention(qa.reshape(b, l, N_DIFF_HEADS, 2, DIFF_HEAD_DIM),
                         ka.reshape(b, l, N_DIFF_HEADS, 2, DIFF_HEAD_DIM),
                         va.reshape(b, l, N_DIFF_HEADS, 2 * DIFF_HEAD_DIM),
                         lam, slopes, pos)
    oa = (_rms_norm(oa, g_subln, SUBLN_EPS) * (1.0 - lambda_init)).reshape(b, l, DIFF_WIDTH)

    log_f = jax.nn.log_sigmoid(zf.astype(jnp.float32) + b_f.astype(jnp.float32))
    of = _forgetting_attention(qf.reshape(b, l, N_FOX_HEADS, FOX_HEAD_DIM),
                               kf.reshape(b, l, N_FOX_HEADS, FOX_HEAD_DIM),
                               vf.reshape(b, l, N_FOX_HEADS, FOX_HEAD_DIM),
                               log_f, pos)

    merged = jax.nn.sigmoid(za_gate) * (oa @ w_branch_a) + jax.nn.sigmoid(zf_gate) * (of @ w_branch_f)
    h = h + merged @ w_out

    v2 = _rms_norm(h, g_ffn, RMS_EPS)
    h = h + (jax.nn.silu(v2 @ w_gate) * (v2 @ w_up)) @ w_down
    return h


def setup_inputs(seed: int = 0) -> dict:
    key = jax.random.key(seed)
    ks = jax.random.split(key, 20)
    f32 = jnp.float32

    def nrm(k, shape, scale):
        return jax.random.normal(k, shape, f32) * scale

    return {
        "x": nrm(ks[0], (BATCH, SEQ, D_MODEL), 1.0),
        "meta": nrm(ks[1], (N_META, D_MODEL), 1.0),
        "g_mix": 1.0 + nrm(ks[2], (DEPTH, D_MODEL), 0.02),
        "w_in": nrm(ks[3], (DEPTH, D_MODEL, N_IN_COLS), D_MODEL ** -0.5),
        "lambda_q1": nrm(ks[4], (DEPTH, DIFF_HEAD_DIM), 0.1),
        "lambda_k1": nrm(ks[5], (DEPTH, DIFF_HEAD_DIM), 0.1),
        "lambda_q2": nrm(ks[6], (DEPTH, DIFF_HEAD_DIM), 0.1),
        "lambda_k2": nrm(ks[7], (DEPTH, DIFF_HEAD_DIM), 0.1),
        "g_subln": 1.0 + nrm(ks[8], (DEPTH, 2 * DIFF_HEAD_DIM), 0.02),
        "b_f": jax.random.uniform(ks[9], (DEPTH, N_FOX_HEADS), f32, 1.0, 4.0),
        "w_branch_a": nrm(ks[10], (DEPTH, DIFF_WIDTH, D_MODEL), DIFF_WIDTH ** -0.5),
        "w_branch_f": nrm(ks[11], (DEPTH, FOX_WIDTH, D_MODEL), FOX_WIDTH ** -0.5),
        "w_out": nrm(ks[12], (DEPTH, D_MODEL, D_MODEL), D_MODEL ** -0.5),
        "g_ffn": 1.0 + nrm(ks[13], (DEPTH, D_MODEL), 0.02),
        "w_gate": nrm(ks[14], (DEPTH, D_MODEL, D_FF), D_MODEL ** -0.5),
        "w_up": nrm(ks[15], (DEPTH, D_MODEL, D_FF), D_MODEL ** -0.5),
        "w_down": nrm(ks[16], (DEPTH, D_FF, D_MODEL), D_FF ** -0.5),
        "g_final": 1.0 + nrm(ks[17], (D_MODEL,), 0.02),
    }


def reference(x, meta, g_mix, w_in, lambda_q1, lambda_k1, lambda_q2, lambda_k2, g_subln, b_f,
              w_branch_a, w_branch_f, w_out, g_ffn, w_gate, w_up, w_down, g_final):
    b, s, _ = x.shape
    l_real = s + N_META
    l_pad = -(-l_real // Q_BLOCK) * Q_BLOCK
    h = jnp.concatenate([jnp.broadcast_to(meta[None].astype(x.dtype), (b, N_META, D_MODEL)), x], axis=1)
    h = jnp.pad(h, ((0, 0), (0, l_pad - l_real), (0, 0)))
    pos = jnp.arange(l_pad, dtype=jnp.int32)
    slopes = _alibi_slopes(N_DIFF_HEADS)
    for i in range(DEPTH):
        lambda_init = 0.8 - 0.6 * math.exp(-0.3 * i)
        h = _hybrid_layer(h, pos, slopes, lambda_init, g_mix[i], w_in[i], lambda_q1[i],
                          lambda_k1[i], lambda_q2[i], lambda_k2[i], g_subln[i], b_f[i],
                          w_branch_a[i], w_branch_f[i], w_out[i], g_ffn[i], w_gate[i],
                          w_up[i], w_down[i])
    h = _rms_norm(h, g_final, RMS_EPS)
    return h[:, N_META:N_META + s]
```

```python
import numpy as np
from contextlib import ExitStack
import concourse.bass as bass
import concourse.mybir as mybir
from concourse.bass_utils import run_bass_kernel_spmd

F32 = mybir.dt.float32
BF16 = mybir.dt.bfloat16
AF = mybir.ActivationFunctionType
ALU = mybir.AluOpType

N_DMA_SEMS = 48
N_SW_SEMS = 16


class Sched:
    ENGS = ("pe", "act", "dve", "pool", "sp")

    def __init__(self, nc):
        self.nc = nc
        self.ops = []
        self.last_w = {}
        self.readers = {}
        self.dma_uses = [0] * N_DMA_SEMS
        self.dma_rr = 0
        self.dma_rr_sw = 0
        self.final_deps = []

    def add(self, eng, fn, reads=(), writes=(), dma=False):
        oid = len(self.ops)
        reads = list(reads) + ["__all__"]
        deps = set()
        for k in reads:
            w = self.last_w.get(k)
            if w is not None:
                deps.add(w)
        for k in writes:
            w = self.last_w.get(k)
            if w is not None:
                deps.add(w)
            for r in self.readers.get(k, ()):
                deps.add(r)
        deps.discard(oid)
        best = {}
        keep = []
        bestd = {}
        for d in deps:
            od = self.ops[d]
            if od["dma"]:
                s_ = od["dsem"]
                if s_ not in bestd or od["dval"] > self.ops[bestd[s_]]["dval"]:
                    bestd[s_] = d
            else:
                e = od["eng"]
                if e not in best or d > best[e]:
                    best[e] = d
        keep.extend(bestd.values())
        for e, d in best.items():
            if e == "pe" and eng == "pe" and not dma:
                continue
            keep.append(d)
        op = {"eng": eng, "fn": fn, "deps": keep, "dma": dma, "signal": False,
              "seq": None, "dsem": None, "dval": None}
        if dma:
            if eng == "pool":
                s = N_DMA_SEMS - N_SW_SEMS + self.dma_rr_sw
                self.dma_rr_sw = (self.dma_rr_sw + 1) % N_SW_SEMS
            else:
                s = self.dma_rr
                self.dma_rr = (self.dma_rr + 1) % (N_DMA_SEMS - N_SW_SEMS)
            self.dma_uses[s] += 1
            op["dsem"] = s
            op["dval"] = 16 * self.dma_uses[s]
        for d in keep:
            self.ops[d]["signal"] = True
        self.ops.append(op)
        for k in reads:
            self.readers.setdefault(k, []).append(oid)
        for k in writes:
            self.last_w[k] = oid
            self.readers[k] = []
        return oid

    def fence(self, dummy_ap):
        return self.add("dve", lambda e: e.memset(dummy_ap, 0.0), writes=["__all__"])

    def finish(self, oids):
        self.final_deps = list(oids)
        for d in oids:
            self.ops[d]["signal"] = True

    def emit(self, stack):
        nc = self.nc
        esem = {e: stack.enter_context(nc.semaphore("es_" + e)) for e in self.ENGS}
        dsem = [stack.enter_context(nc.semaphore("ds_%d" % i)) for i in range(N_DMA_SEMS)]
        cnt = {e: 0 for e in self.ENGS}
        for op in self.ops:
            if not op["dma"] and op["signal"]:
                cnt[op["eng"]] += 1
                op["seq"] = cnt[op["eng"]]
        block = stack.enter_context(nc.Block())
        ops = self.ops
        final_deps = self.final_deps

        def stream(ename, eng):
            waited = {}

            def wait_for(d):
                od = ops[d]
                if od["dma"]:
                    key = ("d", od["dsem"])
                    val = od["dval"]
                    sem = dsem[od["dsem"]]
                else:
                    key = ("e", od["eng"])
                    val = od["seq"]
                    sem = esem[od["eng"]]
                if waited.get(key, 0) >= val:
                    return
                waited[key] = val
                eng.wait_ge(sem, val)

            for op in ops:
                if op["eng"] != ename:
                    continue
                for d in op["deps"]:
                    wait_for(d)
                if op["dma"]:
                    prev = op["dval"] - 16
                    key = ("d", op["dsem"])
                    if prev > 0 and waited.get(key, 0) < prev:
                        waited[key] = prev
                        eng.wait_ge(dsem[op["dsem"]], prev)
                    ins = op["fn"](eng)
                    ins.then_inc(dsem[op["dsem"]], 16)
                else:
                    ins = op["fn"](eng)
                    if op["signal"]:
                        ins.then_inc(esem[ename], 1)
            if ename == "sp":
                for d in final_deps:
                    wait_for(d)

        @block.sync
        def _(eng):
            stream("sp", eng)

        @block.scalar
        def _(eng):
            stream("act", eng)

        @block.vector
        def _(eng):
            stream("dve", eng)

        @block.gpsimd
        def _(eng):
            stream("pool", eng)

        @block.tensor
        def _(eng):
            stream("pe", eng)


class Arena:
    def __init__(self, ap_f32, nbytes):
        self.base = ap_f32
        self.nbytes = nbytes
        self.off = 0
        self.marks = []

    def alloc(self, shape_free, dtype):
        esz = 4 if dtype == F32 else 2
        n = int(np.prod(shape_free))
        nb = (n * esz + 31) // 32 * 32
        assert self.off + nb <= self.nbytes, ("arena overflow", self.off, nb, self.nbytes)
        a = self.base[:, self.off // 4:(self.off + nb) // 4]
        if dtype != F32:
            a = a.bitcast(dtype)
        a = a[:, 0:n]
        self.off += nb
        if len(shape_free) == 2:
            a = a.rearrange("p (a b) -> p a b", a=shape_free[0])
        elif len(shape_free) == 3:
            a = a.rearrange("p (a b c) -> p a b c", a=shape_free[0], b=shape_free[1])
        return a

    def mark(self):
        self.marks.append(self.off)

    def release(self):
        self.off = self.marks.pop()

D = 2048
KC = 16
SEQ = 8192
NOWN = 4096
NTOK = 8320
DFF = 5632
FC = 44
NEG = -30000.0
SCALE = 128 ** -0.5
LAMBDA_INIT = 0.2
T_TABD = 0
T_DG = T_TABD + 4 * 96
T_FM = T_DG + 4 * 128
T_FO = T_FM + 128
T_RM = T_FO + 1
T_PREF = T_RM + 1
T_TRI = T_PREF + 65
T_SEL = T_TRI + 128
T_ID = T_SEL + 128
T_G = T_ID + 128
T_GSUB = T_G + 48
T_BF = T_GSUB + 256
T_LAM = T_BF + 8
T_END = T_LAM + 512


def build_program(NT=32, dbg=False, upto=3):
    nc = bass.Bass("TRN2", target_bir_lowering=False)
    NOWN_ = NT * 128
    NKT = 2 * NT + 1
    META = 2 * NT
    NTOK_ = NKT * 128
    SK = "ExternalOutput" if dbg else "Internal"
    xT = nc.dram_tensor("xT", [D, NTOK_], F32, kind="ExternalInput").ap()
    tabs_d = nc.dram_tensor("tabs", [128, T_END], F32, kind="ExternalInput").ap()
    w_in = nc.dram_tensor("w_in", [D, 10248], F32, kind="ExternalInput").ap()
    w_a = nc.dram_tensor("w_a", [1024, D], F32, kind="ExternalInput").ap()
    w_f = nc.dram_tensor("w_f", [1024, D], F32, kind="ExternalInput").ap()
    w_o = nc.dram_tensor("w_o", [D, D], F32, kind="ExternalInput").ap()
    w_g = nc.dram_tensor("w_g", [D, DFF], F32, kind="ExternalInput").ap()
    w_u = nc.dram_tensor("w_u", [D, DFF], F32, kind="ExternalInput").ap()
    w_d = nc.dram_tensor("w_d", [DFF, D], F32, kind="ExternalInput").ap()
    outT = nc.dram_tensor("outT", [D, NOWN_], F32, kind="ExternalOutput").ap()
    kT_s = nc.dram_tensor("kT_s", [16, 128, NTOK_], BF16, kind=SK).ap()
    qT_s = nc.dram_tensor("qT_s", [16, 128, NOWN_], BF16, kind=SK).ap()
    v_s = nc.dram_tensor("v_s", [NTOK_, D], BF16, kind=SK).ap()
    oT_s = nc.dram_tensor("oT_s", [D, NOWN_], BF16, kind=SK).ap()
    wsc = nc.dram_tensor("wsc", [58, 128, 8192], BF16, kind="Internal").ap()

    with ExitStack() as st:
        ARENA_W = 53000
        arena_t = st.enter_context(nc.sbuf_tensor("arena", [128, ARENA_W], F32))
        ar = Arena(arena_t[:, :], ARENA_W * 4)
        ps = [st.enter_context(nc.psum_tensor("ps%d" % i, [128, 512], F32)) for i in range(8)]
        S = Sched(nc)
        uid = [0]

        def U(p):
            uid[0] += 1
            return "%s#%d" % (p, uid[0])

        tabs = ar.alloc((T_END,), F32)
        S.add("sp", lambda e: e.dma_start(out=tabs, in_=tabs_d), writes=["tabs"], dma=True)
        ident = ar.alloc((128,), BF16)
        S.add("dve", lambda e: e.tensor_copy(out=ident, in_=tabs[:, T_ID:T_ID + 128]), reads=["tabs"], writes=["cst"])
        ones_b = ar.alloc((128,), BF16)
        S.add("pool", lambda e: e.memset(ones_b, 1.0), writes=["cst1"])
        ones_f = ar.alloc((128,), F32)
        S.add("pool", lambda e: e.memset(ones_f, 1.0), writes=["cst2"])
        gsub = ar.alloc((256,), F32)
        S.add("dve", lambda e: e.tensor_scalar(out=gsub, in0=tabs[:, T_GSUB:T_GSUB + 256], scalar1=1.0 - LAMBDA_INIT,
                                              scalar2=None, op0=ALU.mult), reads=["tabs"], writes=["cst3"])
        lt = ar.alloc((128,), F32)
        lsum = ar.alloc((2,), F32)
        nlam = ar.alloc((1,), F32)
        for i in range(2):
            a0 = T_LAM + i * 256
            S.add("dve", lambda e, a0=a0: e.tensor_tensor(out=lt, in0=tabs[:, a0:a0 + 128], in1=tabs[:, a0 + 128:a0 + 256],
                                                          op=ALU.mult), reads=["tabs"], writes=["lt"])
            S.add("dve", lambda e, i=i: e.tensor_reduce(out=lsum[:, i:i + 1], in_=lt, axis=mybir.AxisListType.X, op=ALU.add),
                  reads=["lt"], writes=["lsum"])
        S.add("act", lambda e: e.activation(out=lsum, in_=lsum, func=AF.Exp), reads=["lsum"], writes=["lsum"])
        S.add("dve", lambda e: e.tensor_tensor(out=nlam, in0=lsum[:, 1:2], in1=lsum[:, 0:1], op=ALU.subtract),
              reads=["lsum"], writes=["nlam"])
        S.add("dve", lambda e: e.tensor_scalar(out=nlam, in0=nlam, scalar1=-LAMBDA_INIT, scalar2=None, op0=ALU.add),
              reads=["nlam"], writes=["nlam"])
        SPL = ar.alloc((NKT, 8), F32)
        dummy = ar.alloc((8,), F32)
        ev = [0]

        def evac(out, in_, reads, writes):
            ev[0] += 1
            if ev[0] % 2:
                S.add("dve", lambda e: e.tensor_copy(out=out, in_=in_), reads=reads, writes=writes)
            else:
                S.add("act", lambda e: e.copy(out=out, in_=in_), reads=reads, writes=writes)

        bank_rr = [0]

        def next_bank():
            b = 2 + bank_rr[0] % 6
            bank_rr[0] += 1
            return b

        xT_v = xT.rearrange("(kc p) t -> p kc t", p=128)

        def rms_stats(src, n, sq, ksrc, eps=1e-6):
            S.add("act", lambda e: e.activation(out=sq[:, :, 0:n], in_=src, func=AF.Square), reads=[ksrc], writes=["sq"])
            for kc in range(KC):
                S.add("pe", lambda e, kc=kc: e.matmul(ps[0][0:1, 0:n], lhsT=ones_b[:, 0:1], rhs=sq[:, kc, 0:n],
                                                      start=(kc == 0), stop=(kc == KC - 1)),
                      reads=["sq", "cst1"], writes=["ps0"])
            S.add("act", lambda e: e.activation(out=rs_row[0:1, 0:n], in_=ps[0][0:1, 0:n], func=AF.Sqrt, bias=eps_t[0:1, 0:1],
                                                scale=1.0 / D), reads=["ps0", "eps"], writes=["rsrow"])
            S.add("dve", lambda e: e.reciprocal(out=rs_row[0:1, 0:n], in_=rs_row[0:1, 0:n]), reads=["rsrow"], writes=["rsrow"])
            S.add("pe", lambda e: e.matmul(ps[1][:, 0:n], lhsT=ones_f[0:1, :], rhs=rs_row[0:1, 0:n], start=True, stop=True),
                  reads=["rsrow", "cst2"], writes=["ps1"])

        rs_row = ar.alloc((512,), F32)
        eps_t = ar.alloc((2,), F32)
        S.add("pool", lambda e: e.memset(eps_t[:, 0:1], 1e-6), writes=["eps"])
        S.add("pool", lambda e: e.memset(eps_t[:, 1:2], 1e-5), writes=["eps"])

        def norm_apply(src, n, gofs, dst, ksrc, kdst):
            for kc in range(KC):
                S.add("dve", lambda e, kc=kc: e.scalar_tensor_tensor(
                    out=dst[:, kc, 0:n], in0=src[:, kc, 0:n], scalar=tabs[:, T_G + gofs + kc:T_G + gofs + kc + 1],
                    in1=ps[1][:, 0:n], op0=ALU.mult, op1=ALU.mult), reads=[ksrc, "ps1", "tabs"], writes=[kdst])

        def load_slab(dst, w_ap, r0, nk, c0, ncol, key):
            src = w_ap[r0:r0 + nk * 128, c0:c0 + ncol].rearrange("(kc p) c -> p kc c", p=128)
            S.add("pool", lambda e: e.dma_start(out=dst[:, 0:nk, 0:ncol], in_=src), writes=[key], dma=True)

        ar.mark()
        SBT = min(1024, NOWN_)
        uTs = [ar.alloc((KC, SBT), BF16) for _ in range(2)]
        xs2 = [ar.alloc((KC, 512), F32) for _ in range(2)]
        sq = ar.alloc((KC, 512), BF16)
        slabs = [ar.alloc((KC, 512), BF16) for _ in range(2)]
        wz = ar.alloc((KC, 8), BF16)
        stg = [ar.alloc((512,), BF16) for _ in range(6)]
        ztmp = ar.alloc((8,), F32)
        load_slab(wz, w_in, 0, KC, 6144, 8, "wz")
        sbs = [(i * SBT, SBT, True) for i in range(NOWN_ // SBT)] + [(NOWN_ + i * SBT, SBT, False) for i in range(NOWN_ // SBT)] + [(2 * NOWN_, 128, False)]
        xi = [0]
        si = [0]
        gi = [0]
        out_dmas = []
        slab_list = []
        for s in range(2):
            slab_list.append((1024 + s * 512, "KT", s * 4))
        for s in range(2):
            slab_list.append((4096 + s * 512, "KT", 8 + s * 4))
        for s in range(2):
            slab_list.append((2048 + s * 512, "V", s * 512))
        for s in range(2):
            slab_list.append((5120 + s * 512, "V", 1024 + s * 512))
        for s in range(2):
            slab_list.append((0 + s * 512, "QT", s * 4))
        for s in range(2):
            slab_list.append((3072 + s * 512, "QT", 8 + s * 4))
        def emit_norm(sbi):
            t0, nt, own = sbs[sbi]
            uT = uTs[sbi % 2]
            ku = "uT%d" % (sbi % 2)
            nch = (nt + 511) // 512
            for c in range(nch):
                n = min(512, nt - c * 512)
                xs = xs2[xi[0] % 2]
                kx = "xs%d" % (xi[0] % 2)
                xi[0] += 1
                S.add("sp", lambda e, xs=xs, n=n, a=t0 + c * 512: e.dma_start(out=xs[:, :, 0:n], in_=xT_v[:, :, a:a + n]),
                      writes=[kx], dma=True)
                rms_stats(xs[:, :, 0:n], n, sq, kx)
                norm_apply(xs, n, 0, uT[:, :, c * 512:c * 512 + n], kx, ku)

        emit_norm(0)
        for sbi, (t0, nt, own) in enumerate(sbs):
            nch = (nt + 511) // 512
            uT = uTs[sbi % 2]
            ku = "uT%d" % (sbi % 2)
            slab_n = [0]
            for tt in range(nt // 128):
                b = next_bank()
                for kc in range(KC):
                    S.add("pe", lambda e, uT=uT, kc=kc, tt=tt, b=b: e.matmul(ps[b][:, 0:8], lhsT=uT[:, kc, tt * 128:(tt + 1) * 128],
                                                                      rhs=wz[:, kc, :], start=(kc == 0), stop=(kc == KC - 1)),
                          reads=[ku, "wz"], writes=["ps%d" % b])
                tile_idx = (t0 // 128 + tt) if t0 < 2 * NOWN_ else META
                S.add("dve", lambda e, b=b: e.tensor_tensor(out=ztmp, in0=ps[b][:, 0:8], in1=tabs[:, T_BF:T_BF + 8], op=ALU.add),
                      reads=["ps%d" % b, "tabs"], writes=["ztmp"])
                S.add("act", lambda e: e.activation(out=ztmp, in_=ztmp, func=AF.Exp, scale=-1.0), reads=["ztmp"], writes=["ztmp"])
                S.add("act", lambda e, ti=tile_idx: e.activation(out=SPL[:, ti, :], in_=ztmp, func=AF.Ln, bias=ones_f[:, 0:1], scale=1.0),
                      reads=["ztmp", "cst2"], writes=["SPL"])
            for (c0, kind, idx) in slab_list:
                if kind == "QT" and not own:
                    continue
                if slab_n[0] == 1 and sbi + 1 < len(sbs):
                    emit_norm(sbi + 1)
                slab_n[0] += 1
                sl = slabs[si[0] % 2]
                ks = "slab%d" % (si[0] % 2)
                si[0] += 1
                load_slab(sl, w_in, 0, KC, c0, 512, ks)
                if kind in ("KT", "QT"):
                    dst_s = kT_s if kind == "KT" else qT_s
                    for ct in range(4):
                        for c in range(nch):
                            n = min(512, nt - c * 512)
                            b = next_bank()
                            for kc in range(KC):
                                S.add("pe", lambda e, uT=uT, kc=kc, ct=ct, c=c, n=n, b=b, sl=sl: e.matmul(
                                    ps[b][:, 0:n], lhsT=sl[:, kc, ct * 128:(ct + 1) * 128], rhs=uT[:, kc, c * 512:c * 512 + n],
                                    start=(kc == 0), stop=(kc == KC - 1)), reads=[ku, ks], writes=["ps%d" % b])
                            sg = stg[gi[0] % 6]
                            kg = "stg%d" % (gi[0] % 6)
                            gi[0] += 1
                            evac(sg[:, 0:n], ps[b][:, 0:n], ["ps%d" % b], [kg])
                            S.add("sp", lambda e, sg=sg, n=n, m=idx + ct, a=t0 + c * 512, dst_s=dst_s: e.dma_start(
                                out=dst_s[m, :, a:a + n], in_=sg[:, 0:n]), reads=[kg], writes=[U("scr")], dma=True)
                else:
                    for tt in range(nt // 128):
                        b = next_bank()
                        for kc in range(KC):
                            S.add("pe", lambda e, uT=uT, kc=kc, tt=tt, b=b, sl=sl: e.matmul(
                                ps[b][:, :], lhsT=uT[:, kc, tt * 128:(tt + 1) * 128], rhs=sl[:, kc, :],
                                start=(kc == 0), stop=(kc == KC - 1)), reads=[ku, ks], writes=["ps%d" % b])
                        sg = stg[gi[0] % 6]
                        kg = "stg%d" % (gi[0] % 6)
                        gi[0] += 1
                        evac(sg, ps[b][:, :], ["ps%d" % b], [kg])
                        S.add("sp", lambda e, sg=sg, a=t0 + tt * 128, idx=idx: e.dma_start(
                            out=v_s[a:a + 128, idx:idx + 512], in_=sg), reads=[kg], writes=[U("scr")], dma=True)
        ar.release()
        S.fence(dummy)

        ar.mark()
        Wc = ar.alloc((NKT, 8), F32)
        A = ar.alloc((NKT, 8), F32)
        tot = ar.alloc((8,), F32)
        totb = ar.alloc((128,), F32)
        CB = ar.alloc((NT, 8), F32)
        trif = tabs[:, T_TRI:T_TRI + 128]
        self64 = tabs[:, T_SEL:T_SEL + 128]
        pref = tabs[0:NKT, T_PREF:T_PREF + NKT]
        S.add("dve", lambda e: e.tensor_scalar(out=SPL[:, META, :], in0=SPL[:, META, :], scalar1=tabs[:, T_RM:T_RM + 1], scalar2=None,
                                              op0=ALU.mult), reads=["SPL", "tabs"], writes=["SPL"])
        SPLf = SPL.rearrange("p a b -> p (a b)")
        Wcf = Wc.rearrange("p a b -> p (a b)")
        for c0_ in range(0, NKT * 8, 512):
            n_ = min(512, NKT * 8 - c0_)
            S.add("pe", lambda e, c0_=c0_, n_=n_: e.matmul(ps[2][:, 0:n_], lhsT=trif, rhs=SPLf[:, c0_:c0_ + n_], start=True, stop=True),
                  reads=["SPL", "tabs"], writes=["ps2"])
            S.add("dve", lambda e, c0_=c0_, n_=n_: e.tensor_copy(out=Wcf[:, c0_:c0_ + n_], in_=ps[2][:, 0:n_]), reads=["ps2"], writes=["Wc"])
        for h in range(8):
            S.add("pe", lambda e, h=h: e.matmul(ps[4][0:NKT, h:h + 1], lhsT=SPL[:, :, h], rhs=ones_f[:, 0:1], start=True, stop=True),
                  reads=["SPL", "cst2"], writes=["ps4"])
        S.add("dve", lambda e: e.tensor_copy(out=tot[0:NKT, :], in_=ps[4][0:NKT, 0:8]), reads=["ps4"], writes=["tot"])
        for h in range(8):
            S.add("dve", lambda e, h=h: e.tensor_scalar(out=totb[0:NKT, :], in0=ones_f[0:NKT, :], scalar1=tot[0:NKT, h:h + 1], scalar2=None,
                                                         op0=ALU.mult), reads=["tot", "cst2"], writes=["totb"])
            S.add("pe", lambda e: e.matmul(ps[5][:, 0:NKT], lhsT=totb[0:NKT, :], rhs=pref, start=True, stop=True),
                  reads=["totb", "tabs"], writes=["ps5"])
            S.add("dve", lambda e, h=h: e.tensor_tensor(out=A[:, :, h], in0=Wc[:, :, h], in1=ps[5][:, 0:NKT], op=ALU.add),
                  reads=["Wc", "ps5"], writes=["A"])
        Af = A.rearrange("p a b -> p (a b)")
        S.add("pe", lambda e: e.matmul(ps[6][:, 0:NT * 8], lhsT=self64, rhs=Af[:, 0:NT * 8], start=True, stop=True), reads=["A", "tabs"], writes=["ps6"])
        S.add("dve", lambda e: e.tensor_copy(out=CB.rearrange("p a b -> p (a b)"), in_=ps[6][:, 0:NT * 8]), reads=["ps6"], writes=["CB"])

        wreg = {}
        wlist = []
        for sgp in range(4):
            wlist += [(w_in, 0, KC, 6152 + sgp * 512, 512), (w_a, 0, 8, sgp * 512, 512),
                      (w_in, 0, KC, 8200 + sgp * 512, 512), (w_f, 0, 8, sgp * 512, 512)]
        for sgp in range(4):
            wlist.append((w_o, 0, KC, sgp * 512, 512))
        for sgp in range(11):
            wlist += [(w_g, 0, KC, sgp * 512, 512), (w_u, 0, KC, sgp * 512, 512)]
        for ct in range(16):
            wlist.append((w_d, 0, FC, ct * 128, 128))
        for idx, (w_ap, r0, nk, c0, ncol) in enumerate(wlist):
            wreg[(w_ap.tensor.name, r0, nk, c0, ncol)] = idx
            src = w_ap[r0:r0 + nk * 128, c0:c0 + ncol].rearrange("(kc p) c -> p kc c", p=128)
            dst = wsc[idx][:, 0:nk * ncol].rearrange("p (a b) -> p a b", a=nk)
            S.add("pool", lambda e, src=src, dst=dst: e.dma_start(out=dst, in_=src), writes=[U("wsc")], dma=True)
        KTb = [ar.alloc((NTOK_,), BF16) for _ in range(3)]
        QTb = [ar.alloc((NOWN_,), BF16) for _ in range(3)]
        q_i = [0]
        head_no = [0]
        Vb = [ar.alloc((NKT, 257), BF16) for _ in range(2)]
        oTh = ar.alloc((2, NOWN_), BF16)
        dtmp = [ar.alloc((128,), F32) for _ in range(2)]
        fin4 = [ar.alloc((8,), F32) for _ in range(4)]
        ofp4 = [ar.alloc((256,), F32) for _ in range(4)]
        junk = ar.alloc((256,), F32)
        obf4 = [ar.alloc((256,), BF16) for _ in range(4)]
        kq_i = [0]
        v_i = [0]
        pt_i = [0]
        dt_i = [0]
        ss_i = [0]
        acc_i = [0]
        bf_i = [0]
        v_sv = v_s.rearrange("(kt p) c -> p kt c", p=128)

        NPB = 8
        PB = [ar.alloc((512,), BF16) for _ in range(NPB)]
        ewD = ar.alloc((96,), F32)
        bFa = ar.alloc((NT, NKT), F32)
        o0 = ar.alloc((4, 256), F32)
        NG = NT // 4

        def attention_head(is_diff, h):
            nmap = 2 if is_diff else 1
            ncol = 256 if is_diff else 128
            maps = [2 * h, 2 * h + 1] if is_diff else [8 + h]
            vcol = h * 256 if is_diff else 1024 + h * 128
            orow = h * 256 if is_diff else 1024 + h * 128
            KT, QT, kk, kqk = [], [], [], []
            for m in maps:
                i = kq_i[0] % 3
                kq_i[0] += 1
                S.add("sp", lambda e, i=i, m=m: e.dma_start(out=KTb[i], in_=kT_s[m]), writes=["KT%d" % i], dma=True)
                iq = q_i[0] % 3
                q_i[0] += 1
                S.add("sp", lambda e, iq=iq, m=m: e.dma_start(out=QTb[iq], in_=qT_s[m]), writes=["QT%d" % iq], dma=True)
                KT.append(KTb[i]); QT.append(QTb[iq]); kk.append("KT%d" % i); kqk.append("QT%d" % iq)
            vi = v_i[0] % 2
            v_i[0] += 1
            V = Vb[vi]
            kv = "V%d" % vi
            for g0 in range(0, NKT, 13):
                g1 = min(NKT, g0 + 13)
                S.add("sp", lambda e, g0=g0, g1=g1: e.dma_start(out=V[:, g0:g1, 0:ncol], in_=v_sv[:, g0:g1, vcol:vcol + ncol]),
                      writes=[kv], dma=True)
            S.add("dve", lambda e: e.memset(V[:, :, ncol:ncol + 1], 1.0), writes=[kv])
            pool_ok = head_no[0] >= 2
            head_no[0] += 1
            if is_diff:
                S.add("act", lambda e: e.activation(out=ewD, in_=tabs[:, T_TABD + h * 96:T_TABD + (h + 1) * 96], func=AF.Exp),
                      reads=["tabs"], writes=["ew"])
            else:
                for j in range(NT):
                    S.add("dve", lambda e, j=j: e.tensor_scalar(out=bFa[:, j, :], in0=A[:, :, h], scalar1=CB[:, j, h:h + 1], scalar2=60.0,
                                                                 op0=ALU.subtract, op1=ALU.min), reads=["A", "CB"], writes=["ew"])
                    S.add("dve", lambda e, j=j: e.tensor_tensor(out=bFa[:, j, NT + j:NT + j + 1], in0=bFa[:, j, NT + j:NT + j + 1],
                                                                 in1=tabs[:, T_FO:T_FO + 1], op=ALU.add), reads=["ew", "tabs"], writes=["ew"])
                S.add("act", lambda e: e.activation(out=bFa, in_=bFa, func=AF.Exp), reads=["ew"], writes=["ew"])

            def ew_ap(G, kind, i, b0, nb):
                j0 = 4 * G + b0
                if is_diff:
                    if kind == "own":
                        c0 = j0 - i
                    elif kind == "oth":
                        c0 = 32 + j0 - i
                    else:
                        c0 = 64 + j0
                    return ewD[:, c0:c0 + nb]
                kt = i if kind == "own" else (NT + i if kind == "oth" else META)
                return bFa[:, j0:j0 + nb, kt]

            items = []
            for G in range(NG):
                for c in range(nmap):
                    lst = [dict(kt=META, kp=16, kind="meta", i=0, b0=0)]
                    for i in range(4 * G):
                        lst.append(dict(kt=i, kp=128, kind="own", i=i, b0=0))
                        lst.append(dict(kt=NT + i, kp=128, kind="oth", i=i, b0=0))
                    for a in range(4):
                        lst.append(dict(kt=NT + 4 * G + a, kp=128, kind="oth", i=4 * G + a, b0=a))
                        lst.append(dict(kt=4 * G + a, kp=128, kind="own", i=4 * G + a, b0=a, diag=True))
                    if is_diff:
                        slope = 2.0 ** (-8.0 * (h + 1) / 4)
                        keep = []
                        for it in lst:
                            if it["kind"] == "meta" or it.get("diag"):
                                keep.append(it)
                                continue
                            dl = 4 * G + it["b0"] - it["i"]
                            mx = slope * (127 - 256 * dl + (128 if it["kind"] == "oth" else 0))
                            if mx > -120.0:
                                keep.append(it)
                        lst = keep
                    for n_, it in enumerate(lst):
                        it.update(G=G, c=c, first=(n_ == 0), last=(n_ == len(lst) - 1))
                        items.append(it)
                    items.append(dict(T=True, G=G, c=c))
            par = [0]

            def emit_st(n, it):
                if it.get("T"):
                    return
                bank = n % 3
                G, c, kt, kp, b0 = it["G"], it["c"], it["kt"], it["kp"], it["b0"]
                q0 = (4 * G + b0) * 128
                q1 = (4 * G + 4) * 128
                S.add("pe", lambda e: e.matmul(ps[bank][0:kp, b0 * 128:512], lhsT=KT[c][:, kt * 128:kt * 128 + kp], rhs=QT[c][:, q0:q1],
                                               start=True, stop=True), reads=[kk[c], kqk[c]], writes=["ps%d" % bank])

            def emit_rest(n, it):
                if it.get("T"):
                    return
                bank = n % 3
                kbank = "ps%d" % bank
                P = PB[n % NPB]
                kPb = ["PB%d_%d" % (n % NPB, b_) for b_ in range(4)]
                kP = kPb[0]
                G, c, kt, kp, b0, kind = it["G"], it["c"], it["kt"], it["kp"], it["b0"], it["kind"]
                c0 = b0 * 128
                diag = it.get("diag", False)
                if diag:
                    di = dt_i[0] % 2
                    dt_i[0] += 1
                    dtm = dtmp[di]
                    kd = "dt%d" % di
                    mk = tabs[:, T_DG + h * 128:T_DG + (h + 1) * 128] if is_diff else tabs[:, T_FM:T_FM + 128]
                    S.add("dve", lambda e: e.scalar_tensor_tensor(out=dtm, in0=ps[bank][:, c0:c0 + 128], scalar=SCALE, in1=mk,
                                                                  op0=ALU.mult, op1=ALU.add), reads=["tabs"], writes=[kd, kbank])
                    S.add("act", lambda e: e.activation(out=P[:, c0:c0 + 128], in_=dtm, func=AF.Exp), reads=[kd], writes=[kPb[b0]])
                    if c0 + 128 < 512:
                        S.add("act", lambda e: e.activation(out=P[:, c0 + 128:512], in_=ps[bank][:, c0 + 128:512], func=AF.Exp, scale=SCALE),
                              reads=[kbank], writes=kPb[b0 + 1:4])
                else:
                    S.add("act", lambda e: e.activation(out=P[0:kp, c0:512], in_=ps[bank][0:kp, c0:512], func=AF.Exp, scale=SCALE),
                          reads=[kbank], writes=kPb[b0:4])
                sb0 = b0 + 1 if (diag and is_diff) else b0
                nb = 4 - sb0
                if nb > 0:
                    ew = ew_ap(G, kind, it["i"], sb0, nb)[0:kp].unsqueeze(2).broadcast_to([kp, nb, 128])
                    Pv = P[0:kp, sb0 * 128:512].rearrange("p (a b) -> p a b", a=nb)
                    par[0] += 1
                    eng_ = "pool" if (par[0] % 2 == 0 and pool_ok) else "dve"
                    if eng_ == "pool":
                        S.add(eng_, lambda e: e.tensor_tensor(out=Pv, in0=Pv, in1=ew, op=ALU.mult), reads=["ew"], writes=kPb[sb0:4])
                    else:
                        ew2 = ew_ap(G, kind, it["i"], sb0, nb)
                        for bb in range(nb):
                            S.add("dve", lambda e, bb=bb: e.tensor_scalar(out=P[0:kp, (sb0 + bb) * 128:(sb0 + bb + 1) * 128],
                                                                         in0=P[0:kp, (sb0 + bb) * 128:(sb0 + bb + 1) * 128],
                                                                         scalar1=ew2[0:kp, bb:bb + 1], scalar2=None, op0=ALU.mult),
                                  reads=["ew"], writes=[kPb[sb0 + bb]])

            def emit_pv(n, it):
                if it.get("T"):
                    finalize_pass(n, it["G"], it["c"])
                    return
                P = PB[n % NPB]
                kPb = ["PB%d_%d" % (n % NPB, b_) for b_ in range(4)]
                G, c, kt, kp, b0 = it["G"], it["c"], it["kt"], it["kp"], it["b0"]
                diag = it.get("diag", False)
                for b in range(b0, 4):
                    S.add("pe", lambda e, b=b: e.matmul(ps[4 + b][:, 0:ncol + 1], lhsT=P[0:kp, b * 128:(b + 1) * 128], rhs=V[0:kp, kt, 0:ncol + 1],
                                                        start=it["first"], stop=(it["last"] or (b == b0 and diag))),
                          reads=[kPb[b], kv], writes=["ps%d" % (4 + b)])

            def finalize_pass(n, G, c):
                bank = 3
                kbank = "ps%d" % bank
                tb = ps[bank][:, :].bitcast(BF16)
                nblk = ncol // 128
                for b in range(4):
                    a = 4 + b
                    ka = "ps%d" % a
                    fin, ofp, obf = fin4[b], ofp4[b], obf4[b]
                    kf, ko, kob = "fin%d" % b, "ofp%d" % b, "obf%d" % b
                    if is_diff:
                        S.add("dve", lambda e, a=a, fin=fin: e.reciprocal(out=fin[:, 0:1], in_=ps[a][:, 256:257]), reads=[ka], writes=[kf])
                        if c == 0:
                            S.add("dve", lambda e, a=a, fin=fin, b=b: e.tensor_scalar(out=o0[:, b, :], in0=ps[a][:, 0:256], scalar1=fin[:, 0:1],
                                                                                       scalar2=None, op0=ALU.mult), reads=[ka, kf], writes=["o0_%d" % b])
                            continue
                        S.add("dve", lambda e, fin=fin: e.tensor_tensor(out=fin[:, 2:3], in0=fin[:, 0:1], in1=nlam, op=ALU.mult),
                              reads=[kf, "nlam"], writes=[kf])
                        S.add("dve", lambda e, a=a, fin=fin, ofp=ofp, b=b: e.scalar_tensor_tensor(
                            out=ofp, in0=ps[a][:, 0:256], scalar=fin[:, 2:3], in1=o0[:, b, :], op0=ALU.mult, op1=ALU.add),
                            reads=[ka, kf, "o0_%d" % b], writes=[ko])
                    else:
                        S.add("dve", lambda e, a=a, fin=fin: e.reciprocal(out=fin[:, 0:1], in_=ps[a][:, 128:129]), reads=[ka], writes=[kf])
                        S.add("dve", lambda e, a=a, fin=fin, obf=obf: e.tensor_scalar(out=obf[:, 0:128], in0=ps[a][:, 0:128], scalar1=fin[:, 0:1],
                                                                                       scalar2=None, op0=ALU.mult), reads=[ka, kf], writes=[kob])
                if is_diff and c == 0:
                    return
                if is_diff:
                    for b in range(4):
                        fin, ofp = fin4[b], ofp4[b]
                        kf, ko = "fin%d" % b, "ofp%d" % b
                        S.add("act", lambda e, fin=fin, ofp=ofp: e.activation(out=junk, in_=ofp, func=AF.Square, accum_out=fin[:, 3:4]),
                              reads=[ko], writes=["junk", kf])
                        S.add("act", lambda e, fin=fin: e.activation(out=fin[:, 4:5], in_=fin[:, 3:4], func=AF.Sqrt, bias=eps_t[:, 1:2], scale=1.0 / 256),
                              reads=[kf, "eps"], writes=[kf])
                    for b in range(4):
                        fin, ofp, obf = fin4[b], ofp4[b], obf4[b]
                        kf, ko, kob = "fin%d" % b, "ofp%d" % b, "obf%d" % b
                        S.add("dve", lambda e, fin=fin: e.reciprocal(out=fin[:, 5:6], in_=fin[:, 4:5]), reads=[kf], writes=[kf])
                        S.add("dve", lambda e, fin=fin, ofp=ofp, obf=obf: e.scalar_tensor_tensor(out=obf, in0=ofp, scalar=fin[:, 5:6], in1=gsub,
                                                                                                  op0=ALU.mult, op1=ALU.mult),
                              reads=[ko, kf, "cst3"], writes=[kob])
                for b in range(4):
                    obf = obf4[b]
                    for blk in range(nblk):
                        sl_ = b * nblk + blk
                        S.add("pe", lambda e, obf=obf, blk=blk, sl_=sl_: e.transpose(tb[:, sl_ * 128:(sl_ + 1) * 128], obf[:, blk * 128:(blk + 1) * 128], ident),
                              reads=["obf%d" % b, "cst"], writes=[kbank])
                for b in range(4):
                    j = 4 * G + b
                    src = tb[:, b * nblk * 128:(b + 1) * nblk * 128].rearrange("p (a q) -> p a q", a=nblk)
                    S.add("dve", lambda e, src=src, j=j: e.tensor_copy(out=oTh[:, 0:nblk, j * 128:(j + 1) * 128], in_=src),
                          reads=[kbank], writes=["oTh"])

            LA, LP = 2, 6
            for n in range(len(items) + LP):
                if n < len(items):
                    emit_st(n, items[n])
                if 0 <= n - LA < len(items):
                    emit_rest(n - LA, items[n - LA])
                if n - LP >= 0:
                    emit_pv(n - LP, items[n - LP])
            for blk in range(ncol // 128):
                S.add("sp", lambda e, blk=blk: e.dma_start(out=oT_s[orow + blk * 128:orow + (blk + 1) * 128, :], in_=oTh[:, blk, :]),
                      reads=["oTh"], writes=[U("scr")], dma=True)

        if upto >= 2:
            for hd in range(4):
                attention_head(True, hd)
                attention_head(False, 2 * hd)
                attention_head(False, 2 * hd + 1)
        ar.release()
        S.fence(dummy)

        ar.mark()
        xs = ar.alloc((KC, 512), F32)
        uT3 = ar.alloc((KC, 512), BF16)
        sq3 = ar.alloc((KC, 512), BF16)
        aT = ar.alloc((FC, 512), BF16)
        oTc = aT[:, 0:KC, :]
        mrg = sq3
        gA = ar.alloc((512,), BF16)
        gF = ar.alloc((512,), BF16)
        m1 = ar.alloc((512,), F32)
        m2 = ar.alloc((512,), F32)
        sgt = ar.alloc((512,), F32)
        NSL = 4
        sl3 = [ar.alloc((KC * 512,), BF16) for _ in range(NSL)]
        s3 = [0]

        def get_slab(w_ap, r0, nk, c0, ncol):
            i = s3[0] % NSL
            s3[0] += 1
            v = sl3[i][:, 0:nk * ncol].rearrange("p (a b) -> p a b", a=nk)
            idx = wreg[(w_ap.tensor.name, r0, nk, c0, ncol)]
            S.add("sp", lambda e, i=i, idx=idx, n_=nk * ncol: e.dma_start(out=sl3[i][:, 0:n_], in_=wsc[idx][:, 0:n_]),
                  writes=["sl3_%d" % i], dma=True)
            return v, "sl3_%d" % i

        oT_v = oT_s.rearrange("(kc p) t -> p kc t", p=128)
        outT_v = outT.rearrange("(kc p) t -> p kc t", p=128)

        def mm_group(b, lhs_fn, rhs_fn, nk, reads):
            for kc in range(nk):
                S.add("pe", lambda e, kc=kc, l_=lhs_fn(kc), r_=rhs_fn(kc): e.matmul(ps[b][:, :], lhsT=l_, rhs=r_, start=(kc == 0), stop=(kc == nk - 1)),
                      reads=reads, writes=["ps%d" % b])

        for ch in range(NOWN_ // 512 if upto >= 3 else 0):
            a0 = ch * 512
            S.add("sp", lambda e, a0=a0: e.dma_start(out=xs, in_=xT_v[:, :, a0:a0 + 512]), writes=["xs"], dma=True)
            S.add("sp", lambda e, a0=a0: e.dma_start(out=oTc, in_=oT_v[:, :, a0:a0 + 512]), writes=["aT"], dma=True)
            rms_stats(xs, 512, sq3, "xs")
            norm_apply(xs, 512, 0, uT3, "xs", "uT3")
            for sgp in range(4):
                wga, kga = get_slab(w_in, 0, KC, 6152 + sgp * 512, 512)
                wa_, kwa = get_slab(w_a, 0, 8, sgp * 512, 512)
                for ct in range(4):
                    b = next_bank()
                    mm_group(b, lambda kc, ct=ct: wga[:, kc, ct * 128:(ct + 1) * 128], lambda kc: uT3[:, kc, :], KC, ["uT3", kga])
                    S.add("act", lambda e, b=b: e.activation(out=gA, in_=ps[b][:, :], func=AF.Sigmoid), reads=["ps%d" % b], writes=["gA"])
                    b2 = next_bank()
                    mm_group(b2, lambda kc, ct=ct: wa_[:, kc, ct * 128:(ct + 1) * 128], lambda kc: oTc[:, kc, :], 8, ["aT", kwa])
                    S.add("dve", lambda e, b2=b2, ct=ct, sgp=sgp: e.tensor_tensor(out=mrg[:, sgp * 4 + ct, :], in0=ps[b2][:, :], in1=gA, op=ALU.mult),
                          reads=["ps%d" % b2, "gA"], writes=["sq"])
                wgf, kgf = get_slab(w_in, 0, KC, 8200 + sgp * 512, 512)
                wf_, kwf = get_slab(w_f, 0, 8, sgp * 512, 512)
                for ct in range(4):
                    b = next_bank()
                    mm_group(b, lambda kc, ct=ct: wgf[:, kc, ct * 128:(ct + 1) * 128], lambda kc: uT3[:, kc, :], KC, ["uT3", kgf])
                    S.add("act", lambda e, b=b: e.activation(out=gF, in_=ps[b][:, :], func=AF.Sigmoid), reads=["ps%d" % b], writes=["gF"])
                    b2 = next_bank()
                    mm_group(b2, lambda kc, ct=ct: wf_[:, kc, ct * 128:(ct + 1) * 128], lambda kc: oTc[:, 8 + kc, :], 8, ["aT", kwf])
                    S.add("dve", lambda e, b2=b2: e.tensor_tensor(out=m2, in0=ps[b2][:, :], in1=gF, op=ALU.mult),
                          reads=["ps%d" % b2, "gF"], writes=["m2"])
                    S.add("dve", lambda e, ct=ct, sgp=sgp: e.tensor_tensor(out=mrg[:, sgp * 4 + ct, :], in0=mrg[:, sgp * 4 + ct, :], in1=m2, op=ALU.add),
                          reads=["m2", "sq"], writes=["sq"])
            for sgp in range(4):
                wo_, kwo = get_slab(w_o, 0, KC, sgp * 512, 512)
                for ct in range(4):
                    b = next_bank()
                    mm_group(b, lambda kc, ct=ct: wo_[:, kc, ct * 128:(ct + 1) * 128], lambda kc: mrg[:, kc, :], KC, ["sq", kwo])
                    S.add("dve", lambda e, b=b, ct=ct, sgp=sgp: e.tensor_tensor(out=xs[:, sgp * 4 + ct, :], in0=xs[:, sgp * 4 + ct, :], in1=ps[b][:, :], op=ALU.add),
                          reads=["ps%d" % b, "xs"], writes=["xs"])
            rms_stats(xs, 512, sq3, "xs")
            norm_apply(xs, 512, 16, uT3, "xs", "uT3")
            for sgp in range(11):
                wg_, kwg = get_slab(w_g, 0, KC, sgp * 512, 512)
                wu_, kwu = get_slab(w_u, 0, KC, sgp * 512, 512)
                for ct in range(4):
                    b = next_bank()
                    mm_group(b, lambda kc, ct=ct: wg_[:, kc, ct * 128:(ct + 1) * 128], lambda kc: uT3[:, kc, :], KC, ["uT3", kwg])
                    S.add("act", lambda e, b=b: e.activation(out=sgt, in_=ps[b][:, :], func=AF.Silu), reads=["ps%d" % b], writes=["sgt"])
                    b2 = next_bank()
                    mm_group(b2, lambda kc, ct=ct: wu_[:, kc, ct * 128:(ct + 1) * 128], lambda kc: uT3[:, kc, :], KC, ["uT3", kwu])
                    S.add("dve", lambda e, b2=b2, ct=ct, sgp=sgp: e.tensor_tensor(out=aT[:, sgp * 4 + ct, :], in0=ps[b2][:, :], in1=sgt, op=ALU.mult),
                          reads=["ps%d" % b2, "sgt"], writes=["aT"])
            for ct in range(16):
                b = next_bank()
                wd_, kwd = get_slab(w_d, 0, FC, ct * 128, 128)
                for kc in range(FC):
                    S.add("pe", lambda e, kc=kc, b=b, wd_=wd_: e.matmul(ps[b][:, :], lhsT=wd_[:, kc, :], rhs=aT[:, kc, :], start=(kc == 0), stop=(kc == FC - 1)),
                          reads=["aT", kwd], writes=["ps%d" % b])
                S.add("dve", lambda e, b=b, ct=ct: e.tensor_tensor(out=xs[:, ct, :], in0=xs[:, ct, :], in1=ps[b][:, :], op=ALU.add),
                      reads=["ps%d" % b, "xs"], writes=["xs"])
            rms_stats(xs, 512, sq3, "xs")
            norm_apply(xs, 512, 32, xs, "xs", "xs")
            od = S.add("sp", lambda e, a0=a0: e.dma_start(out=outT_v[:, :, a0:a0 + 512], in_=xs), reads=["xs"], dma=True)
            out_dmas.append(od)
        ar.release()
        fz = S.fence(dummy)
        S.finish(out_dmas + [fz])
        S.emit(st)
    return nc


_CACHE = {}


def _tables(half, NT=32):
    t = np.zeros((128, T_END), np.float32)
    p = np.arange(128, dtype=np.float64)
    slopes = [2.0 ** (-8.0 * (i + 1) / 4) for i in range(4)]
    for h in range(4):
        s = slopes[h]
        for dl in range(32):
            t[:, T_TABD + h * 96 + dl] = s * (p - 256 * dl)
            if half == 0:
                t[:, T_TABD + h * 96 + 32 + dl] = NEG if dl == 0 else s * (p - 256 * dl + 128)
            else:
                t[:, T_TABD + h * 96 + 32 + dl] = s * (p - 256 * dl - 128)
            t[:, T_TABD + h * 96 + 64 + dl] = s * (p - 16 - 128 * (2 * dl + half))
        k = p[:, None]
        q = p[None, :]
        vis = (k // 64) <= (q // 64)
        t[:, T_DG + h * 128:T_DG + (h + 1) * 128] = np.where(vis, -s * np.abs(q - k) + s * q, NEG)
    k = p[:, None]
    q = p[None, :]
    t[:, T_FM:T_FM + 128] = np.where(k <= q, 0.0, NEG)
    t[:, T_FO] = 0.0 if half == 1 else NEG
    t[:, T_RM] = (p < 16)
    NKT = 2 * NT + 1
    rank = np.zeros(NKT)
    rank[2 * NT] = 0
    for T in range(2 * NT):
        idx = T // 2 + (0 if T % 2 == half else NT)
        rank[idx] = T + 1
    t[0:NKT, T_PREF:T_PREF + NKT] = (rank[:, None] < rank[None, :])
    t[:, T_TRI:T_TRI + 128] = (k <= q)
    t[:, T_SEL:T_SEL + 128] = (k == 64)
    t[:, T_ID:T_ID + 128] = (k == q)
    return t


def kernel(x, meta, g_mix, w_in, lambda_q1, lambda_k1, lambda_q2, lambda_k2, g_subln, b_f,
           w_branch_a, w_branch_f, w_out, g_ffn, w_gate, w_up, w_down, g_final):
    x = np.asarray(x, np.float32)
    f = lambda a: np.ascontiguousarray(np.asarray(a, np.float32))
    if "nc" not in _CACHE:
        _CACHE["nc"] = build_program()
    nc = _CACHE["nc"]
    w_in0, w_a0, w_f0, w_o0 = f(w_in[0]), f(w_branch_a[0]), f(w_branch_f[0]), f(w_out[0])
    w_g0, w_u0, w_d0 = f(w_gate[0]), f(w_up[0]), f(w_down[0])
    meta = f(meta)
    in_maps = []
    for c in range(8):
        b, half = c // 2, c % 2
        xb = x[b].reshape(64, 128, D)
        toks = np.concatenate([xb[half::2].reshape(NOWN, D), xb[1 - half::2].reshape(NOWN, D), meta,
                               np.zeros((112, D), np.float32)], axis=0)
        t = _tables(half)
        gv = np.concatenate([f(g_mix[0]).reshape(16, 128).T, f(g_ffn[0]).reshape(16, 128).T, f(g_final).reshape(16, 128).T], axis=1)
        t[:, T_G:T_G + 48] = gv
        t[:, T_GSUB:T_GSUB + 256] = f(g_subln[0])[None, :]
        t[:, T_BF:T_BF + 8] = f(b_f[0])[None, :]
        t[:, T_LAM:T_LAM + 512] = np.concatenate([f(lambda_q1[0]), f(lambda_k1[0]), f(lambda_q2[0]), f(lambda_k2[0])])[None, :]
        in_maps.append({"xT": np.ascontiguousarray(toks.T), "tabs": t, "w_in": w_in0, "w_a": w_a0, "w_f": w_f0, "w_o": w_o0,
                        "w_g": w_g0, "w_u": w_u0, "w_d": w_d0})
    res = run_bass_kernel_spmd(nc, in_maps, core_ids=list(range(8)))
    out = np.zeros((4, SEQ, D), np.float32)
    for c in range(8):
        b, half = c // 2, c % 2
        o = np.asarray(res.results[c]["outT"], np.float32).T.reshape(32, 128, D)
        out[b].reshape(64, 128, D)[half::2] = o
    return out
```

```python
import numpy as np
from contextlib import ExitStack
import concourse.bass as bass
import concourse.mybir as mybir
from concourse.bass_utils import run_bass_kernel_spmd

F32 = mybir.dt.float32
BF16 = mybir.dt.bfloat16
AF = mybir.ActivationFunctionType
ALU = mybir.AluOpType

N_DMA_SEMS = 48
N_SW_SEMS = 16


class Sched:
    ENGS = ("pe", "act", "dve", "pool", "sp")

    def __init__(self, nc):
        self.nc = nc
        self.ops = []
        self.last_w = {}
        self.readers = {}
        self.dma_uses = [0] * N_DMA_SEMS
        self.dma_rr = 0
        self.dma_rr_sw = 0
        self.final_deps = []

    def add(self, eng, fn, reads=(), writes=(), dma=False):
        oid = len(self.ops)
        reads = list(reads) + ["__all__"]
        deps = set()
        for k in reads:
            w = self.last_w.get(k)
            if w is not None:
                deps.add(w)
        for k in writes:
            w = self.last_w.get(k)
            if w is not None:
                deps.add(w)
            for r in self.readers.get(k, ()):
                deps.add(r)
        deps.discard(oid)
        best = {}
        keep = []
        bestd = {}
        for d in deps:
            od = self.ops[d]
            if od["dma"]:
                s_ = od["dsem"]
                if s_ not in bestd or od["dval"] > self.ops[bestd[s_]]["dval"]:
                    bestd[s_] = d
            else:
                e = od["eng"]
                if e not in best or d > best[e]:
                    best[e] = d
        keep.extend(bestd.values())
        for e, d in best.items():
            if e == "pe" and eng == "pe" and not dma:
                continue
            keep.append(d)
        op = {"eng": eng, "fn": fn, "deps": keep, "dma": dma, "signal": False,
              "seq": None, "dsem": None, "dval": None}
        if dma:
            if eng == "pool":
                s = N_DMA_SEMS - N_SW_SEMS + self.dma_rr_sw
                self.dma_rr_sw = (self.dma_rr_sw + 1) % N_SW_SEMS
            else:
                s = self.dma_rr
                self.dma_rr = (self.dma_rr + 1) % (N_DMA_SEMS - N_SW_SEMS)
            self.dma_uses[s] += 1
            op["dsem"] = s
            op["dval"] = 16 * self.dma_uses[s]
        for d in keep:
            self.ops[d]["signal"] = True
        self.ops.append(op)
        for k in reads:
            self.readers.setdefault(k, []).append(oid)
        for k in writes:
            self.last_w[k] = oid
            self.readers[k] = []
        return oid

    def fence(self, dummy_ap):
        return self.add("dve", lambda e: e.memset(dummy_ap, 0.0), writes=["__all__"])

    def finish(self, oids):
        self.final_deps = list(oids)
        for d in oids:
            self.ops[d]["signal"] = True

    def emit(self, stack):
        nc = self.nc
        esem = {e: stack.enter_context(nc.semaphore("es_" + e)) for e in self.ENGS}
        dsem = [stack.enter_context(nc.semaphore("ds_%d" % i)) for i in range(N_DMA_SEMS)]
        cnt = {e: 0 for e in self.ENGS}
        for op in self.ops:
            if not op["dma"] and op["signal"]:
                cnt[op["eng"]] += 1
                op["seq"] = cnt[op["eng"]]
        block = stack.enter_context(nc.Block())
        ops = self.ops
        final_deps = self.final_deps

        def stream(ename, eng):
            waited = {}

            def wait_for(d):
                od = ops[d]
                if od["dma"]:
                    key = ("d", od["dsem"])
                    val = od["dval"]
                    sem = dsem[od["dsem"]]
                else:
                    key = ("e", od["eng"])
                    val = od["seq"]
                    sem = esem[od["eng"]]
                if waited.get(key, 0) >= val:
                    return
                waited[key] = val
                eng.wait_ge(sem, val)

            for op in ops:
                if op["eng"] != ename:
                    continue
                for d in op["deps"]:
                    wait_for(d)
                if op["dma"]:
                    prev = op["dval"] - 16
                    key = ("d", op["dsem"])
                    if prev > 0 and waited.get(key, 0) < prev:
                        waited[key] = prev
                        eng.wait_ge(dsem[op["dsem"]], prev)
                    ins = op["fn"](eng)
                    ins.then_inc(dsem[op["dsem"]], 16)
                else:
                    ins = op["fn"](eng)
                    if op["signal"]:
                        ins.then_inc(esem[ename], 1)
            if ename == "sp":
                for d in final_deps:
                    wait_for(d)

        @block.sync
        def _(eng):
            stream("sp", eng)

        @block.scalar
        def _(eng):
            stream("act", eng)

        @block.vector
        def _(eng):
            stream("dve", eng)

        @block.gpsimd
        def _(eng):
            stream("pool", eng)

        @block.tensor
        def _(eng):
            stream("pe", eng)


class Arena:
    def __init__(self, ap_f32, nbytes):
        self.base = ap_f32
        self.nbytes = nbytes
        self.off = 0
        self.marks = []

    def alloc(self, shape_free, dtype):
        esz = 4 if dtype == F32 else 2
        n = int(np.prod(shape_free))
        nb = (n * esz + 31) // 32 * 32
        assert self.off + nb <= self.nbytes, ("arena overflow", self.off, nb, self.nbytes)
        a = self.base[:, self.off // 4:(self.off + nb) // 4]
        if dtype != F32:
            a = a.bitcast(dtype)
        a = a[:, 0:n]
        self.off += nb
        if len(shape_free) == 2:
            a = a.rearrange("p (a b) -> p a b", a=shape_free[0])
        elif len(shape_free) == 3:
            a = a.rearrange("p (a b c) -> p a b c", a=shape_free[0], b=shape_free[1])
        return a

    def mark(self):
        self.marks.append(self.off)

    def release(self):
        self.off = self.marks.pop()

D = 2048
KC = 16
SEQ = 8192
NOWN = 4096
NTOK = 8320
DFF = 5632
FC = 44
NEG = -30000.0
SCALE = 128 ** -0.5
LAMBDA_INIT = 0.2
T_TABD = 0
T_DG = T_TABD + 4 * 96
T_FM = T_DG + 4 * 128
T_FO = T_FM + 128
T_RM = T_FO + 1
T_PREF = T_RM + 1
T_TRI = T_PREF + 65
T_SEL = T_TRI + 128
T_ID = T_SEL + 128
T_G = T_ID + 128
T_GSUB = T_G + 48
T_BF = T_GSUB + 256
T_LAM = T_BF + 8
T_END = T_LAM + 512


def build_program(NT=32, dbg=False, upto=3):
    nc = bass.Bass("TRN2", target_bir_lowering=False)
    NOWN_ = NT * 128
    NKT = 2 * NT + 1
    META = 2 * NT
    NTOK_ = NKT * 128
    SK = "ExternalOutput" if dbg else "Internal"
    xT = nc.dram_tensor("xT", [D, NTOK_], F32, kind="ExternalInput").ap()
    tabs_d = nc.dram_tensor("tabs", [128, T_END], F32, kind="ExternalInput").ap()
    w_in = nc.dram_tensor("w_in", [D, 10248], F32, kind="ExternalInput").ap()
    w_a = nc.dram_tensor("w_a", [1024, D], F32, kind="ExternalInput").ap()
    w_f = nc.dram_tensor("w_f", [1024, D], F32, kind="ExternalInput").ap()
    w_o = nc.dram_tensor("w_o", [D, D], F32, kind="ExternalInput").ap()
    w_g = nc.dram_tensor("w_g", [D, DFF], F32, kind="ExternalInput").ap()
    w_u = nc.dram_tensor("w_u", [D, DFF], F32, kind="ExternalInput").ap()
    w_d = nc.dram_tensor("w_d", [DFF, D], F32, kind="ExternalInput").ap()
    outT = nc.dram_tensor("outT", [D, NOWN_], F32, kind="ExternalOutput").ap()
    kT_s = nc.dram_tensor("kT_s", [16, 128, NTOK_], BF16, kind=SK).ap()
    qT_s = nc.dram_tensor("qT_s", [16, 128, NOWN_], BF16, kind=SK).ap()
    v_s = nc.dram_tensor("v_s", [NTOK_, D], BF16, kind=SK).ap()
    oT_s = nc.dram_tensor("oT_s", [D, NOWN_], BF16, kind=SK).ap()
    wsc = nc.dram_tensor("wsc", [58, 128, 8192], BF16, kind="Internal").ap()

    with ExitStack() as st:
        ARENA_W = 53200
        arena_t = st.enter_context(nc.sbuf_tensor("arena", [128, ARENA_W], F32))
        ar = Arena(arena_t[:, :], ARENA_W * 4)
        ps = [st.enter_context(nc.psum_tensor("ps%d" % i, [128, 512], F32)) for i in range(8)]
        S = Sched(nc)
        uid = [0]

        def U(p):
            uid[0] += 1
            return "%s#%d" % (p, uid[0])

        tabs = ar.alloc((T_END,), F32)
        S.add("sp", lambda e: e.dma_start(out=tabs, in_=tabs_d), writes=["tabs"], dma=True)
        ident = ar.alloc((128,), BF16)
        S.add("dve", lambda e: e.tensor_copy(out=ident, in_=tabs[:, T_ID:T_ID + 128]), reads=["tabs"], writes=["cst"])
        ones_b = ar.alloc((128,), BF16)
        S.add("pool", lambda e: e.memset(ones_b, 1.0), writes=["cst1"])
        ones_f = ar.alloc((128,), F32)
        S.add("pool", lambda e: e.memset(ones_f, 1.0), writes=["cst2"])
        gsub = ar.alloc((256,), F32)
        S.add("dve", lambda e: e.tensor_scalar(out=gsub, in0=tabs[:, T_GSUB:T_GSUB + 256], scalar1=1.0 - LAMBDA_INIT,
                                              scalar2=None, op0=ALU.mult), reads=["tabs"], writes=["cst3"])
        lt = ar.alloc((128,), F32)
        lsum = ar.alloc((2,), F32)
        nlam = ar.alloc((1,), F32)
        for i in range(2):
            a0 = T_LAM + i * 256
            S.add("dve", lambda e, a0=a0: e.tensor_tensor(out=lt, in0=tabs[:, a0:a0 + 128], in1=tabs[:, a0 + 128:a0 + 256],
                                                          op=ALU.mult), reads=["tabs"], writes=["lt"])
            S.add("dve", lambda e, i=i: e.tensor_reduce(out=lsum[:, i:i + 1], in_=lt, axis=mybir.AxisListType.X, op=ALU.add),
                  reads=["lt"], writes=["lsum"])
        S.add("act", lambda e: e.activation(out=lsum, in_=lsum, func=AF.Exp), reads=["lsum"], writes=["lsum"])
        S.add("dve", lambda e: e.tensor_tensor(out=nlam, in0=lsum[:, 1:2], in1=lsum[:, 0:1], op=ALU.subtract),
              reads=["lsum"], writes=["nlam"])
        S.add("dve", lambda e: e.tensor_scalar(out=nlam, in0=nlam, scalar1=-LAMBDA_INIT, scalar2=None, op0=ALU.add),
              reads=["nlam"], writes=["nlam"])
        SPL = ar.alloc((NKT, 8), F32)
        dummy = ar.alloc((8,), F32)
        ev = [0]

        def evac(out, in_, reads, writes):
            ev[0] += 1
            if ev[0] % 2:
                S.add("dve", lambda e: e.tensor_copy(out=out, in_=in_), reads=reads, writes=writes)
            else:
                S.add("act", lambda e: e.copy(out=out, in_=in_), reads=reads, writes=writes)

        bank_rr = [0]

        def next_bank():
            b = 2 + bank_rr[0] % 6
            bank_rr[0] += 1
            return b

        xT_v = xT.rearrange("(kc p) t -> p kc t", p=128)

        def rms_stats(src, n, sq, ksrc, eps=1e-6):
            S.add("act", lambda e: e.activation(out=sq[:, :, 0:n], in_=src, func=AF.Square), reads=[ksrc], writes=["sq"])
            for kc in range(KC):
                S.add("pe", lambda e, kc=kc: e.matmul(ps[0][0:1, 0:n], lhsT=ones_b[:, 0:1], rhs=sq[:, kc, 0:n],
                                                      start=(kc == 0), stop=(kc == KC - 1)),
                      reads=["sq", "cst1"], writes=["ps0"])
            S.add("act", lambda e: e.activation(out=rs_row[0:1, 0:n], in_=ps[0][0:1, 0:n], func=AF.Sqrt, bias=eps_t[0:1, 0:1],
                                                scale=1.0 / D), reads=["ps0", "eps"], writes=["rsrow"])
            S.add("dve", lambda e: e.reciprocal(out=rs_row[0:1, 0:n], in_=rs_row[0:1, 0:n]), reads=["rsrow"], writes=["rsrow"])
            S.add("pe", lambda e: e.matmul(ps[1][:, 0:n], lhsT=ones_f[0:1, :], rhs=rs_row[0:1, 0:n], start=True, stop=True),
                  reads=["rsrow", "cst2"], writes=["ps1"])

        rs_row = ar.alloc((512,), F32)
        eps_t = ar.alloc((2,), F32)
        S.add("pool", lambda e: e.memset(eps_t[:, 0:1], 1e-6), writes=["eps"])
        S.add("pool", lambda e: e.memset(eps_t[:, 1:2], 1e-5), writes=["eps"])

        def norm_apply(src, n, gofs, dst, ksrc, kdst):
            for kc in range(KC):
                S.add("dve", lambda e, kc=kc: e.scalar_tensor_tensor(
                    out=dst[:, kc, 0:n], in0=src[:, kc, 0:n], scalar=tabs[:, T_G + gofs + kc:T_G + gofs + kc + 1],
                    in1=ps[1][:, 0:n], op0=ALU.mult, op1=ALU.mult), reads=[ksrc, "ps1", "tabs"], writes=[kdst])

        def load_slab(dst, w_ap, r0, nk, c0, ncol, key):
            src = w_ap[r0:r0 + nk * 128, c0:c0 + ncol].rearrange("(kc p) c -> p kc c", p=128)
            S.add("pool", lambda e: e.dma_start(out=dst[:, 0:nk, 0:ncol], in_=src), writes=[key], dma=True)

        ar.mark()
        SBT = min(1024, NOWN_)
        uTs = [ar.alloc((KC, SBT), BF16) for _ in range(2)]
        xs2 = [ar.alloc((KC, 512), F32) for _ in range(2)]
        sq = ar.alloc((KC, 512), BF16)
        slabs = [ar.alloc((KC, 512), BF16) for _ in range(2)]
        wz = ar.alloc((KC, 8), BF16)
        stg = [ar.alloc((512,), BF16) for _ in range(6)]
        ztmp = ar.alloc((8,), F32)
        load_slab(wz, w_in, 0, KC, 6144, 8, "wz")
        sbs = [(i * SBT, SBT, True) for i in range(NOWN_ // SBT)] + [(NOWN_ + i * SBT, SBT, False) for i in range(NOWN_ // SBT)] + [(2 * NOWN_, 128, False)]
        xi = [0]
        si = [0]
        gi = [0]
        out_dmas = []
        slab_list = []
        for s in range(2):
            slab_list.append((1024 + s * 512, "KT", s * 4))
        for s in range(2):
            slab_list.append((4096 + s * 512, "KT", 8 + s * 4))
        for s in range(2):
            slab_list.append((2048 + s * 512, "V", s * 512))
        for s in range(2):
            slab_list.append((5120 + s * 512, "V", 1024 + s * 512))
        for s in range(2):
            slab_list.append((0 + s * 512, "QT", s * 4))
        for s in range(2):
            slab_list.append((3072 + s * 512, "QT", 8 + s * 4))
        def emit_norm(sbi):
            t0, nt, own = sbs[sbi]
            uT = uTs[sbi % 2]
            ku = "uT%d" % (sbi % 2)
            nch = (nt + 511) // 512
            for c in range(nch):
                n = min(512, nt - c * 512)
                xs = xs2[xi[0] % 2]
                kx = "xs%d" % (xi[0] % 2)
                xi[0] += 1
                S.add("sp", lambda e, xs=xs, n=n, a=t0 + c * 512: e.dma_start(out=xs[:, :, 0:n], in_=xT_v[:, :, a:a + n]),
                      writes=[kx], dma=True)
                rms_stats(xs[:, :, 0:n], n, sq, kx)
                norm_apply(xs, n, 0, uT[:, :, c * 512:c * 512 + n], kx, ku)

        emit_norm(0)
        for sbi, (t0, nt, own) in enumerate(sbs):
            nch = (nt + 511) // 512
            uT = uTs[sbi % 2]
            ku = "uT%d" % (sbi % 2)
            slab_n = [0]
            for tt in range(nt // 128):
                b = next_bank()
                for kc in range(KC):
                    S.add("pe", lambda e, uT=uT, kc=kc, tt=tt, b=b: e.matmul(ps[b][:, 0:8], lhsT=uT[:, kc, tt * 128:(tt + 1) * 128],
                                                                      rhs=wz[:, kc, :], start=(kc == 0), stop=(kc == KC - 1)),
                          reads=[ku, "wz"], writes=["ps%d" % b])
                tile_idx = (t0 // 128 + tt) if t0 < 2 * NOWN_ else META
                S.add("dve", lambda e, b=b: e.tensor_tensor(out=ztmp, in0=ps[b][:, 0:8], in1=tabs[:, T_BF:T_BF + 8], op=ALU.add),
                      reads=["ps%d" % b, "tabs"], writes=["ztmp"])
                S.add("act", lambda e: e.activation(out=ztmp, in_=ztmp, func=AF.Exp, scale=-1.0), reads=["ztmp"], writes=["ztmp"])
                S.add("act", lambda e, ti=tile_idx: e.activation(out=SPL[:, ti, :], in_=ztmp, func=AF.Ln, bias=ones_f[:, 0:1], scale=1.0),
                      reads=["ztmp", "cst2"], writes=["SPL"])
            for (c0, kind, idx) in slab_list:
                if kind == "QT" and not own:
                    continue
                if slab_n[0] == 1 and sbi + 1 < len(sbs):
                    emit_norm(sbi + 1)
                slab_n[0] += 1
                sl = slabs[si[0] % 2]
                ks = "slab%d" % (si[0] % 2)
                si[0] += 1
                load_slab(sl, w_in, 0, KC, c0, 512, ks)
                if kind in ("KT", "QT"):
                    dst_s = kT_s if kind == "KT" else qT_s
                    for ct in range(4):
                        for c in range(nch):
                            n = min(512, nt - c * 512)
                            b = next_bank()
                            for kc in range(KC):
                                S.add("pe", lambda e, uT=uT, kc=kc, ct=ct, c=c, n=n, b=b, sl=sl: e.matmul(
                                    ps[b][:, 0:n], lhsT=sl[:, kc, ct * 128:(ct + 1) * 128], rhs=uT[:, kc, c * 512:c * 512 + n],
                                    start=(kc == 0), stop=(kc == KC - 1)), reads=[ku, ks], writes=["ps%d" % b])
                            sg = stg[gi[0] % 6]
                            kg = "stg%d" % (gi[0] % 6)
                            gi[0] += 1
                            evac(sg[:, 0:n], ps[b][:, 0:n], ["ps%d" % b], [kg])
                            S.add("sp", lambda e, sg=sg, n=n, m=idx + ct, a=t0 + c * 512, dst_s=dst_s: e.dma_start(
                                out=dst_s[m, :, a:a + n], in_=sg[:, 0:n]), reads=[kg], writes=[U("scr")], dma=True)
                else:
                    for tt in range(nt // 128):
                        b = next_bank()
                        for kc in range(KC):
                            S.add("pe", lambda e, uT=uT, kc=kc, tt=tt, b=b, sl=sl: e.matmul(
                                ps[b][:, :], lhsT=uT[:, kc, tt * 128:(tt + 1) * 128], rhs=sl[:, kc, :],
                                start=(kc == 0), stop=(kc == KC - 1)), reads=[ku, ks], writes=["ps%d" % b])
                        sg = stg[gi[0] % 6]
                        kg = "stg%d" % (gi[0] % 6)
                        gi[0] += 1
                        evac(sg, ps[b][:, :], ["ps%d" % b], [kg])
                        S.add("sp", lambda e, sg=sg, a=t0 + tt * 128, idx=idx: e.dma_start(
                            out=v_s[a:a + 128, idx:idx + 512], in_=sg), reads=[kg], writes=[U("scr")], dma=True)
        ar.release()
        S.fence(dummy)

        ar.mark()
        Wc = ar.alloc((NKT, 8), F32)
        A = ar.alloc((NKT, 8), F32)
        tot = ar.alloc((8,), F32)
        totb = ar.alloc((128,), F32)
        CB = ar.alloc((NT, 8), F32)
        trif = tabs[:, T_TRI:T_TRI + 128]
        self64 = tabs[:, T_SEL:T_SEL + 128]
        pref = tabs[0:NKT, T_PREF:T_PREF + NKT]
        S.add("dve", lambda e: e.tensor_scalar(out=SPL[:, META, :], in0=SPL[:, META, :], scalar1=tabs[:, T_RM:T_RM + 1], scalar2=None,
                                              op0=ALU.mult), reads=["SPL", "tabs"], writes=["SPL"])
        SPLf = SPL.rearrange("p a b -> p (a b)")
        Wcf = Wc.rearrange("p a b -> p (a b)")
        for c0_ in range(0, NKT * 8, 512):
            n_ = min(512, NKT * 8 - c0_)
            S.add("pe", lambda e, c0_=c0_, n_=n_: e.matmul(ps[2][:, 0:n_], lhsT=trif, rhs=SPLf[:, c0_:c0_ + n_], start=True, stop=True),
                  reads=["SPL", "tabs"], writes=["ps2"])
            S.add("dve", lambda e, c0_=c0_, n_=n_: e.tensor_copy(out=Wcf[:, c0_:c0_ + n_], in_=ps[2][:, 0:n_]), reads=["ps2"], writes=["Wc"])
        for h in range(8):
            S.add("pe", lambda e, h=h: e.matmul(ps[4][0:NKT, h:h + 1], lhsT=SPL[:, :, h], rhs=ones_f[:, 0:1], start=True, stop=True),
                  reads=["SPL", "cst2"], writes=["ps4"])
        S.add("dve", lambda e: e.tensor_copy(out=tot[0:NKT, :], in_=ps[4][0:NKT, 0:8]), reads=["ps4"], writes=["tot"])
        for h in range(8):
            S.add("dve", lambda e, h=h: e.tensor_scalar(out=totb[0:NKT, :], in0=ones_f[0:NKT, :], scalar1=tot[0:NKT, h:h + 1], scalar2=None,
                                                         op0=ALU.mult), reads=["tot", "cst2"], writes=["totb"])
            S.add("pe", lambda e: e.matmul(ps[5][:, 0:NKT], lhsT=totb[0:NKT, :], rhs=pref, start=True, stop=True),
                  reads=["totb", "tabs"], writes=["ps5"])
            S.add("dve", lambda e, h=h: e.tensor_tensor(out=A[:, :, h], in0=Wc[:, :, h], in1=ps[5][:, 0:NKT], op=ALU.add),
                  reads=["Wc", "ps5"], writes=["A"])
        Af = A.rearrange("p a b -> p (a b)")
        S.add("pe", lambda e: e.matmul(ps[6][:, 0:NT * 8], lhsT=self64, rhs=Af[:, 0:NT * 8], start=True, stop=True), reads=["A", "tabs"], writes=["ps6"])
        S.add("dve", lambda e: e.tensor_copy(out=CB.rearrange("p a b -> p (a b)"), in_=ps[6][:, 0:NT * 8]), reads=["ps6"], writes=["CB"])

        wreg = {}
        wlist = []
        for sgp in range(4):
            wlist += [(w_in, 0, KC, 6152 + sgp * 512, 512), (w_a, 0, 8, sgp * 512, 512),
                      (w_in, 0, KC, 8200 + sgp * 512, 512), (w_f, 0, 8, sgp * 512, 512)]
        for sgp in range(4):
            wlist.append((w_o, 0, KC, sgp * 512, 512))
        for sgp in range(11):
            wlist += [(w_g, 0, KC, sgp * 512, 512), (w_u, 0, KC, sgp * 512, 512)]
        for ct in range(16):
            wlist.append((w_d, 0, FC, ct * 128, 128))
        for idx, (w_ap, r0, nk, c0, ncol) in enumerate(wlist):
            wreg[(w_ap.tensor.name, r0, nk, c0, ncol)] = idx
            src = w_ap[r0:r0 + nk * 128, c0:c0 + ncol].rearrange("(kc p) c -> p kc c", p=128)
            dst = wsc[idx][:, 0:nk * ncol].rearrange("p (a b) -> p a b", a=nk)
            S.add("pool", lambda e, src=src, dst=dst: e.dma_start(out=dst, in_=src), writes=[U("wsc")], dma=True)
        KTb = [ar.alloc((NTOK_,), BF16) for _ in range(3)]
        QTb = [ar.alloc((NOWN_,), BF16) for _ in range(3)]
        q_i = [0]
        head_no = [0]
        Vb = [ar.alloc((NKT, 257), BF16) for _ in range(2)]
        oTh = ar.alloc((2, NOWN_), BF16)
        dtmp = [ar.alloc((128,), F32) for _ in range(2)]
        fin4 = [ar.alloc((8,), F32) for _ in range(4)]
        ofp4 = [ar.alloc((256,), F32) for _ in range(4)]
        junk = ar.alloc((256,), F32)
        obf4 = [ar.alloc((256,), BF16) for _ in range(4)]
        kq_i = [0]
        v_i = [0]
        pt_i = [0]
        dt_i = [0]
        ss_i = [0]
        acc_i = [0]
        bf_i = [0]
        v_sv = v_s.rearrange("(kt p) c -> p kt c", p=128)

        NPB = 7
        NBFA = 1
        PB = [ar.alloc((512,), BF16) for _ in range(NPB)]
        ewD = ar.alloc((96,), F32)
        bFas = [ar.alloc((NT, NKT), F32) for _ in range(NBFA)]
        o0 = ar.alloc((4, 256), F32)
        NG = NT // 4

        def prep_head(is_diff, h):
            ncol = 256 if is_diff else 128
            maps = [2 * h, 2 * h + 1] if is_diff else [8 + h]
            vcol = h * 256 if is_diff else 1024 + h * 128
            KT, QT, kk, kqk = [], [], [], []
            for m in maps:
                i = kq_i[0] % 3
                kq_i[0] += 1
                S.add("sp", lambda e, i=i, m=m: e.dma_start(out=KTb[i], in_=kT_s[m]), writes=["KT%d" % i], dma=True)
                iq = q_i[0] % 3
                q_i[0] += 1
                S.add("sp", lambda e, iq=iq, m=m: e.dma_start(out=QTb[iq], in_=qT_s[m]), writes=["QT%d" % iq], dma=True)
                KT.append(KTb[i]); QT.append(QTb[iq]); kk.append("KT%d" % i); kqk.append("QT%d" % iq)
            vi = v_i[0] % 2
            v_i[0] += 1
            V = Vb[vi]
            kv = "V%d" % vi
            for g0 in range(0, NKT, 13):
                g1 = min(NKT, g0 + 13)
                S.add("sp", lambda e, g0=g0, g1=g1: e.dma_start(out=V[:, g0:g1, 0:ncol], in_=v_sv[:, g0:g1, vcol:vcol + ncol]),
                      writes=[kv], dma=True)
            S.add("dve", lambda e: e.memset(V[:, :, ncol:ncol + 1], 1.0), writes=[kv])
            pool_ok = head_no[0] >= 2
            head_no[0] += 1
            return dict(KT=KT, QT=QT, kk=kk, kqk=kqk, V=V, kv=kv, pool_ok=pool_ok)

        def attention_head(is_diff, h, PR):
            nmap = 2 if is_diff else 1
            ncol = 256 if is_diff else 128
            orow = h * 256 if is_diff else 1024 + h * 128
            KT, QT, kk, kqk, V, kv, pool_ok = PR["KT"], PR["QT"], PR["kk"], PR["kqk"], PR["V"], PR["kv"], PR["pool_ok"]
            if is_diff:
                ewb, kew = ewD, "ewD"
                S.add("act", lambda e: e.activation(out=ewD, in_=tabs[:, T_TABD + h * 96:T_TABD + (h + 1) * 96], func=AF.Exp),
                      reads=["tabs"], writes=[kew])
            else:
                bi = bf_i[0] % len(bFas)
                bf_i[0] += 1
                ewb, kew = bFas[bi], "bFa%d" % bi
                for j in range(NT):
                    S.add("dve", lambda e, j=j: e.tensor_scalar(out=ewb[:, j, :], in0=A[:, :, h], scalar1=CB[:, j, h:h + 1], scalar2=60.0,
                                                                 op0=ALU.subtract, op1=ALU.min), reads=["A", "CB"], writes=[kew])
                    S.add("dve", lambda e, j=j: e.tensor_tensor(out=ewb[:, j, NT + j:NT + j + 1], in0=ewb[:, j, NT + j:NT + j + 1],
                                                                 in1=tabs[:, T_FO:T_FO + 1], op=ALU.add), reads=[kew, "tabs"], writes=[kew])
                S.add("act", lambda e: e.activation(out=ewb, in_=ewb, func=AF.Exp), reads=[kew], writes=[kew])
            ewD_ = ewb
            bFa = ewb

            def ew_ap(G, kind, i, b0, nb):
                j0 = 4 * G + b0
                if is_diff:
                    if kind == "own":
                        c0 = j0 - i
                    elif kind == "oth":
                        c0 = 32 + j0 - i
                    else:
                        c0 = 64 + j0
                    return ewD_[:, c0:c0 + nb]
                kt = i if kind == "own" else (NT + i if kind == "oth" else META)
                return bFa[:, j0:j0 + nb, kt]

            items = []
            for G in range(NG):
                for c in range(nmap):
                    lst = [dict(kt=META, kp=16, kind="meta", i=0, b0=0)]
                    for i in range(4 * G):
                        lst.append(dict(kt=i, kp=128, kind="own", i=i, b0=0))
                        lst.append(dict(kt=NT + i, kp=128, kind="oth", i=i, b0=0))
                    for a in range(4):
                        lst.append(dict(kt=NT + 4 * G + a, kp=128, kind="oth", i=4 * G + a, b0=a))
                        lst.append(dict(kt=4 * G + a, kp=128, kind="own", i=4 * G + a, b0=a, diag=True))
                    if is_diff:
                        slope = 2.0 ** (-8.0 * (h + 1) / 4)
                        keep = []
                        for it in lst:
                            if it["kind"] == "meta" or it.get("diag"):
                                keep.append(it)
                                continue
                            dl = 4 * G + it["b0"] - it["i"]
                            mx = slope * (127 - 256 * dl + (128 if it["kind"] == "oth" else 0))
                            if mx > -120.0:
                                keep.append(it)
                        lst = keep
                    for n_, it in enumerate(lst):
                        it.update(G=G, c=c, first=(n_ == 0), last=(n_ == len(lst) - 1))
                        items.append(it)
                    items.append(dict(T=True, G=G, c=c))
            par = [0]

            def emit_st(n, it):
                if it.get("T"):
                    return
                bank = n % 3
                G, c, kt, kp, b0 = it["G"], it["c"], it["kt"], it["kp"], it["b0"]
                q0 = (4 * G + b0) * 128
                q1 = (4 * G + 4) * 128
                S.add("pe", lambda e: e.matmul(ps[bank][0:kp, b0 * 128:512], lhsT=KT[c][:, kt * 128:kt * 128 + kp], rhs=QT[c][:, q0:q1],
                                               start=True, stop=True), reads=[kk[c], kqk[c]], writes=["ps%d" % bank])

            def emit_rest(n, it):
                if it.get("T"):
                    return
                bank = n % 3
                kbank = "ps%d" % bank
                P = PB[n % NPB]
                kPb = ["PB%d_%d" % (n % NPB, b_) for b_ in range(4)]
                kP = kPb[0]
                G, c, kt, kp, b0, kind = it["G"], it["c"], it["kt"], it["kp"], it["b0"], it["kind"]
                c0 = b0 * 128
                diag = it.get("diag", False)
                if diag:
                    di = dt_i[0] % 2
                    dt_i[0] += 1
                    dtm = dtmp[di]
                    kd = "dt%d" % di
                    mk = tabs[:, T_DG + h * 128:T_DG + (h + 1) * 128] if is_diff else tabs[:, T_FM:T_FM + 128]
                    S.add("dve", lambda e: e.scalar_tensor_tensor(out=dtm, in0=ps[bank][:, c0:c0 + 128], scalar=SCALE, in1=mk,
                                                                  op0=ALU.mult, op1=ALU.add), reads=["tabs"], writes=[kd, kbank])
                    S.add("act", lambda e: e.activation(out=P[:, c0:c0 + 128], in_=dtm, func=AF.Exp), reads=[kd], writes=[kPb[b0]])
                    if c0 + 128 < 512:
                        S.add("act", lambda e: e.activation(out=P[:, c0 + 128:512], in_=ps[bank][:, c0 + 128:512], func=AF.Exp, scale=SCALE),
                              reads=[kbank], writes=kPb[b0 + 1:4])
                else:
                    S.add("act", lambda e: e.activation(out=P[0:kp, c0:512], in_=ps[bank][0:kp, c0:512], func=AF.Exp, scale=SCALE),
                          reads=[kbank], writes=kPb[b0:4])
                sb0 = b0 + 1 if (diag and is_diff) else b0
                nb = 4 - sb0
                if nb > 0:
                    ew = ew_ap(G, kind, it["i"], sb0, nb)[0:kp].unsqueeze(2).broadcast_to([kp, nb, 128])
                    Pv = P[0:kp, sb0 * 128:512].rearrange("p (a b) -> p a b", a=nb)
                    par[0] += 1
                    eng_ = "pool" if (par[0] % 2 == 0 and pool_ok) else "dve"
                    if eng_ == "pool":
                        S.add(eng_, lambda e: e.tensor_tensor(out=Pv, in0=Pv, in1=ew, op=ALU.mult), reads=[kew], writes=kPb[sb0:4])
                    else:
                        ew2 = ew_ap(G, kind, it["i"], sb0, nb)
                        for bb in range(nb):
                            S.add("dve", lambda e, bb=bb: e.tensor_scalar(out=P[0:kp, (sb0 + bb) * 128:(sb0 + bb + 1) * 128],
                                                                         in0=P[0:kp, (sb0 + bb) * 128:(sb0 + bb + 1) * 128],
                                                                         scalar1=ew2[0:kp, bb:bb + 1], scalar2=None, op0=ALU.mult),
                                  reads=[kew], writes=[kPb[sb0 + bb]])

            def emit_pv(n, it):
                if it.get("T"):
                    finalize_pass(n, it["G"], it["c"])
                    return
                P = PB[n % NPB]
                kPb = ["PB%d_%d" % (n % NPB, b_) for b_ in range(4)]
                G, c, kt, kp, b0 = it["G"], it["c"], it["kt"], it["kp"], it["b0"]
                diag = it.get("diag", False)
                for b in range(b0, 4):
                    S.add("pe", lambda e, b=b: e.matmul(ps[4 + b][:, 0:ncol + 1], lhsT=P[0:kp, b * 128:(b + 1) * 128], rhs=V[0:kp, kt, 0:ncol + 1],
                                                        start=it["first"], stop=(it["last"] or (b == b0 and diag))),
                          reads=[kPb[b], kv], writes=["ps%d" % (4 + b)])

            def finalize_pass(n, G, c):
                bank = 3
                kbank = "ps%d" % bank
                tb = ps[bank][:, :].bitcast(BF16)
                nblk = ncol // 128
                for b in range(4):
                    a = 4 + b
                    ka = "ps%d" % a
                    fin, ofp, obf = fin4[b], ofp4[b], obf4[b]
                    kf, ko, kob = "fin%d" % b, "ofp%d" % b, "obf%d" % b
                    if is_diff:
                        S.add("dve", lambda e, a=a, fin=fin: e.reciprocal(out=fin[:, 0:1], in_=ps[a][:, 256:257]), reads=[ka], writes=[kf])
                        if c == 0:
                            S.add("dve", lambda e, a=a, fin=fin, b=b: e.tensor_scalar(out=o0[:, b, :], in0=ps[a][:, 0:256], scalar1=fin[:, 0:1],
                                                                                       scalar2=None, op0=ALU.mult), reads=[ka, kf], writes=["o0_%d" % b])
                            continue
                        S.add("dve", lambda e, fin=fin: e.tensor_tensor(out=fin[:, 2:3], in0=fin[:, 0:1], in1=nlam, op=ALU.mult),
                              reads=[kf, "nlam"], writes=[kf])
                        S.add("dve", lambda e, a=a, fin=fin, ofp=ofp, b=b: e.scalar_tensor_tensor(
                            out=ofp, in0=ps[a][:, 0:256], scalar=fin[:, 2:3], in1=o0[:, b, :], op0=ALU.mult, op1=ALU.add),
                            reads=[ka, kf, "o0_%d" % b], writes=[ko])
                    else:
                        S.add("dve", lambda e, a=a, fin=fin: e.reciprocal(out=fin[:, 0:1], in_=ps[a][:, 128:129]), reads=[ka], writes=[kf])
                        S.add("dve", lambda e, a=a, fin=fin, obf=obf: e.tensor_scalar(out=obf[:, 0:128], in0=ps[a][:, 0:128], scalar1=fin[:, 0:1],
                                                                                       scalar2=None, op0=ALU.mult), reads=[ka, kf], writes=[kob])
                if is_diff and c == 0:
                    return
                if is_diff:
                    for b in range(4):
                        fin, ofp = fin4[b], ofp4[b]
                        kf, ko = "fin%d" % b, "ofp%d" % b
                        S.add("act", lambda e, fin=fin, ofp=ofp: e.activation(out=junk, in_=ofp, func=AF.Square, accum_out=fin[:, 3:4]),
                              reads=[ko], writes=["junk", kf])
                        S.add("act", lambda e, fin=fin: e.activation(out=fin[:, 4:5], in_=fin[:, 3:4], func=AF.Sqrt, bias=eps_t[:, 1:2], scale=1.0 / 256),
                              reads=[kf, "eps"], writes=[kf])
                    for b in range(4):
                        fin, ofp, obf = fin4[b], ofp4[b], obf4[b]
                        kf, ko, kob = "fin%d" % b, "ofp%d" % b, "obf%d" % b
                        S.add("dve", lambda e, fin=fin: e.reciprocal(out=fin[:, 5:6], in_=fin[:, 4:5]), reads=[kf], writes=[kf])
                        S.add("dve", lambda e, fin=fin, ofp=ofp, obf=obf: e.scalar_tensor_tensor(out=obf, in0=ofp, scalar=fin[:, 5:6], in1=gsub,
                                                                                                  op0=ALU.mult, op1=ALU.mult),
                              reads=[ko, kf, "cst3"], writes=[kob])
                for b in range(4):
                    obf = obf4[b]
                    for blk in range(nblk):
                        sl_ = b * nblk + blk
                        S.add("pe", lambda e, obf=obf, blk=blk, sl_=sl_: e.transpose(tb[:, sl_ * 128:(sl_ + 1) * 128], obf[:, blk * 128:(blk + 1) * 128], ident),
                              reads=["obf%d" % b, "cst"], writes=[kbank])
                for b in range(4):
                    j = 4 * G + b
                    src = tb[:, b * nblk * 128:(b + 1) * nblk * 128].rearrange("p (a q) -> p a q", a=nblk)
                    S.add("dve", lambda e, src=src, j=j: e.tensor_copy(out=oTh[:, 0:nblk, j * 128:(j + 1) * 128], in_=src),
                          reads=[kbank], writes=["oTh"])

            LA, LP = 2, 6
            for n in range(len(items) + LP):
                if n < len(items):
                    emit_st(n, items[n])
                if 0 <= n - LA < len(items):
                    emit_rest(n - LA, items[n - LA])
                if n - LP >= 0:
                    emit_pv(n - LP, items[n - LP])
            for blk in range(ncol // 128):
                S.add("sp", lambda e, blk=blk: e.dma_start(out=oT_s[orow + blk * 128:orow + (blk + 1) * 128, :], in_=oTh[:, blk, :]),
                      reads=["oTh"], writes=[U("scr")], dma=True)

        if upto >= 2:
            horder = []
            for hd in range(4):
                horder += [(True, hd), (False, 2 * hd), (False, 2 * hd + 1)]
            cur = prep_head(*horder[0])
            for hi_, (isd, hh) in enumerate(horder):
                nxt = prep_head(*horder[hi_ + 1]) if hi_ + 1 < len(horder) else None
                attention_head(isd, hh, cur)
                cur = nxt
        ar.release()
        S.fence(dummy)

        ar.mark()
        xs = ar.alloc((KC, 512), F32)
        uT3 = ar.alloc((KC, 512), BF16)
        sq3 = ar.alloc((KC, 512), BF16)
        aT = ar.alloc((FC, 512), BF16)
        oTc = aT[:, 0:KC, :]
        mrg = sq3
        gA = ar.alloc((512,), BF16)
        gF = ar.alloc((512,), BF16)
        m1 = ar.alloc((512,), F32)
        m2 = ar.alloc((512,), F32)
        sgt = ar.alloc((512,), F32)
        NSL = 4
        sl3 = [ar.alloc((KC * 512,), BF16) for _ in range(NSL)]
        s3 = [0]

        def get_slab(w_ap, r0, nk, c0, ncol):
            i = s3[0] % NSL
            s3[0] += 1
            v = sl3[i][:, 0:nk * ncol].rearrange("p (a b) -> p a b", a=nk)
            idx = wreg[(w_ap.tensor.name, r0, nk, c0, ncol)]
            S.add("sp", lambda e, i=i, idx=idx, n_=nk * ncol: e.dma_start(out=sl3[i][:, 0:n_], in_=wsc[idx][:, 0:n_]),
                  writes=["sl3_%d" % i], dma=True)
            return v, "sl3_%d" % i

        oT_v = oT_s.rearrange("(kc p) t -> p kc t", p=128)
        outT_v = outT.rearrange("(kc p) t -> p kc t", p=128)

        def mm_group(b, lhs_fn, rhs_fn, nk, reads):
            for kc in range(nk):
                S.add("pe", lambda e, kc=kc, l_=lhs_fn(kc), r_=rhs_fn(kc): e.matmul(ps[b][:, :], lhsT=l_, rhs=r_, start=(kc == 0), stop=(kc == nk - 1)),
                      reads=reads, writes=["ps%d" % b])

        for ch in range(NOWN_ // 512 if upto >= 3 else 0):
            a0 = ch * 512
            S.add("sp", lambda e, a0=a0: e.dma_start(out=xs, in_=xT_v[:, :, a0:a0 + 512]), writes=["xs"], dma=True)
            S.add("sp", lambda e, a0=a0: e.dma_start(out=oTc, in_=oT_v[:, :, a0:a0 + 512]), writes=["aT"], dma=True)
            rms_stats(xs, 512, sq3, "xs")
            norm_apply(xs, 512, 0, uT3, "xs", "uT3")
            for sgp in range(4):
                wga, kga = get_slab(w_in, 0, KC, 6152 + sgp * 512, 512)
                wa_, kwa = get_slab(w_a, 0, 8, sgp * 512, 512)
                for ct in range(4):
                    b = next_bank()
                    mm_group(b, lambda kc, ct=ct: wga[:, kc, ct * 128:(ct + 1) * 128], lambda kc: uT3[:, kc, :], KC, ["uT3", kga])
                    S.add("act", lambda e, b=b: e.activation(out=gA, in_=ps[b][:, :], func=AF.Sigmoid), reads=["ps%d" % b], writes=["gA"])
                    b2 = next_bank()
                    mm_group(b2, lambda kc, ct=ct: wa_[:, kc, ct * 128:(ct + 1) * 128], lambda kc: oTc[:, kc, :], 8, ["aT", kwa])
                    S.add("dve", lambda e, b2=b2, ct=ct, sgp=sgp: e.tensor_tensor(out=mrg[:, sgp * 4 + ct, :], in0=ps[b2][:, :], in1=gA, op=ALU.mult),
                          reads=["ps%d" % b2, "gA"], writes=["sq"])
                wgf, kgf = get_slab(w_in, 0, KC, 8200 + sgp * 512, 512)
                wf_, kwf = get_slab(w_f, 0, 8, sgp * 512, 512)
                for ct in range(4):
                    b = next_bank()
                    mm_group(b, lambda kc, ct=ct: wgf[:, kc, ct * 128:(ct + 1) * 128], lambda kc: uT3[:, kc, :], KC, ["uT3", kgf])
                    S.add("act", lambda e, b=b: e.activation(out=gF, in_=ps[b][:, :], func=AF.Sigmoid), reads=["ps%d" % b], writes=["gF"])
                    b2 = next_bank()
                    mm_group(b2, lambda kc, ct=ct: wf_[:, kc, ct * 128:(ct + 1) * 128], lambda kc: oTc[:, 8 + kc, :], 8, ["aT", kwf])
                    S.add("dve", lambda e, b2=b2: e.tensor_tensor(out=m2, in0=ps[b2][:, :], in1=gF, op=ALU.mult),
                          reads=["ps%d" % b2, "gF"], writes=["m2"])
                    S.add("dve", lambda e, ct=ct, sgp=sgp: e.tensor_tensor(out=mrg[:, sgp * 4 + ct, :], in0=mrg[:, sgp * 4 + ct, :], in1=m2, op=ALU.add),
                          reads=["m2", "sq"], writes=["sq"])
            for sgp in range(4):
                wo_, kwo = get_slab(w_o, 0, KC, sgp * 512, 512)
                for ct in range(4):
                    b = next_bank()
                    mm_group(b, lambda kc, ct=ct: wo_[:, kc, ct * 128:(ct + 1) * 128], lambda kc: mrg[:, kc, :], KC, ["sq", kwo])
                    S.add("dve", lambda e, b=b, ct=ct, sgp=sgp: e.tensor_tensor(out=xs[:, sgp * 4 + ct, :], in0=xs[:, sgp * 4 + ct, :], in1=ps[b][:, :], op=ALU.add),
                          reads=["ps%d" % b, "xs"], writes=["xs"])
            rms_stats(xs, 512, sq3, "xs")
            norm_apply(xs, 512, 16, uT3, "xs", "uT3")
            for sgp in range(11):
                wg_, kwg = get_slab(w_g, 0, KC, sgp * 512, 512)
                wu_, kwu = get_slab(w_u, 0, KC, sgp * 512, 512)
                for ct in range(4):
                    b = next_bank()
                    mm_group(b, lambda kc, ct=ct: wg_[:, kc, ct * 128:(ct + 1) * 128], lambda kc: uT3[:, kc, :], KC, ["uT3", kwg])
                    S.add("act", lambda e, b=b: e.activation(out=sgt, in_=ps[b][:, :], func=AF.Silu), reads=["ps%d" % b], writes=["sgt"])
                    b2 = next_bank()
                    mm_group(b2, lambda kc, ct=ct: wu_[:, kc, ct * 128:(ct + 1) * 128], lambda kc: uT3[:, kc, :], KC, ["uT3", kwu])
                    S.add("dve", lambda e, b2=b2, ct=ct, sgp=sgp: e.tensor_tensor(out=aT[:, sgp * 4 + ct, :], in0=ps[b2][:, :], in1=sgt, op=ALU.mult),
                          reads=["ps%d" % b2, "sgt"], writes=["aT"])
            for ct in range(16):
                b = next_bank()
                wd_, kwd = get_slab(w_d, 0, FC, ct * 128, 128)
                for kc in range(FC):
                    S.add("pe", lambda e, kc=kc, b=b, wd_=wd_: e.matmul(ps[b][:, :], lhsT=wd_[:, kc, :], rhs=aT[:, kc, :], start=(kc == 0), stop=(kc == FC - 1)),
                          reads=["aT", kwd], writes=["ps%d" % b])
                S.add("dve", lambda e, b=b, ct=ct: e.tensor_tensor(out=xs[:, ct, :], in0=xs[:, ct, :], in1=ps[b][:, :], op=ALU.add),
                      reads=["ps%d" % b, "xs"], writes=["xs"])
            rms_stats(xs, 512, sq3, "xs")
            norm_apply(xs, 512, 32, xs, "xs", "xs")
            od = S.add("sp", lambda e, a0=a0: e.dma_start(out=outT_v[:, :, a0:a0 + 512], in_=xs), reads=["xs"], dma=True)
            out_dmas.append(od)
        ar.release()
        fz = S.fence(dummy)
        S.finish(out_dmas + [fz])
        S.emit(st)
    return nc


_CACHE = {}


def _tables(half, NT=32):
    t = np.zeros((128, T_END), np.float32)
    p = np.arange(128, dtype=np.float64)
    slopes = [2.0 ** (-8.0 * (i + 1) / 4) for i in range(4)]
    for h in range(4):
        s = slopes[h]
        for dl in range(32):
            t[:, T_TABD + h * 96 + dl] = s * (p - 256 * dl)
            if half == 0:
                t[:, T_TABD + h * 96 + 32 + dl] = NEG if dl == 0 else s * (p - 256 * dl + 128)
            else:
                t[:, T_TABD + h * 96 + 32 + dl] = s * (p - 256 * dl - 128)
            t[:, T_TABD + h * 96 + 64 + dl] = s * (p - 16 - 128 * (2 * dl + half))
        k = p[:, None]
        q = p[None, :]
        vis = (k // 64) <= (q // 64)
        t[:, T_DG + h * 128:T_DG + (h + 1) * 128] = np.where(vis, -s * np.abs(q - k) + s * q, NEG)
    k = p[:, None]
    q = p[None, :]
    t[:, T_FM:T_FM + 128] = np.where(k <= q, 0.0, NEG)
    t[:, T_FO] = 0.0 if half == 1 else NEG
    t[:, T_RM] = (p < 16)
    NKT = 2 * NT + 1
    rank = np.zeros(NKT)
    rank[2 * NT] = 0
    for T in range(2 * NT):
        idx = T // 2 + (0 if T % 2 == half else NT)
        rank[idx] = T + 1
    t[0:NKT, T_PREF:T_PREF + NKT] = (rank[:, None] < rank[None, :])
    t[:, T_TRI:T_TRI + 128] = (k <= q)
    t[:, T_SEL:T_SEL + 128] = (k == 64)
    t[:, T_ID:T_ID + 128] = (k == q)
    return t


def kernel(x, meta, g_mix, w_in, lambda_q1, lambda_k1, lambda_q2, lambda_k2, g_subln, b_f,
           w_branch_a, w_branch_f, w_out, g_ffn, w_gate, w_up, w_down, g_final):
    x = np.asarray(x, np.float32)
    f = lambda a: np.ascontiguousarray(np.asarray(a, np.float32))
    if "nc" not in _CACHE:
        _CACHE["nc"] = build_program()
    nc = _CACHE["nc"]
    w_in0, w_a0, w_f0, w_o0 = f(w_in[0]), f(w_branch_a[0]), f(w_branch_f[0]), f(w_out[0])
    w_g0, w_u0, w_d0 = f(w_gate[0]), f(w_up[0]), f(w_down[0])
    meta = f(meta)
    in_maps = []
    for c in range(8):
        b, half = c // 2, c % 2
        xb = x[b].reshape(64, 128, D)
        toks = np.concatenate([xb[half::2].reshape(NOWN, D), xb[1 - half::2].reshape(NOWN, D), meta,
                               np.zeros((112, D), np.float32)], axis=0)
        t = _tables(half)
        gv = np.concatenate([f(g_mix[0]).reshape(16, 128).T, f(g_ffn[0]).reshape(16, 128).T, f(g_final).reshape(16, 128).T], axis=1)
        t[:, T_G:T_G + 48] = gv
        t[:, T_GSUB:T_GSUB + 256] = f(g_subln[0])[None, :]
        t[:, T_BF:T_BF + 8] = f(b_f[0])[None, :]
        t[:, T_LAM:T_LAM + 512] = np.concatenate([f(lambda_q1[0]), f(lambda_k1[0]), f(lambda_q2[0]), f(lambda_k2[0])])[None, :]
        in_maps.append({"xT": np.ascontiguousarray(toks.T), "tabs": t, "w_in": w_in0, "w_a": w_a0, "w_f": w_f0, "w_o": w_o0,
                        "w_g": w_g0, "w_u": w_u0, "w_d": w_d0})
    res = run_bass_kernel_spmd(nc, in_maps, core_ids=list(range(8)))
    out = np.zeros((4, SEQ, D), np.float32)
    for c in range(8):
        b, half = c // 2, c % 2
        o = np.asarray(res.results[c]["outT"], np.float32).T.reshape(32, 128, D)
        out[b].reshape(64, 128, D)[half::2] = o
    return out
```

```python
import numpy as np
from contextlib import ExitStack
import concourse.bass as bass
import concourse.mybir as mybir
from concourse.bass_utils import run_bass_kernel_spmd

F32 = mybir.dt.float32
BF16 = mybir.dt.bfloat16
AF = mybir.ActivationFunctionType
ALU = mybir.AluOpType

N_DMA_SEMS = 48
N_SW_SEMS = 16


class Sched:
    ENGS = ("pe", "act", "dve", "pool", "sp")

    def __init__(self, nc):
        self.nc = nc
        self.ops = []
        self.last_w = {}
        self.readers = {}
        self.dma_uses = [0] * N_DMA_SEMS
        self.dma_rr = 0
        self.dma_rr_sw = 0
        self.final_deps = []

    def add(self, eng, fn, reads=(), writes=(), dma=False):
        oid = len(self.ops)
        reads = list(reads) + ["__all__"]
        deps = set()
        for k in reads:
            w = self.last_w.get(k)
            if w is not None:
                deps.add(w)
        for k in writes:
            w = self.last_w.get(k)
            if w is not None:
                deps.add(w)
            for r in self.readers.get(k, ()):
                deps.add(r)
        deps.discard(oid)
        best = {}
        keep = []
        bestd = {}
        for d in deps:
            od = self.ops[d]
            if od["dma"]:
                s_ = od["dsem"]
                if s_ not in bestd or od["dval"] > self.ops[bestd[s_]]["dval"]:
                    bestd[s_] = d
            else:
                e = od["eng"]
                if e not in best or d > best[e]:
                    best[e] = d
        keep.extend(bestd.values())
        for e, d in best.items():
            if e == "pe" and eng == "pe" and not dma:
                continue
            keep.append(d)
        op = {"eng": eng, "fn": fn, "deps": keep, "dma": dma, "signal": False,
              "seq": None, "dsem": None, "dval": None}
        if dma:
            if eng == "pool":
                s = N_DMA_SEMS - N_SW_SEMS + self.dma_rr_sw
                self.dma_rr_sw = (self.dma_rr_sw + 1) % N_SW_SEMS
            else:
                s = self.dma_rr
                self.dma_rr = (self.dma_rr + 1) % (N_DMA_SEMS - N_SW_SEMS)
            self.dma_uses[s] += 1
            op["dsem"] = s
            op["dval"] = 16 * self.dma_uses[s]
        for d in keep:
            self.ops[d]["signal"] = True
        self.ops.append(op)
        for k in reads:
            self.readers.setdefault(k, []).append(oid)
        for k in writes:
            self.last_w[k] = oid
            self.readers[k] = []
        return oid

    def fence(self, dummy_ap):
        return self.add("dve", lambda e: e.memset(dummy_ap, 0.0), writes=["__all__"])

    def finish(self, oids):
        self.final_deps = list(oids)
        for d in oids:
            self.ops[d]["signal"] = True

    def emit(self, stack):
        nc = self.nc
        esem = {e: stack.enter_context(nc.semaphore("es_" + e)) for e in self.ENGS}
        dsem = [stack.enter_context(nc.semaphore("ds_%d" % i)) for i in range(N_DMA_SEMS)]
        cnt = {e: 0 for e in self.ENGS}
        for op in self.ops:
            if not op["dma"] and op["signal"]:
                cnt[op["eng"]] += 1
                op["seq"] = cnt[op["eng"]]
        block = stack.enter_context(nc.Block())
        ops = self.ops
        final_deps = self.final_deps

        def stream(ename, eng):
            waited = {}

            def wait_for(d):
                od = ops[d]
                if od["dma"]:
                    key = ("d", od["dsem"])
                    val = od["dval"]
                    sem = dsem[od["dsem"]]
                else:
                    key = ("e", od["eng"])
                    val = od["seq"]
                    sem = esem[od["eng"]]
                if waited.get(key, 0) >= val:
                    return
                waited[key] = val
                eng.wait_ge(sem, val)

            for op in ops:
                if op["eng"] != ename:
                    continue
                for d in op["deps"]:
                    wait_for(d)
                if op["dma"]:
                    prev = op["dval"] - 16
                    key = ("d", op["dsem"])
                    if prev > 0 and waited.get(key, 0) < prev:
                        waited[key] = prev
                        eng.wait_ge(dsem[op["dsem"]], prev)
                    ins = op["fn"](eng)
                    ins.then_inc(dsem[op["dsem"]], 16)
                else:
                    ins = op["fn"](eng)
                    if op["signal"]:
                        ins.then_inc(esem[ename], 1)
            if ename == "sp":
                for d in final_deps:
                    wait_for(d)

        @block.sync
        def _(eng):
            stream("sp", eng)

        @block.scalar
        def _(eng):
            stream("act", eng)

        @block.vector
        def _(eng):
            stream("dve", eng)

        @block.gpsimd
        def _(eng):
            stream("pool", eng)

        @block.tensor
        def _(eng):
            stream("pe", eng)


class Arena:
    def __init__(self, ap_f32, nbytes):
        self.base = ap_f32
        self.nbytes = nbytes
        self.off = 0
        self.marks = []

    def alloc(self, shape_free, dtype):
        esz = 4 if dtype == F32 else 2
        n = int(np.prod(shape_free))
        nb = (n * esz + 31) // 32 * 32
        assert self.off + nb <= self.nbytes, ("arena overflow", self.off, nb, self.nbytes)
        a = self.base[:, self.off // 4:(self.off + nb) // 4]
        if dtype != F32:
            a = a.bitcast(dtype)
        a = a[:, 0:n]
        self.off += nb
        if len(shape_free) == 2:
            a = a.rearrange("p (a b) -> p a b", a=shape_free[0])
        elif len(shape_free) == 3:
            a = a.rearrange("p (a b c) -> p a b c", a=shape_free[0], b=shape_free[1])
        return a

    def mark(self):
        self.marks.append(self.off)

    def release(self):
        self.off = self.marks.pop()

D = 2048
KC = 16
SEQ = 8192
NOWN = 4096
NTOK = 8320
DFF = 5632
FC = 44
NEG = -30000.0
SCALE = 128 ** -0.5
LAMBDA_INIT = 0.2
T_TABD = 0
T_DG = T_TABD + 4 * 96
T_FM = T_DG + 4 * 128
T_FO = T_FM + 128
T_RM = T_FO + 1
T_PREF = T_RM + 1
T_TRI = T_PREF + 65
T_SEL = T_TRI + 128
T_ID = T_SEL + 128
T_G = T_ID + 128
T_GSUB = T_G + 48
T_BF = T_GSUB + 256
T_LAM = T_BF + 8
T_END = T_LAM + 512


def build_program(NT=32, dbg=False, upto=3):
    nc = bass.Bass("TRN2", target_bir_lowering=False)
    NOWN_ = NT * 128
    NKT = 2 * NT + 1
    META = 2 * NT
    NTOK_ = NKT * 128
    SK = "ExternalOutput" if dbg else "Internal"
    xT = nc.dram_tensor("xT", [D, NTOK_], F32, kind="ExternalInput").ap()
    tabs_d = nc.dram_tensor("tabs", [128, T_END], F32, kind="ExternalInput").ap()
    w_in = nc.dram_tensor("w_in", [D, 10248], F32, kind="ExternalInput").ap()
    w_a = nc.dram_tensor("w_a", [1024, D], F32, kind="ExternalInput").ap()
    w_f = nc.dram_tensor("w_f", [1024, D], F32, kind="ExternalInput").ap()
    w_o = nc.dram_tensor("w_o", [D, D], F32, kind="ExternalInput").ap()
    w_g = nc.dram_tensor("w_g", [D, DFF], F32, kind="ExternalInput").ap()
    w_u = nc.dram_tensor("w_u", [D, DFF], F32, kind="ExternalInput").ap()
    w_d = nc.dram_tensor("w_d", [DFF, D], F32, kind="ExternalInput").ap()
    outT = nc.dram_tensor("outT", [D, NOWN_], F32, kind="ExternalOutput").ap()
    kT_s = nc.dram_tensor("kT_s", [16, 128, NTOK_], BF16, kind=SK).ap()
    qT_s = nc.dram_tensor("qT_s", [16, 128, NOWN_], BF16, kind=SK).ap()
    v_s = nc.dram_tensor("v_s", [NTOK_, D], BF16, kind=SK).ap()
    oT_s = nc.dram_tensor("oT_s", [D, NOWN_], BF16, kind=SK).ap()
    wsc = nc.dram_tensor("wsc", [58, 128, 8192], BF16, kind="Internal").ap()

    with ExitStack() as st:
        ARENA_W = 53000
        arena_t = st.enter_context(nc.sbuf_tensor("arena", [128, ARENA_W], F32))
        ar = Arena(arena_t[:, :], ARENA_W * 4)
        ps = [st.enter_context(nc.psum_tensor("ps%d" % i, [128, 512], F32)) for i in range(8)]
        S = Sched(nc)
        uid = [0]

        def U(p):
            uid[0] += 1
            return "%s#%d" % (p, uid[0])

        tabs = ar.alloc((T_END,), F32)
        S.add("sp", lambda e: e.dma_start(out=tabs, in_=tabs_d), writes=["tabs"], dma=True)
        ident = ar.alloc((128,), BF16)
        S.add("dve", lambda e: e.tensor_copy(out=ident, in_=tabs[:, T_ID:T_ID + 128]), reads=["tabs"], writes=["cst"])
        ones_b = ar.alloc((128,), BF16)
        S.add("pool", lambda e: e.memset(ones_b, 1.0), writes=["cst1"])
        ones_f = ar.alloc((128,), F32)
        S.add("pool", lambda e: e.memset(ones_f, 1.0), writes=["cst2"])
        gsub = ar.alloc((256,), F32)
        S.add("dve", lambda e: e.tensor_scalar(out=gsub, in0=tabs[:, T_GSUB:T_GSUB + 256], scalar1=1.0 - LAMBDA_INIT,
                                              scalar2=None, op0=ALU.mult), reads=["tabs"], writes=["cst3"])
        lt = ar.alloc((128,), F32)
        lsum = ar.alloc((2,), F32)
        nlam = ar.alloc((1,), F32)
        for i in range(2):
            a0 = T_LAM + i * 256
            S.add("dve", lambda e, a0=a0: e.tensor_tensor(out=lt, in0=tabs[:, a0:a0 + 128], in1=tabs[:, a0 + 128:a0 + 256],
                                                          op=ALU.mult), reads=["tabs"], writes=["lt"])
            S.add("dve", lambda e, i=i: e.tensor_reduce(out=lsum[:, i:i + 1], in_=lt, axis=mybir.AxisListType.X, op=ALU.add),
                  reads=["lt"], writes=["lsum"])
        S.add("act", lambda e: e.activation(out=lsum, in_=lsum, func=AF.Exp), reads=["lsum"], writes=["lsum"])
        S.add("dve", lambda e: e.tensor_tensor(out=nlam, in0=lsum[:, 1:2], in1=lsum[:, 0:1], op=ALU.subtract),
              reads=["lsum"], writes=["nlam"])
        S.add("dve", lambda e: e.tensor_scalar(out=nlam, in0=nlam, scalar1=-LAMBDA_INIT, scalar2=None, op0=ALU.add),
              reads=["nlam"], writes=["nlam"])
        SPL = ar.alloc((NKT, 8), F32)
        dummy = ar.alloc((8,), F32)
        ev = [0]

        def evac(out, in_, reads, writes):
            ev[0] += 1
            if ev[0] % 2:
                S.add("dve", lambda e: e.tensor_copy(out=out, in_=in_), reads=reads, writes=writes)
            else:
                S.add("act", lambda e: e.copy(out=out, in_=in_), reads=reads, writes=writes)

        bank_rr = [0]

        def next_bank():
            b = 2 + bank_rr[0] % 6
            bank_rr[0] += 1
            return b

        xT_v = xT.rearrange("(kc p) t -> p kc t", p=128)

        def rms_stats(src, n, sq, ksrc, eps=1e-6):
            S.add("act", lambda e: e.activation(out=sq[:, :, 0:n], in_=src, func=AF.Square), reads=[ksrc], writes=["sq"])
            for kc in range(KC):
                S.add("pe", lambda e, kc=kc: e.matmul(ps[0][0:1, 0:n], lhsT=ones_b[:, 0:1], rhs=sq[:, kc, 0:n],
                                                      start=(kc == 0), stop=(kc == KC - 1)),
                      reads=["sq", "cst1"], writes=["ps0"])
            S.add("act", lambda e: e.activation(out=rs_row[0:1, 0:n], in_=ps[0][0:1, 0:n], func=AF.Sqrt, bias=eps_t[0:1, 0:1],
                                                scale=1.0 / D), reads=["ps0", "eps"], writes=["rsrow"])
            S.add("dve", lambda e: e.reciprocal(out=rs_row[0:1, 0:n], in_=rs_row[0:1, 0:n]), reads=["rsrow"], writes=["rsrow"])
            S.add("pe", lambda e: e.matmul(ps[1][:, 0:n], lhsT=ones_f[0:1, :], rhs=rs_row[0:1, 0:n], start=True, stop=True),
                  reads=["rsrow", "cst2"], writes=["ps1"])

        rs_row = ar.alloc((512,), F32)
        eps_t = ar.alloc((2,), F32)
        S.add("pool", lambda e: e.memset(eps_t[:, 0:1], 1e-6), writes=["eps"])
        S.add("pool", lambda e: e.memset(eps_t[:, 1:2], 1e-5), writes=["eps"])

        def norm_apply(src, n, gofs, dst, ksrc, kdst):
            for kc in range(KC):
                S.add("dve", lambda e, kc=kc: e.scalar_tensor_tensor(
                    out=dst[:, kc, 0:n], in0=src[:, kc, 0:n], scalar=tabs[:, T_G + gofs + kc:T_G + gofs + kc + 1],
                    in1=ps[1][:, 0:n], op0=ALU.mult, op1=ALU.mult), reads=[ksrc, "ps1", "tabs"], writes=[kdst])

        def load_slab(dst, w_ap, r0, nk, c0, ncol, key):
            src = w_ap[r0:r0 + nk * 128, c0:c0 + ncol].rearrange("(kc p) c -> p kc c", p=128)
            S.add("pool", lambda e: e.dma_start(out=dst[:, 0:nk, 0:ncol], in_=src), writes=[key], dma=True)

        ar.mark()
        SBT = min(1024, NOWN_)
        uTs = [ar.alloc((KC, SBT), BF16) for _ in range(2)]
        xs2 = [ar.alloc((KC, 512), F32) for _ in range(2)]
        sq = ar.alloc((KC, 512), BF16)
        slabs = [ar.alloc((KC, 512), BF16) for _ in range(2)]
        wz = ar.alloc((KC, 8), BF16)
        stg = [ar.alloc((512,), BF16) for _ in range(6)]
        ztmp = ar.alloc((8,), F32)
        load_slab(wz, w_in, 0, KC, 6144, 8, "wz")
        sbs = [(i * SBT, SBT, True) for i in range(NOWN_ // SBT)] + [(NOWN_ + i * SBT, SBT, False) for i in range(NOWN_ // SBT)] + [(2 * NOWN_, 128, False)]
        xi = [0]
        si = [0]
        gi = [0]
        out_dmas = []
        slab_list = []
        for s in range(2):
            slab_list.append((1024 + s * 512, "KT", s * 4))
        for s in range(2):
            slab_list.append((4096 + s * 512, "KT", 8 + s * 4))
        for s in range(2):
            slab_list.append((2048 + s * 512, "V", s * 512))
        for s in range(2):
            slab_list.append((5120 + s * 512, "V", 1024 + s * 512))
        for s in range(2):
            slab_list.append((0 + s * 512, "QT", s * 4))
        for s in range(2):
            slab_list.append((3072 + s * 512, "QT", 8 + s * 4))
        def emit_norm(sbi):
            t0, nt, own = sbs[sbi]
            uT = uTs[sbi % 2]
            ku = "uT%d" % (sbi % 2)
            nch = (nt + 511) // 512
            for c in range(nch):
                n = min(512, nt - c * 512)
                xs = xs2[xi[0] % 2]
                kx = "xs%d" % (xi[0] % 2)
                xi[0] += 1
                S.add("sp", lambda e, xs=xs, n=n, a=t0 + c * 512: e.dma_start(out=xs[:, :, 0:n], in_=xT_v[:, :, a:a + n]),
                      writes=[kx], dma=True)
                rms_stats(xs[:, :, 0:n], n, sq, kx)
                norm_apply(xs, n, 0, uT[:, :, c * 512:c * 512 + n], kx, ku)

        emit_norm(0)
        for sbi, (t0, nt, own) in enumerate(sbs):
            nch = (nt + 511) // 512
            uT = uTs[sbi % 2]
            ku = "uT%d" % (sbi % 2)
            slab_n = [0]
            for tt in range(nt // 128):
                b = next_bank()
                for kc in range(KC):
                    S.add("pe", lambda e, uT=uT, kc=kc, tt=tt, b=b: e.matmul(ps[b][:, 0:8], lhsT=uT[:, kc, tt * 128:(tt + 1) * 128],
                                                                      rhs=wz[:, kc, :], start=(kc == 0), stop=(kc == KC - 1)),
                          reads=[ku, "wz"], writes=["ps%d" % b])
                tile_idx = (t0 // 128 + tt) if t0 < 2 * NOWN_ else META
                S.add("dve", lambda e, b=b: e.tensor_tensor(out=ztmp, in0=ps[b][:, 0:8], in1=tabs[:, T_BF:T_BF + 8], op=ALU.add),
                      reads=["ps%d" % b, "tabs"], writes=["ztmp"])
                S.add("act", lambda e: e.activation(out=ztmp, in_=ztmp, func=AF.Exp, scale=-1.0), reads=["ztmp"], writes=["ztmp"])
                S.add("act", lambda e, ti=tile_idx: e.activation(out=SPL[:, ti, :], in_=ztmp, func=AF.Ln, bias=ones_f[:, 0:1], scale=1.0),
                      reads=["ztmp", "cst2"], writes=["SPL"])
            for (c0, kind, idx) in slab_list:
                if kind == "QT" and not own:
                    continue
                if slab_n[0] == 1 and sbi + 1 < len(sbs):
                    emit_norm(sbi + 1)
                slab_n[0] += 1
                sl = slabs[si[0] % 2]
                ks = "slab%d" % (si[0] % 2)
                si[0] += 1
                load_slab(sl, w_in, 0, KC, c0, 512, ks)
                if kind in ("KT", "QT"):
                    dst_s = kT_s if kind == "KT" else qT_s
                    for ct in range(4):
                        for c in range(nch):
                            n = min(512, nt - c * 512)
                            b = next_bank()
                            for kc in range(KC):
                                S.add("pe", lambda e, uT=uT, kc=kc, ct=ct, c=c, n=n, b=b, sl=sl: e.matmul(
                                    ps[b][:, 0:n], lhsT=sl[:, kc, ct * 128:(ct + 1) * 128], rhs=uT[:, kc, c * 512:c * 512 + n],
                                    start=(kc == 0), stop=(kc == KC - 1)), reads=[ku, ks], writes=["ps%d" % b])
                            sg = stg[gi[0] % 6]
                            kg = "stg%d" % (gi[0] % 6)
                            gi[0] += 1
                            evac(sg[:, 0:n], ps[b][:, 0:n], ["ps%d" % b], [kg])
                            S.add("sp", lambda e, sg=sg, n=n, m=idx + ct, a=t0 + c * 512, dst_s=dst_s: e.dma_start(
                                out=dst_s[m, :, a:a + n], in_=sg[:, 0:n]), reads=[kg], writes=[U("scr")], dma=True)
                else:
                    for tt in range(nt // 128):
                        b = next_bank()
                        for kc in range(KC):
                            S.add("pe", lambda e, uT=uT, kc=kc, tt=tt, b=b, sl=sl: e.matmul(
                                ps[b][:, :], lhsT=uT[:, kc, tt * 128:(tt + 1) * 128], rhs=sl[:, kc, :],
                                start=(kc == 0), stop=(kc == KC - 1)), reads=[ku, ks], writes=["ps%d" % b])
                        sg = stg[gi[0] % 6]
                        kg = "stg%d" % (gi[0] % 6)
                        gi[0] += 1
                        evac(sg, ps[b][:, :], ["ps%d" % b], [kg])
                        S.add("sp", lambda e, sg=sg, a=t0 + tt * 128, idx=idx: e.dma_start(
                            out=v_s[a:a + 128, idx:idx + 512], in_=sg), reads=[kg], writes=[U("scr")], dma=True)
        ar.release()
        S.fence(dummy)

        ar.mark()
        Wc = ar.alloc((NKT, 8), F32)
        A = ar.alloc((NKT, 8), F32)
        tot = ar.alloc((8,), F32)
        totb = ar.alloc((128,), F32)
        CB = ar.alloc((NT, 8), F32)
        trif = tabs[:, T_TRI:T_TRI + 128]
        self64 = tabs[:, T_SEL:T_SEL + 128]
        pref = tabs[0:NKT, T_PREF:T_PREF + NKT]
        S.add("dve", lambda e: e.tensor_scalar(out=SPL[:, META, :], in0=SPL[:, META, :], scalar1=tabs[:, T_RM:T_RM + 1], scalar2=None,
                                              op0=ALU.mult), reads=["SPL", "tabs"], writes=["SPL"])
        SPLf = SPL.rearrange("p a b -> p (a b)")
        Wcf = Wc.rearrange("p a b -> p (a b)")
        for c0_ in range(0, NKT * 8, 512):
            n_ = min(512, NKT * 8 - c0_)
            S.add("pe", lambda e, c0_=c0_, n_=n_: e.matmul(ps[2][:, 0:n_], lhsT=trif, rhs=SPLf[:, c0_:c0_ + n_], start=True, stop=True),
                  reads=["SPL", "tabs"], writes=["ps2"])
            S.add("dve", lambda e, c0_=c0_, n_=n_: e.tensor_copy(out=Wcf[:, c0_:c0_ + n_], in_=ps[2][:, 0:n_]), reads=["ps2"], writes=["Wc"])
        for h in range(8):
            S.add("pe", lambda e, h=h: e.matmul(ps[4][0:NKT, h:h + 1], lhsT=SPL[:, :, h], rhs=ones_f[:, 0:1], start=True, stop=True),
                  reads=["SPL", "cst2"], writes=["ps4"])
        S.add("dve", lambda e: e.tensor_copy(out=tot[0:NKT, :], in_=ps[4][0:NKT, 0:8]), reads=["ps4"], writes=["tot"])
        for h in range(8):
            S.add("dve", lambda e, h=h: e.tensor_scalar(out=totb[0:NKT, :], in0=ones_f[0:NKT, :], scalar1=tot[0:NKT, h:h + 1], scalar2=None,
                                                         op0=ALU.mult), reads=["tot", "cst2"], writes=["totb"])
            S.add("pe", lambda e: e.matmul(ps[5][:, 0:NKT], lhsT=totb[0:NKT, :], rhs=pref, start=True, stop=True),
                  reads=["totb", "tabs"], writes=["ps5"])
            S.add("dve", lambda e, h=h: e.tensor_tensor(out=A[:, :, h], in0=Wc[:, :, h], in1=ps[5][:, 0:NKT], op=ALU.add),
                  reads=["Wc", "ps5"], writes=["A"])
        Af = A.rearrange("p a b -> p (a b)")
        S.add("pe", lambda e: e.matmul(ps[6][:, 0:NT * 8], lhsT=self64, rhs=Af[:, 0:NT * 8], start=True, stop=True), reads=["A", "tabs"], writes=["ps6"])
        S.add("dve", lambda e: e.tensor_copy(out=CB.rearrange("p a b -> p (a b)"), in_=ps[6][:, 0:NT * 8]), reads=["ps6"], writes=["CB"])

        wreg = {}
        wlist = []
        for sgp in range(4):
            wlist += [(w_in, 0, KC, 6152 + sgp * 512, 512), (w_a, 0, 8, sgp * 512, 512),
                      (w_in, 0, KC, 8200 + sgp * 512, 512), (w_f, 0, 8, sgp * 512, 512)]
        for sgp in range(4):
            wlist.append((w_o, 0, KC, sgp * 512, 512))
        for sgp in range(11):
            wlist += [(w_g, 0, KC, sgp * 512, 512), (w_u, 0, KC, sgp * 512, 512)]
        for ct in range(16):
            wlist.append((w_d, 0, FC, ct * 128, 128))
        for idx, (w_ap, r0, nk, c0, ncol) in enumerate(wlist):
            wreg[(w_ap.tensor.name, r0, nk, c0, ncol)] = idx
            src = w_ap[r0:r0 + nk * 128, c0:c0 + ncol].rearrange("(kc p) c -> p kc c", p=128)
            dst = wsc[idx][:, 0:nk * ncol].rearrange("p (a b) -> p a b", a=nk)
            S.add("pool", lambda e, src=src, dst=dst: e.dma_start(out=dst, in_=src), writes=[U("wsc")], dma=True)
        KTb = [ar.alloc((NTOK_,), BF16) for _ in range(3)]
        QTb = [ar.alloc((NOWN_,), BF16) for _ in range(3)]
        q_i = [0]
        head_no = [0]
        Vb = [ar.alloc((NKT, 257), BF16) for _ in range(2)]
        oTh = ar.alloc((2, NOWN_), BF16)
        dtmp = [ar.alloc((128,), F32) for _ in range(2)]
        fin4 = [ar.alloc((8,), F32) for _ in range(4)]
        ofp4 = [ar.alloc((256,), F32) for _ in range(4)]
        junk = ar.alloc((256,), F32)
        obf4 = [ar.alloc((256,), BF16) for _ in range(4)]
        kq_i = [0]
        v_i = [0]
        pt_i = [0]
        dt_i = [0]
        ss_i = [0]
        acc_i = [0]
        bf_i = [0]
        v_sv = v_s.rearrange("(kt p) c -> p kt c", p=128)

        NPB = 8
        KEEP_WARM = True
        WARM_N = 512
        PB = [ar.alloc((512,), BF16) for _ in range(NPB)]
        ewD = ar.alloc((96,), F32)
        bFa = ar.alloc((NT, NKT), F32)
        o0 = ar.alloc((4, 256), F32)
        NG = NT // 4

        def attention_head(is_diff, h):
            nmap = 2 if is_diff else 1
            ncol = 256 if is_diff else 128
            maps = [2 * h, 2 * h + 1] if is_diff else [8 + h]
            vcol = h * 256 if is_diff else 1024 + h * 128
            orow = h * 256 if is_diff else 1024 + h * 128
            KT, QT, kk, kqk = [], [], [], []
            for m in maps:
                i = kq_i[0] % 3
                kq_i[0] += 1
                S.add("sp", lambda e, i=i, m=m: e.dma_start(out=KTb[i], in_=kT_s[m]), writes=["KT%d" % i], dma=True)
                iq = q_i[0] % 3
                q_i[0] += 1
                S.add("sp", lambda e, iq=iq, m=m: e.dma_start(out=QTb[iq], in_=qT_s[m]), writes=["QT%d" % iq], dma=True)
                KT.append(KTb[i]); QT.append(QTb[iq]); kk.append("KT%d" % i); kqk.append("QT%d" % iq)
            vi = v_i[0] % 2
            v_i[0] += 1
            V = Vb[vi]
            kv = "V%d" % vi
            for g0 in range(0, NKT, 13):
                g1 = min(NKT, g0 + 13)
                S.add("sp", lambda e, g0=g0, g1=g1: e.dma_start(out=V[:, g0:g1, 0:ncol], in_=v_sv[:, g0:g1, vcol:vcol + ncol]),
                      writes=[kv], dma=True)
            S.add("dve", lambda e: e.memset(V[:, :, ncol:ncol + 1], 1.0), writes=[kv])
            pool_ok = head_no[0] >= 2
            head_no[0] += 1
            if is_diff:
                S.add("act", lambda e: e.activation(out=ewD, in_=tabs[:, T_TABD + h * 96:T_TABD + (h + 1) * 96], func=AF.Exp),
                      reads=["tabs"], writes=["ew"])
            else:
                for j in range(NT):
                    S.add("dve", lambda e, j=j: e.tensor_scalar(out=bFa[:, j, :], in0=A[:, :, h], scalar1=CB[:, j, h:h + 1], scalar2=60.0,
                                                                 op0=ALU.subtract, op1=ALU.min), reads=["A", "CB"], writes=["ew"])
                    S.add("dve", lambda e, j=j: e.tensor_tensor(out=bFa[:, j, NT + j:NT + j + 1], in0=bFa[:, j, NT + j:NT + j + 1],
                                                                 in1=tabs[:, T_FO:T_FO + 1], op=ALU.add), reads=["ew", "tabs"], writes=["ew"])
                S.add("act", lambda e: e.activation(out=bFa, in_=bFa, func=AF.Exp), reads=["ew"], writes=["ew"])

            def ew_ap(G, kind, i, b0, nb):
                j0 = 4 * G + b0
                if is_diff:
                    if kind == "own":
                        c0 = j0 - i
                    elif kind == "oth":
                        c0 = 32 + j0 - i
                    else:
                        c0 = 64 + j0
                    return ewD[:, c0:c0 + nb]
                kt = i if kind == "own" else (NT + i if kind == "oth" else META)
                return bFa[:, j0:j0 + nb, kt]

            items = []
            for G in range(NG):
                for c in range(nmap):
                    lst = [dict(kt=META, kp=16, kind="meta", i=0, b0=0)]
                    for i in range(4 * G):
                        lst.append(dict(kt=i, kp=128, kind="own", i=i, b0=0))
                        lst.append(dict(kt=NT + i, kp=128, kind="oth", i=i, b0=0))
                    for a in range(4):
                        lst.append(dict(kt=NT + 4 * G + a, kp=128, kind="oth", i=4 * G + a, b0=a))
                        lst.append(dict(kt=4 * G + a, kp=128, kind="own", i=4 * G + a, b0=a, diag=True))
                    if is_diff:
                        slope = 2.0 ** (-8.0 * (h + 1) / 4)
                        keep = []
                        for it in lst:
                            if it["kind"] == "meta" or it.get("diag"):
                                keep.append(it)
                                continue
                            dl = 4 * G + it["b0"] - it["i"]
                            mx = slope * (127 - 256 * dl + (128 if it["kind"] == "oth" else 0))
                            if mx > -120.0:
                                keep.append(it)
                        lst = keep
                    for n_, it in enumerate(lst):
                        it.update(G=G, c=c, first=(n_ == 0), last=(n_ == len(lst) - 1), ip=n_)
                        items.append(it)
                    items.append(dict(T=True, G=G, c=c))
            par = [0]

            def emit_st(n, it):
                if it.get("T"):
                    return
                bank = n % 3
                G, c, kt, kp, b0 = it["G"], it["c"], it["kt"], it["kp"], it["b0"]
                q0 = (4 * G + b0) * 128
                q1 = (4 * G + 4) * 128
                S.add("pe", lambda e: e.matmul(ps[bank][0:kp, b0 * 128:512], lhsT=KT[c][:, kt * 128:kt * 128 + kp], rhs=QT[c][:, q0:q1],
                                               start=True, stop=True), reads=[kk[c], kqk[c]], writes=["ps%d" % bank])

            def emit_rest(n, it):
                if it.get("T"):
                    return
                bank = n % 3
                kbank = "ps%d" % bank
                P = PB[n % NPB]
                kPb = ["PB%d_%d" % (n % NPB, b_) for b_ in range(4)]
                kP = kPb[0]
                G, c, kt, kp, b0, kind = it["G"], it["c"], it["kt"], it["kp"], it["b0"], it["kind"]
                c0 = b0 * 128
                diag = it.get("diag", False)
                if diag:
                    di = dt_i[0] % 2
                    dt_i[0] += 1
                    dtm = dtmp[di]
                    kd = "dt%d" % di
                    mk = tabs[:, T_DG + h * 128:T_DG + (h + 1) * 128] if is_diff else tabs[:, T_FM:T_FM + 128]
                    S.add("dve", lambda e: e.scalar_tensor_tensor(out=dtm, in0=ps[bank][:, c0:c0 + 128], scalar=SCALE, in1=mk,
                                                                  op0=ALU.mult, op1=ALU.add), reads=["tabs"], writes=[kd, kbank])
                    S.add("act", lambda e: e.activation(out=P[:, c0:c0 + 128], in_=dtm, func=AF.Exp), reads=[kd], writes=[kPb[b0]])
                    if c0 + 128 < 512:
                        S.add("act", lambda e: e.activation(out=P[:, c0 + 128:512], in_=ps[bank][:, c0 + 128:512], func=AF.Exp, scale=SCALE),
                              reads=[kbank], writes=kPb[b0 + 1:4])
                else:
                    S.add("act", lambda e: e.activation(out=P[0:kp, c0:512], in_=ps[bank][0:kp, c0:512], func=AF.Exp, scale=SCALE),
                          reads=[kbank], writes=kPb[b0:4])
                sb0 = b0 + 1 if (diag and is_diff) else b0
                nb = 4 - sb0
                if nb > 0:
                    ew = ew_ap(G, kind, it["i"], sb0, nb)[0:kp].unsqueeze(2).broadcast_to([kp, nb, 128])
                    Pv = P[0:kp, sb0 * 128:512].rearrange("p (a b) -> p a b", a=nb)
                    par[0] += 1
                    eng_ = "pool" if (par[0] % 2 == 0 and pool_ok) else "dve"
                    if eng_ == "pool":
                        S.add(eng_, lambda e: e.tensor_tensor(out=Pv, in0=Pv, in1=ew, op=ALU.mult), reads=["ew"], writes=kPb[sb0:4])
                    else:
                        ew2 = ew_ap(G, kind, it["i"], sb0, nb)
                        for bb in range(nb):
                            S.add("dve", lambda e, bb=bb: e.tensor_scalar(out=P[0:kp, (sb0 + bb) * 128:(sb0 + bb + 1) * 128],
                                                                         in0=P[0:kp, (sb0 + bb) * 128:(sb0 + bb + 1) * 128],
                                                                         scalar1=ew2[0:kp, bb:bb + 1], scalar2=None, op0=ALU.mult),
                                  reads=["ew"], writes=[kPb[sb0 + bb]])

            def emit_pv(n, it):
                if it.get("T"):
                    finalize_pass(n, it["G"], it["c"])
                    return
                P = PB[n % NPB]
                kPb = ["PB%d_%d" % (n % NPB, b_) for b_ in range(4)]
                G, c, kt, kp, b0 = it["G"], it["c"], it["kt"], it["kp"], it["b0"]
                diag = it.get("diag", False)
                if KEEP_WARM and (not is_diff) and it["ip"] >= 8:
                    S.add("pe", lambda e: e.matmul(ps[3][:, 0:WARM_N], lhsT=ident, rhs=QT[0][:, 0:WARM_N], start=True, stop=True),
                          reads=[kqk[0], "cst"], writes=["ps3"])
                for b in range(b0, 4):
                    S.add("pe", lambda e, b=b: e.matmul(ps[4 + b][:, 0:ncol + 1], lhsT=P[0:kp, b * 128:(b + 1) * 128], rhs=V[0:kp, kt, 0:ncol + 1],
                                                        start=it["first"], stop=(it["last"] or (b == b0 and diag))),
                          reads=[kPb[b], kv], writes=["ps%d" % (4 + b)])

            def finalize_pass(n, G, c):
                bank = 3
                kbank = "ps%d" % bank
                tb = ps[bank][:, :].bitcast(BF16)
                nblk = ncol // 128
                for b in range(4):
                    a = 4 + b
                    ka = "ps%d" % a
                    fin, ofp, obf = fin4[b], ofp4[b], obf4[b]
                    kf, ko, kob = "fin%d" % b, "ofp%d" % b, "obf%d" % b
                    if is_diff:
                        S.add("dve", lambda e, a=a, fin=fin: e.reciprocal(out=fin[:, 0:1], in_=ps[a][:, 256:257]), reads=[ka], writes=[kf])
                        if c == 0:
                            S.add("dve", lambda e, a=a, fin=fin, b=b: e.tensor_scalar(out=o0[:, b, :], in0=ps[a][:, 0:256], scalar1=fin[:, 0:1],
                                                                                       scalar2=None, op0=ALU.mult), reads=[ka, kf], writes=["o0_%d" % b])
                            continue
                        S.add("dve", lambda e, fin=fin: e.tensor_tensor(out=fin[:, 2:3], in0=fin[:, 0:1], in1=nlam, op=ALU.mult),
                              reads=[kf, "nlam"], writes=[kf])
                        S.add("dve", lambda e, a=a, fin=fin, ofp=ofp, b=b: e.scalar_tensor_tensor(
                            out=ofp, in0=ps[a][:, 0:256], scalar=fin[:, 2:3], in1=o0[:, b, :], op0=ALU.mult, op1=ALU.add),
                            reads=[ka, kf, "o0_%d" % b], writes=[ko])
                    else:
                        S.add("dve", lambda e, a=a, fin=fin: e.reciprocal(out=fin[:, 0:1], in_=ps[a][:, 128:129]), reads=[ka], writes=[kf])
                        S.add("dve", lambda e, a=a, fin=fin, obf=obf: e.tensor_scalar(out=obf[:, 0:128], in0=ps[a][:, 0:128], scalar1=fin[:, 0:1],
                                                                                       scalar2=None, op0=ALU.mult), reads=[ka, kf], writes=[kob])
                if is_diff and c == 0:
                    return
                if is_diff:
                    for b in range(4):
                        fin, ofp = fin4[b], ofp4[b]
                        kf, ko = "fin%d" % b, "ofp%d" % b
                        S.add("act", lambda e, fin=fin, ofp=ofp: e.activation(out=junk, in_=ofp, func=AF.Square, accum_out=fin[:, 3:4]),
                              reads=[ko], writes=["junk", kf])
                        S.add("act", lambda e, fin=fin: e.activation(out=fin[:, 4:5], in_=fin[:, 3:4], func=AF.Sqrt, bias=eps_t[:, 1:2], scale=1.0 / 256),
                              reads=[kf, "eps"], writes=[kf])
                    for b in range(4):
                        fin, ofp, obf = fin4[b], ofp4[b], obf4[b]
                        kf, ko, kob = "fin%d" % b, "ofp%d" % b, "obf%d" % b
                        S.add("dve", lambda e, fin=fin: e.reciprocal(out=fin[:, 5:6], in_=fin[:, 4:5]), reads=[kf], writes=[kf])
                        S.add("dve", lambda e, fin=fin, ofp=ofp, obf=obf: e.scalar_tensor_tensor(out=obf, in0=ofp, scalar=fin[:, 5:6], in1=gsub,
                                                                                                  op0=ALU.mult, op1=ALU.mult),
                              reads=[ko, kf, "cst3"], writes=[kob])
                for b in range(4):
                    obf = obf4[b]
                    for blk in range(nblk):
                        sl_ = b * nblk + blk
                        S.add("pe", lambda e, obf=obf, blk=blk, sl_=sl_: e.transpose(tb[:, sl_ * 128:(sl_ + 1) * 128], obf[:, blk * 128:(blk + 1) * 128], ident),
                              reads=["obf%d" % b, "cst"], writes=[kbank])
                for b in range(4):
                    j = 4 * G + b
                    src = tb[:, b * nblk * 128:(b + 1) * nblk * 128].rearrange("p (a q) -> p a q", a=nblk)
                    S.add("dve", lambda e, src=src, j=j: e.tensor_copy(out=oTh[:, 0:nblk, j * 128:(j + 1) * 128], in_=src),
                          reads=[kbank], writes=["oTh"])

            LA, LP = 2, 6
            for n in range(len(items) + LP):
                if n < len(items):
                    emit_st(n, items[n])
                if 0 <= n - LA < len(items):
                    emit_rest(n - LA, items[n - LA])
                if n - LP >= 0:
                    emit_pv(n - LP, items[n - LP])
            for blk in range(ncol // 128):
                S.add("sp", lambda e, blk=blk: e.dma_start(out=oT_s[orow + blk * 128:orow + (blk + 1) * 128, :], in_=oTh[:, blk, :]),
                      reads=["oTh"], writes=[U("scr")], dma=True)

        if upto >= 2:
            for hd in range(4):
                attention_head(True, hd)
                attention_head(False, 2 * hd)
                attention_head(False, 2 * hd + 1)
        ar.release()
        S.fence(dummy)

        ar.mark()
        xs = ar.alloc((KC, 512), F32)
        uT3 = ar.alloc((KC, 512), BF16)
        sq3 = ar.alloc((KC, 512), BF16)
        aT = ar.alloc((FC, 512), BF16)
        oTc = aT[:, 0:KC, :]
        mrg = sq3
        gA = ar.alloc((512,), BF16)
        gF = ar.alloc((512,), BF16)
        m1 = ar.alloc((512,), F32)
        m2 = ar.alloc((512,), F32)
        sgt = ar.alloc((512,), F32)
        NSL = 4
        sl3 = [ar.alloc((KC * 512,), BF16) for _ in range(NSL)]
        s3 = [0]

        def get_slab(w_ap, r0, nk, c0, ncol):
            i = s3[0] % NSL
            s3[0] += 1
            v = sl3[i][:, 0:nk * ncol].rearrange("p (a b) -> p a b", a=nk)
            idx = wreg[(w_ap.tensor.name, r0, nk, c0, ncol)]
            S.add("sp", lambda e, i=i, idx=idx, n_=nk * ncol: e.dma_start(out=sl3[i][:, 0:n_], in_=wsc[idx][:, 0:n_]),
                  writes=["sl3_%d" % i], dma=True)
            return v, "sl3_%d" % i

        oT_v = oT_s.rearrange("(kc p) t -> p kc t", p=128)
        outT_v = outT.rearrange("(kc p) t -> p kc t", p=128)

        def mm_group(b, lhs_fn, rhs_fn, nk, reads):
            for kc in range(nk):
                S.add("pe", lambda e, kc=kc, l_=lhs_fn(kc), r_=rhs_fn(kc): e.matmul(ps[b][:, :], lhsT=l_, rhs=r_, start=(kc == 0), stop=(kc == nk - 1)),
                      reads=reads, writes=["ps%d" % b])

        for ch in range(NOWN_ // 512 if upto >= 3 else 0):
            a0 = ch * 512
            S.add("sp", lambda e, a0=a0: e.dma_start(out=xs, in_=xT_v[:, :, a0:a0 + 512]), writes=["xs"], dma=True)
            S.add("sp", lambda e, a0=a0: e.dma_start(out=oTc, in_=oT_v[:, :, a0:a0 + 512]), writes=["aT"], dma=True)
            rms_stats(xs, 512, sq3, "xs")
            norm_apply(xs, 512, 0, uT3, "xs", "uT3")
            for sgp in range(4):
                wga, kga = get_slab(w_in, 0, KC, 6152 + sgp * 512, 512)
                wa_, kwa = get_slab(w_a, 0, 8, sgp * 512, 512)
                for ct in range(4):
                    b = next_bank()
                    mm_group(b, lambda kc, ct=ct: wga[:, kc, ct * 128:(ct + 1) * 128], lambda kc: uT3[:, kc, :], KC, ["uT3", kga])
                    S.add("act", lambda e, b=b: e.activation(out=gA, in_=ps[b][:, :], func=AF.Sigmoid), reads=["ps%d" % b], writes=["gA"])
                    b2 = next_bank()
                    mm_group(b2, lambda kc, ct=ct: wa_[:, kc, ct * 128:(ct + 1) * 128], lambda kc: oTc[:, kc, :], 8, ["aT", kwa])
                    S.add("dve", lambda e, b2=b2, ct=ct, sgp=sgp: e.tensor_tensor(out=mrg[:, sgp * 4 + ct, :], in0=ps[b2][:, :], in1=gA, op=ALU.mult),
                          reads=["ps%d" % b2, "gA"], writes=["sq"])
                wgf, kgf = get_slab(w_in, 0, KC, 8200 + sgp * 512, 512)
                wf_, kwf = get_slab(w_f, 0, 8, sgp * 512, 512)
                for ct in range(4):
                    b = next_bank()
                    mm_group(b, lambda kc, ct=ct: wgf[:, kc, ct * 128:(ct + 1) * 128], lambda kc: uT3[:, kc, :], KC, ["uT3", kgf])
                    S.add("act", lambda e, b=b: e.activation(out=gF, in_=ps[b][:, :], func=AF.Sigmoid), reads=["ps%d" % b], writes=["gF"])
                    b2 = next_bank()
                    mm_group(b2, lambda kc, ct=ct: wf_[:, kc, ct * 128:(ct + 1) * 128], lambda kc: oTc[:, 8 + kc, :], 8, ["aT", kwf])
                    S.add("dve", lambda e, b2=b2: e.tensor_tensor(out=m2, in0=ps[b2][:, :], in1=gF, op=ALU.mult),
                          reads=["ps%d" % b2, "gF"], writes=["m2"])
                    S.add("dve", lambda e, ct=ct, sgp=sgp: e.tensor_tensor(out=mrg[:, sgp * 4 + ct, :], in0=mrg[:, sgp * 4 + ct, :], in1=m2, op=ALU.add),
                          reads=["m2", "sq"], writes=["sq"])
            for sgp in range(4):
                wo_, kwo = get_slab(w_o, 0, KC, sgp * 512, 512)
                for ct in range(4):
                    b = next_bank()
                    mm_group(b, lambda kc, ct=ct: wo_[:, kc, ct * 128:(ct + 1) * 128], lambda kc: mrg[:, kc, :], KC, ["sq", kwo])
                    S.add("dve", lambda e, b=b, ct=ct, sgp=sgp: e.tensor_tensor(out=xs[:, sgp * 4 + ct, :], in0=xs[:, sgp * 4 + ct, :], in1=ps[b][:, :], op=ALU.add),
                          reads=["ps%d" % b, "xs"], writes=["xs"])
            rms_stats(xs, 512, sq3, "xs")
            norm_apply(xs, 512, 16, uT3, "xs", "uT3")
            for sgp in range(11):
                wg_, kwg = get_slab(w_g, 0, KC, sgp * 512, 512)
                wu_, kwu = get_slab(w_u, 0, KC, sgp * 512, 512)
                for ct in range(4):
                    b = next_bank()
                    mm_group(b, lambda kc, ct=ct: wg_[:, kc, ct * 128:(ct + 1) * 128], lambda kc: uT3[:, kc, :], KC, ["uT3", kwg])
                    S.add("act", lambda e, b=b: e.activation(out=sgt, in_=ps[b][:, :], func=AF.Silu), reads=["ps%d" % b], writes=["sgt"])
                    b2 = next_bank()
                    mm_group(b2, lambda kc, ct=ct: wu_[:, kc, ct * 128:(ct + 1) * 128], lambda kc: uT3[:, kc, :], KC, ["uT3", kwu])
                    S.add("dve", lambda e, b2=b2, ct=ct, sgp=sgp: e.tensor_tensor(out=aT[:, sgp * 4 + ct, :], in0=ps[b2][:, :], in1=sgt, op=ALU.mult),
                          reads=["ps%d" % b2, "sgt"], writes=["aT"])
            for ct in range(16):
                b = next_bank()
                wd_, kwd = get_slab(w_d, 0, FC, ct * 128, 128)
                for kc in range(FC):
                    S.add("pe", lambda e, kc=kc, b=b, wd_=wd_: e.matmul(ps[b][:, :], lhsT=wd_[:, kc, :], rhs=aT[:, kc, :], start=(kc == 0), stop=(kc == FC - 1)),
                          reads=["aT", kwd], writes=["ps%d" % b])
                S.add("dve", lambda e, b=b, ct=ct: e.tensor_tensor(out=xs[:, ct, :], in0=xs[:, ct, :], in1=ps[b][:, :], op=ALU.add),
                      reads=["ps%d" % b, "xs"], writes=["xs"])
            rms_stats(xs, 512, sq3, "xs")
            norm_apply(xs, 512, 32, xs, "xs", "xs")
            od = S.add("sp", lambda e, a0=a0: e.dma_start(out=outT_v[:, :, a0:a0 + 512], in_=xs), reads=["xs"], dma=True)
            out_dmas.append(od)
        ar.release()
        fz = S.fence(dummy)
        S.finish(out_dmas + [fz])
        S.emit(st)
    return nc


_CACHE = {}


def _tables(half, NT=32):
    t = np.zeros((128, T_END), np.float32)
    p = np.arange(128, dtype=np.float64)
    slopes = [2.0 ** (-8.0 * (i + 1) / 4) for i in range(4)]
    for h in range(4):
        s = slopes[h]
        for dl in range(32):
            t[:, T_TABD + h * 96 + dl] = s * (p - 256 * dl)
            if half == 0:
                t[:, T_TABD + h * 96 + 32 + dl] = NEG if dl == 0 else s * (p - 256 * dl + 128)
            else:
                t[:, T_TABD + h * 96 + 32 + dl] = s * (p - 256 * dl - 128)
            t[:, T_TABD + h * 96 + 64 + dl] = s * (p - 16 - 128 * (2 * dl + half))
        k = p[:, None]
        q = p[None, :]
        vis = (k // 64) <= (q // 64)
        t[:, T_DG + h * 128:T_DG + (h + 1) * 128] = np.where(vis, -s * np.abs(q - k) + s * q, NEG)
    k = p[:, None]
    q = p[None, :]
    t[:, T_FM:T_FM + 128] = np.where(k <= q, 0.0, NEG)
    t[:, T_FO] = 0.0 if half == 1 else NEG
    t[:, T_RM] = (p < 16)
    NKT = 2 * NT + 1
    rank = np.zeros(NKT)
    rank[2 * NT] = 0
    for T in range(2 * NT):
        idx = T // 2 + (0 if T % 2 == half else NT)
        rank[idx] = T + 1
    t[0:NKT, T_PREF:T_PREF + NKT] = (rank[:, None] < rank[None, :])
    t[:, T_TRI:T_TRI + 128] = (k <= q)
    t[:, T_SEL:T_SEL + 128] = (k == 64)
    t[:, T_ID:T_ID + 128] = (k == q)
    return t


def kernel(x, meta, g_mix, w_in, lambda_q1, lambda_k1, lambda_q2, lambda_k2, g_subln, b_f,
           w_branch_a, w_branch_f, w_out, g_ffn, w_gate, w_up, w_down, g_final):
    x = np.asarray(x, np.float32)
    f = lambda a: np.ascontiguousarray(np.asarray(a, np.float32))
    if "nc" not in _CACHE:
        _CACHE["nc"] = build_program()
    nc = _CACHE["nc"]
    w_in0, w_a0, w_f0, w_o0 = f(w_in[0]), f(w_branch_a[0]), f(w_branch_f[0]), f(w_out[0])
    w_g0, w_u0, w_d0 = f(w_gate[0]), f(w_up[0]), f(w_down[0])
    meta = f(meta)
    in_maps = []
    for c in range(8):
        b, half = c // 2, c % 2
        xb = x[b].reshape(64, 128, D)
        toks = np.concatenate([xb[half::2].reshape(NOWN, D), xb[1 - half::2].reshape(NOWN, D), meta,
                               np.zeros((112, D), np.float32)], axis=0)
        t = _tables(half)
        gv = np.concatenate([f(g_mix[0]).reshape(16, 128).T, f(g_ffn[0]).reshape(16, 128).T, f(g_final).reshape(16, 128).T], axis=1)
        t[:, T_G:T_G + 48] = gv
        t[:, T_GSUB:T_GSUB + 256] = f(g_subln[0])[None, :]
        t[:, T_BF:T_BF + 8] = f(b_f[0])[None, :]
        t[:, T_LAM:T_LAM + 512] = np.concatenate([f(lambda_q1[0]), f(lambda_k1[0]), f(lambda_q2[0]), f(lambda_k2[0])])[None, :]
        in_maps.append({"xT": np.ascontiguousarray(toks.T), "tabs": t, "w_in": w_in0, "w_a": w_a0, "w_f": w_f0, "w_o": w_o0,
                        "w_g": w_g0, "w_u": w_u0, "w_d": w_d0})
    res = run_bass_kernel_spmd(nc, in_maps, core_ids=list(range(8)))
    out = np.zeros((4, SEQ, D), np.float32)
    for c in range(8):
        b, half = c // 2, c % 2
        o = np.asarray(res.results[c]["outT"], np.float32).T.reshape(32, 128, D)
        out[b].reshape(64, 128, D)[half::2] = o
    return out
```

```python
import numpy as np
from contextlib import ExitStack
import concourse.bass as bass
import concourse.mybir as mybir
from concourse.bass_utils import run_bass_kernel_spmd

F32 = mybir.dt.float32
BF16 = mybir.dt.bfloat16
AF = mybir.ActivationFunctionType
ALU = mybir.AluOpType

N_DMA_SEMS = 48
N_SW_SEMS = 16


class Sched:
    ENGS = ("pe", "act", "dve", "pool", "sp")

    def __init__(self, nc):
        self.nc = nc
        self.ops = []
        self.last_w = {}
        self.readers = {}
        self.dma_uses = [0] * N_DMA_SEMS
        self.dma_rr = 0
        self.dma_rr_sw = 0
        self.final_deps = []

    def add(self, eng, fn, reads=(), writes=(), dma=False):
        oid = len(self.ops)
        reads = list(reads) + ["__all__"]
        deps = set()
        for k in reads:
            w = self.last_w.get(k)
            if w is not None:
                deps.add(w)
        for k in writes:
            w = self.last_w.get(k)
            if w is not None:
                deps.add(w)
            for r in self.readers.get(k, ()):
                deps.add(r)
        deps.discard(oid)
        best = {}
        keep = []
        bestd = {}
        for d in deps:
            od = self.ops[d]
            if od["dma"]:
                s_ = od["dsem"]
                if s_ not in bestd or od["dval"] > self.ops[bestd[s_]]["dval"]:
                    bestd[s_] = d
            else:
                e = od["eng"]
                if e not in best or d > best[e]:
                    best[e] = d
        keep.extend(bestd.values())
        for e, d in best.items():
            if e == "pe" and eng == "pe" and not dma:
                continue
            keep.append(d)
        op = {"eng": eng, "fn": fn, "deps": keep, "dma": dma, "signal": False,
              "seq": None, "dsem": None, "dval": None}
        if dma:
            if eng == "pool":
                s = N_DMA_SEMS - N_SW_SEMS + self.dma_rr_sw
                self.dma_rr_sw = (self.dma_rr_sw + 1) % N_SW_SEMS
            else:
                s = self.dma_rr
                self.dma_rr = (self.dma_rr + 1) % (N_DMA_SEMS - N_SW_SEMS)
            self.dma_uses[s] += 1
            op["dsem"] = s
            op["dval"] = 16 * self.dma_uses[s]
        for d in keep:
            self.ops[d]["signal"] = True
        self.ops.append(op)
        for k in reads:
            self.readers.setdefault(k, []).append(oid)
        for k in writes:
            self.last_w[k] = oid
            self.readers[k] = []
        return oid

    def fence(self, dummy_ap):
        return self.add("dve", lambda e: e.memset(dummy_ap, 0.0), writes=["__all__"])

    def finish(self, oids):
        self.final_deps = list(oids)
        for d in oids:
            self.ops[d]["signal"] = True

    def emit(self, stack):
        nc = self.nc
        esem = {e: stack.enter_context(nc.semaphore("es_" + e)) for e in self.ENGS}
        dsem = [stack.enter_context(nc.semaphore("ds_%d" % i)) for i in range(N_DMA_SEMS)]
        cnt = {e: 0 for e in self.ENGS}
        for op in self.ops:
            if not op["dma"] and op["signal"]:
                cnt[op["eng"]] += 1
                op["seq"] = cnt[op["eng"]]
        block = stack.enter_context(nc.Block())
        ops = self.ops
        final_deps = self.final_deps

        def stream(ename, eng):
            waited = {}

            def wait_for(d):
                od = ops[d]
                if od["dma"]:
                    key = ("d", od["dsem"])
                    val = od["dval"]
                    sem = dsem[od["dsem"]]
                else:
                    key = ("e", od["eng"])
                    val = od["seq"]
                    sem = esem[od["eng"]]
                if waited.get(key, 0) >= val:
                    return
                waited[key] = val
                eng.wait_ge(sem, val)

            for op in ops:
                if op["eng"] != ename:
                    continue
                for d in op["deps"]:
                    wait_for(d)
                if op["dma"]:
                    prev = op["dval"] - 16
                    key = ("d", op["dsem"])
                    if prev > 0 and waited.get(key, 0) < prev:
                        waited[key] = prev
                        eng.wait_ge(dsem[op["dsem"]], prev)
                    ins = op["fn"](eng)
                    ins.then_inc(dsem[op["dsem"]], 16)
                else:
                    ins = op["fn"](eng)
                    if op["signal"]:
                        ins.then_inc(esem[ename], 1)
            if ename == "sp":
                for d in final_deps:
                    wait_for(d)

        @block.sync
        def _(eng):
            stream("sp", eng)

        @block.scalar
        def _(eng):
            stream("act", eng)

        @block.vector
        def _(eng):
            stream("dve", eng)

        @block.gpsimd
        def _(eng):
            stream("pool", eng)

        @block.tensor
        def _(eng):
            stream("pe", eng)


class Arena:
    def __init__(self, ap_f32, nbytes):
        self.base = ap_f32
        self.nbytes = nbytes
        self.off = 0
        self.marks = []

    def alloc(self, shape_free, dtype):
        esz = 4 if dtype == F32 else 2
        n = int(np.prod(shape_free))
        nb = (n * esz + 31) // 32 * 32
        assert self.off + nb <= self.nbytes, ("arena overflow", self.off, nb, self.nbytes)
        a = self.base[:, self.off // 4:(self.off + nb) // 4]
        if dtype != F32:
            a = a.bitcast(dtype)
        a = a[:, 0:n]
        self.off += nb
        if len(shape_free) == 2:
            a = a.rearrange("p (a b) -> p a b", a=shape_free[0])
        elif len(shape_free) == 3:
            a = a.rearrange("p (a b c) -> p a b c", a=shape_free[0], b=shape_free[1])
        return a

    def mark(self):
        self.marks.append(self.off)

    def release(self):
        self.off = self.marks.pop()

D = 2048
KC = 16
SEQ = 8192
NOWN = 4096
NTOK = 8320
DFF = 5632
FC = 44
NEG = -30000.0
SCALE = 128 ** -0.5
LAMBDA_INIT = 0.2
T_TABD = 0
T_DG = T_TABD + 4 * 96
T_FM = T_DG + 4 * 128
T_FO = T_FM + 128
T_RM = T_FO + 1
T_PREF = T_RM + 1
T_TRI = T_PREF + 65
T_SEL = T_TRI + 128
T_ID = T_SEL + 128
T_G = T_ID + 128
T_GSUB = T_G + 48
T_BF = T_GSUB + 256
T_LAM = T_BF + 8
T_END = T_LAM + 512


def build_program(NT=32, dbg=False, upto=3):
    nc = bass.Bass("TRN2", target_bir_lowering=False)
    NOWN_ = NT * 128
    NKT = 2 * NT + 1
    META = 2 * NT
    NTOK_ = NKT * 128
    SK = "ExternalOutput" if dbg else "Internal"
    xT = nc.dram_tensor("xT", [D, NTOK_], F32, kind="ExternalInput").ap()
    tabs_d = nc.dram_tensor("tabs", [128, T_END], F32, kind="ExternalInput").ap()
    w_in = nc.dram_tensor("w_in", [D, 10248], F32, kind="ExternalInput").ap()
    w_a = nc.dram_tensor("w_a", [1024, D], F32, kind="ExternalInput").ap()
    w_f = nc.dram_tensor("w_f", [1024, D], F32, kind="ExternalInput").ap()
    w_o = nc.dram_tensor("w_o", [D, D], F32, kind="ExternalInput").ap()
    w_g = nc.dram_tensor("w_g", [D, DFF], F32, kind="ExternalInput").ap()
    w_u = nc.dram_tensor("w_u", [D, DFF], F32, kind="ExternalInput").ap()
    w_d = nc.dram_tensor("w_d", [DFF, D], F32, kind="ExternalInput").ap()
    outT = nc.dram_tensor("outT", [D, NOWN_], F32, kind="ExternalOutput").ap()
    kT_s = nc.dram_tensor("kT_s", [16, 128, NTOK_], BF16, kind=SK).ap()
    qT_s = nc.dram_tensor("qT_s", [16, 128, NOWN_], BF16, kind=SK).ap()
    v_s = nc.dram_tensor("v_s", [NTOK_, D], BF16, kind=SK).ap()
    oT_s = nc.dram_tensor("oT_s", [D, NOWN_], BF16, kind=SK).ap()
    wsc = nc.dram_tensor("wsc", [58, 128, 8192], BF16, kind="Internal").ap()

    with ExitStack() as st:
        ARENA_W = 53000
        arena_t = st.enter_context(nc.sbuf_tensor("arena", [128, ARENA_W], F32))
        ar = Arena(arena_t[:, :], ARENA_W * 4)
        ps = [st.enter_context(nc.psum_tensor("ps%d" % i, [128, 512], F32)) for i in range(8)]
        S = Sched(nc)
        uid = [0]

        def U(p):
            uid[0] += 1
            return "%s#%d" % (p, uid[0])

        tabs = ar.alloc((T_END,), F32)
        S.add("sp", lambda e: e.dma_start(out=tabs, in_=tabs_d), writes=["tabs"], dma=True)
        ident = ar.alloc((128,), BF16)
        S.add("dve", lambda e: e.tensor_copy(out=ident, in_=tabs[:, T_ID:T_ID + 128]), reads=["tabs"], writes=["cst"])
        ones_b = ar.alloc((128,), BF16)
        S.add("pool", lambda e: e.memset(ones_b, 1.0), writes=["cst1"])
        ones_f = ar.alloc((128,), F32)
        S.add("pool", lambda e: e.memset(ones_f, 1.0), writes=["cst2"])
        gsub = ar.alloc((256,), F32)
        S.add("dve", lambda e: e.tensor_scalar(out=gsub, in0=tabs[:, T_GSUB:T_GSUB + 256], scalar1=1.0 - LAMBDA_INIT,
                                              scalar2=None, op0=ALU.mult), reads=["tabs"], writes=["cst3"])
        lt = ar.alloc((128,), F32)
        lsum = ar.alloc((2,), F32)
        nlam = ar.alloc((1,), F32)
        for i in range(2):
            a0 = T_LAM + i * 256
            S.add("dve", lambda e, a0=a0: e.tensor_tensor(out=lt, in0=tabs[:, a0:a0 + 128], in1=tabs[:, a0 + 128:a0 + 256],
                                                          op=ALU.mult), reads=["tabs"], writes=["lt"])
            S.add("dve", lambda e, i=i: e.tensor_reduce(out=lsum[:, i:i + 1], in_=lt, axis=mybir.AxisListType.X, op=ALU.add),
                  reads=["lt"], writes=["lsum"])
        S.add("act", lambda e: e.activation(out=lsum, in_=lsum, func=AF.Exp), reads=["lsum"], writes=["lsum"])
        S.add("dve", lambda e: e.tensor_tensor(out=nlam, in0=lsum[:, 1:2], in1=lsum[:, 0:1], op=ALU.subtract),
              reads=["lsum"], writes=["nlam"])
        S.add("dve", lambda e: e.tensor_scalar(out=nlam, in0=nlam, scalar1=-LAMBDA_INIT, scalar2=None, op0=ALU.add),
              reads=["nlam"], writes=["nlam"])
        SPL = ar.alloc((NKT, 8), F32)
        dummy = ar.alloc((8,), F32)
        ev = [0]

        def evac(out, in_, reads, writes):
            ev[0] += 1
            if ev[0] % 2:
                S.add("dve", lambda e: e.tensor_copy(out=out, in_=in_), reads=reads, writes=writes)
            else:
                S.add("act", lambda e: e.copy(out=out, in_=in_), reads=reads, writes=writes)

        bank_rr = [0]

        def next_bank():
            b = 2 + bank_rr[0] % 6
            bank_rr[0] += 1
            return b

        xT_v = xT.rearrange("(kc p) t -> p kc t", p=128)

        def rms_stats(src, n, sq, ksrc, eps=1e-6):
            S.add("act", lambda e: e.activation(out=sq[:, :, 0:n], in_=src, func=AF.Square), reads=[ksrc], writes=["sq"])
            for kc in range(KC):
                S.add("pe", lambda e, kc=kc: e.matmul(ps[0][0:1, 0:n], lhsT=ones_b[:, 0:1], rhs=sq[:, kc, 0:n],
                                                      start=(kc == 0), stop=(kc == KC - 1)),
                      reads=["sq", "cst1"], writes=["ps0"])
            S.add("act", lambda e: e.activation(out=rs_row[0:1, 0:n], in_=ps[0][0:1, 0:n], func=AF.Sqrt, bias=eps_t[0:1, 0:1],
                                                scale=1.0 / D), reads=["ps0", "eps"], writes=["rsrow"])
            S.add("dve", lambda e: e.reciprocal(out=rs_row[0:1, 0:n], in_=rs_row[0:1, 0:n]), reads=["rsrow"], writes=["rsrow"])
            S.add("pe", lambda e: e.matmul(ps[1][:, 0:n], lhsT=ones_f[0:1, :], rhs=rs_row[0:1, 0:n], start=True, stop=True),
                  reads=["rsrow", "cst2"], writes=["ps1"])

        rs_row = ar.alloc((512,), F32)
        eps_t = ar.alloc((2,), F32)
        S.add("pool", lambda e: e.memset(eps_t[:, 0:1], 1e-6), writes=["eps"])
        S.add("pool", lambda e: e.memset(eps_t[:, 1:2], 1e-5), writes=["eps"])

        def norm_apply(src, n, gofs, dst, ksrc, kdst):
            for kc in range(KC):
                S.add("dve", lambda e, kc=kc: e.scalar_tensor_tensor(
                    out=dst[:, kc, 0:n], in0=src[:, kc, 0:n], scalar=tabs[:, T_G + gofs + kc:T_G + gofs + kc + 1],
                    in1=ps[1][:, 0:n], op0=ALU.mult, op1=ALU.mult), reads=[ksrc, "ps1", "tabs"], writes=[kdst])

        def load_slab(dst, w_ap, r0, nk, c0, ncol, key):
            src = w_ap[r0:r0 + nk * 128, c0:c0 + ncol].rearrange("(kc p) c -> p kc c", p=128)
            S.add("pool", lambda e: e.dma_start(out=dst[:, 0:nk, 0:ncol], in_=src), writes=[key], dma=True)

        ar.mark()
        SBT = min(1024, NOWN_)
        uTs = [ar.alloc((KC, SBT), BF16) for _ in range(2)]
        xs2 = [ar.alloc((KC, 512), F32) for _ in range(2)]
        sq = ar.alloc((KC, 512), BF16)
        slabs = [ar.alloc((KC, 512), BF16) for _ in range(2)]
        wz = ar.alloc((KC, 8), BF16)
        stg = [ar.alloc((512,), BF16) for _ in range(6)]
        ztmp = ar.alloc((8,), F32)
        load_slab(wz, w_in, 0, KC, 6144, 8, "wz")
        sbs = [(i * SBT, SBT, True) for i in range(NOWN_ // SBT)] + [(NOWN_ + i * SBT, SBT, False) for i in range(NOWN_ // SBT)] + [(2 * NOWN_, 128, False)]
        xi = [0]
        si = [0]
        gi = [0]
        out_dmas = []
        slab_list = []
        for s in range(2):
            slab_list.append((1024 + s * 512, "KT", s * 4))
        for s in range(2):
            slab_list.append((4096 + s * 512, "KT", 8 + s * 4))
        for s in range(2):
            slab_list.append((2048 + s * 512, "V", s * 512))
        for s in range(2):
            slab_list.append((5120 + s * 512, "V", 1024 + s * 512))
        for s in range(2):
            slab_list.append((0 + s * 512, "QT", s * 4))
        for s in range(2):
            slab_list.append((3072 + s * 512, "QT", 8 + s * 4))
        def norm_stages(sbi):
            t0, nt, own = sbs[sbi]
            uT = uTs[sbi % 2]
            ku = "uT%d" % (sbi % 2)
            nch = (nt + 511) // 512
            st = {}
            for c in range(nch):
                n = min(512, nt - c * 512)
                xs = xs2[xi[0] % 2]
                kx = "xs%d" % (xi[0] % 2)
                xi[0] += 1

                def f_load(xs=xs, n=n, a=t0 + c * 512, kx=kx):
                    S.add("sp", lambda e: e.dma_start(out=xs[:, :, 0:n], in_=xT_v[:, :, a:a + n]), writes=[kx], dma=True)

                def f_sq(xs=xs, n=n, kx=kx):
                    S.add("act", lambda e: e.activation(out=sq[:, :, 0:n], in_=xs[:, :, 0:n], func=AF.Square), reads=[kx], writes=["sq"])

                def f_stats(n=n):
                    for kc in range(KC):
                        S.add("pe", lambda e, kc=kc: e.matmul(ps[0][0:1, 0:n], lhsT=ones_b[:, 0:1], rhs=sq[:, kc, 0:n],
                                                              start=(kc == 0), stop=(kc == KC - 1)), reads=["sq", "cst1"], writes=["ps0"])
                    S.add("act", lambda e: e.activation(out=rs_row[0:1, 0:n], in_=ps[0][0:1, 0:n], func=AF.Sqrt, bias=eps_t[0:1, 0:1],
                                                        scale=1.0 / D), reads=["ps0", "eps"], writes=["rsrow"])
                    S.add("dve", lambda e: e.reciprocal(out=rs_row[0:1, 0:n], in_=rs_row[0:1, 0:n]), reads=["rsrow"], writes=["rsrow"])

                def f_apply(xs=xs, n=n, c=c, kx=kx):
                    S.add("pe", lambda e: e.matmul(ps[1][:, 0:n], lhsT=ones_f[0:1, :], rhs=rs_row[0:1, 0:n], start=True, stop=True),
                          reads=["rsrow", "cst2"], writes=["ps1"])
                    norm_apply(xs, n, 0, uT[:, :, c * 512:c * 512 + n], kx, ku)

                st.setdefault(1, []).append((3, f_load))
                st.setdefault(2 + c, []).append((2, f_sq))
                st.setdefault(3 + c, []).append((1, f_stats))
                st.setdefault(4 + c, []).append((0, f_apply))
            return {k_: [f for _, f in sorted(v_, key=lambda t_: t_[0])] for k_, v_ in st.items()}

        for stage_, fs_ in sorted(norm_stages(0).items()):
            for f_ in fs_:
                f_()
        for sbi, (t0, nt, own) in enumerate(sbs):
            nch = (nt + 511) // 512
            uT = uTs[sbi % 2]
            ku = "uT%d" % (sbi % 2)
            slab_n = [0]
            pend = [{}]
            for tt in range(nt // 128):
                b = next_bank()
                for kc in range(KC):
                    S.add("pe", lambda e, uT=uT, kc=kc, tt=tt, b=b: e.matmul(ps[b][:, 0:8], lhsT=uT[:, kc, tt * 128:(tt + 1) * 128],
                                                                      rhs=wz[:, kc, :], start=(kc == 0), stop=(kc == KC - 1)),
                          reads=[ku, "wz"], writes=["ps%d" % b])
                tile_idx = (t0 // 128 + tt) if t0 < 2 * NOWN_ else META
                S.add("dve", lambda e, b=b: e.tensor_tensor(out=ztmp, in0=ps[b][:, 0:8], in1=tabs[:, T_BF:T_BF + 8], op=ALU.add),
                      reads=["ps%d" % b, "tabs"], writes=["ztmp"])
                S.add("act", lambda e: e.activation(out=ztmp, in_=ztmp, func=AF.Exp, scale=-1.0), reads=["ztmp"], writes=["ztmp"])
                S.add("act", lambda e, ti=tile_idx: e.activation(out=SPL[:, ti, :], in_=ztmp, func=AF.Ln, bias=ones_f[:, 0:1], scale=1.0),
                      reads=["ztmp", "cst2"], writes=["SPL"])
            for (c0, kind, idx) in slab_list:
                if kind == "QT" and not own:
                    continue
                if sbi + 1 < len(sbs):
                    if slab_n[0] == 1:
                        pend[0] = norm_stages(sbi + 1)
                    for f_ in pend[0].get(slab_n[0], []):
                        f_()
                slab_n[0] += 1
                sl = slabs[si[0] % 2]
                ks = "slab%d" % (si[0] % 2)
                si[0] += 1
                load_slab(sl, w_in, 0, KC, c0, 512, ks)
                if kind in ("KT", "QT"):
                    dst_s = kT_s if kind == "KT" else qT_s
                    for ct in range(4):
                        for c in range(nch):
                            n = min(512, nt - c * 512)
                            b = next_bank()
                            for kc in range(KC):
                                S.add("pe", lambda e, uT=uT, kc=kc, ct=ct, c=c, n=n, b=b, sl=sl: e.matmul(
                                    ps[b][:, 0:n], lhsT=sl[:, kc, ct * 128:(ct + 1) * 128], rhs=uT[:, kc, c * 512:c * 512 + n],
                                    start=(kc == 0), stop=(kc == KC - 1)), reads=[ku, ks], writes=["ps%d" % b])
                            sg = stg[gi[0] % 6]
                            kg = "stg%d" % (gi[0] % 6)
                            gi[0] += 1
                            evac(sg[:, 0:n], ps[b][:, 0:n], ["ps%d" % b], [kg])
                            S.add("sp", lambda e, sg=sg, n=n, m=idx + ct, a=t0 + c * 512, dst_s=dst_s: e.dma_start(
                                out=dst_s[m, :, a:a + n], in_=sg[:, 0:n]), reads=[kg], writes=[U("scr")], dma=True)
                else:
                    for tt in range(nt // 128):
                        b = next_bank()
                        for kc in range(KC):
                            S.add("pe", lambda e, uT=uT, kc=kc, tt=tt, b=b, sl=sl: e.matmul(
                                ps[b][:, :], lhsT=uT[:, kc, tt * 128:(tt + 1) * 128], rhs=sl[:, kc, :],
                                start=(kc == 0), stop=(kc == KC - 1)), reads=[ku, ks], writes=["ps%d" % b])
                        sg = stg[gi[0] % 6]
                        kg = "stg%d" % (gi[0] % 6)
                        gi[0] += 1
                        evac(sg, ps[b][:, :], ["ps%d" % b], [kg])
                        S.add("sp", lambda e, sg=sg, a=t0 + tt * 128, idx=idx: e.dma_start(
                            out=v_s[a:a + 128, idx:idx + 512], in_=sg), reads=[kg], writes=[U("scr")], dma=True)
        ar.release()
        S.fence(dummy)

        ar.mark()
        Wc = ar.alloc((NKT, 8), F32)
        A = ar.alloc((NKT, 8), F32)
        tot = ar.alloc((8,), F32)
        totb = ar.alloc((128,), F32)
        CB = ar.alloc((NT, 8), F32)
        trif = tabs[:, T_TRI:T_TRI + 128]
        self64 = tabs[:, T_SEL:T_SEL + 128]
        pref = tabs[0:NKT, T_PREF:T_PREF + NKT]
        S.add("dve", lambda e: e.tensor_scalar(out=SPL[:, META, :], in0=SPL[:, META, :], scalar1=tabs[:, T_RM:T_RM + 1], scalar2=None,
                                              op0=ALU.mult), reads=["SPL", "tabs"], writes=["SPL"])
        SPLf = SPL.rearrange("p a b -> p (a b)")
        Wcf = Wc.rearrange("p a b -> p (a b)")
        for c0_ in range(0, NKT * 8, 512):
            n_ = min(512, NKT * 8 - c0_)
            S.add("pe", lambda e, c0_=c0_, n_=n_: e.matmul(ps[2][:, 0:n_], lhsT=trif, rhs=SPLf[:, c0_:c0_ + n_], start=True, stop=True),
                  reads=["SPL", "tabs"], writes=["ps2"])
            S.add("dve", lambda e, c0_=c0_, n_=n_: e.tensor_copy(out=Wcf[:, c0_:c0_ + n_], in_=ps[2][:, 0:n_]), reads=["ps2"], writes=["Wc"])
        for h in range(8):
            S.add("pe", lambda e, h=h: e.matmul(ps[4][0:NKT, h:h + 1], lhsT=SPL[:, :, h], rhs=ones_f[:, 0:1], start=True, stop=True),
                  reads=["SPL", "cst2"], writes=["ps4"])
        S.add("dve", lambda e: e.tensor_copy(out=tot[0:NKT, :], in_=ps[4][0:NKT, 0:8]), reads=["ps4"], writes=["tot"])
        for h in range(8):
            S.add("dve", lambda e, h=h: e.tensor_scalar(out=totb[0:NKT, :], in0=ones_f[0:NKT, :], scalar1=tot[0:NKT, h:h + 1], scalar2=None,
                                                         op0=ALU.mult), reads=["tot", "cst2"], writes=["totb"])
            S.add("pe", lambda e: e.matmul(ps[5][:, 0:NKT], lhsT=totb[0:NKT, :], rhs=pref, start=True, stop=True),
                  reads=["totb", "tabs"], writes=["ps5"])
            S.add("dve", lambda e, h=h: e.tensor_tensor(out=A[:, :, h], in0=Wc[:, :, h], in1=ps[5][:, 0:NKT], op=ALU.add),
                  reads=["Wc", "ps5"], writes=["A"])
        Af = A.rearrange("p a b -> p (a b)")
        S.add("pe", lambda e: e.matmul(ps[6][:, 0:NT * 8], lhsT=self64, rhs=Af[:, 0:NT * 8], start=True, stop=True), reads=["A", "tabs"], writes=["ps6"])
        S.add("dve", lambda e: e.tensor_copy(out=CB.rearrange("p a b -> p (a b)"), in_=ps[6][:, 0:NT * 8]), reads=["ps6"], writes=["CB"])

        wreg = {}
        wlist = []
        for sgp in range(4):
            wlist += [(w_in, 0, KC, 6152 + sgp * 512, 512), (w_a, 0, 8, sgp * 512, 512),
                      (w_in, 0, KC, 8200 + sgp * 512, 512), (w_f, 0, 8, sgp * 512, 512)]
        for sgp in range(4):
            wlist.append((w_o, 0, KC, sgp * 512, 512))
        for sgp in range(11):
            wlist += [(w_g, 0, KC, sgp * 512, 512), (w_u, 0, KC, sgp * 512, 512)]
        for ct in range(16):
            wlist.append((w_d, 0, FC, ct * 128, 128))
        for idx, (w_ap, r0, nk, c0, ncol) in enumerate(wlist):
            wreg[(w_ap.tensor.name, r0, nk, c0, ncol)] = idx
            src = w_ap[r0:r0 + nk * 128, c0:c0 + ncol].rearrange("(kc p) c -> p kc c", p=128)
            dst = wsc[idx][:, 0:nk * ncol].rearrange("p (a b) -> p a b", a=nk)
            S.add("pool", lambda e, src=src, dst=dst: e.dma_start(out=dst, in_=src), writes=[U("wsc")], dma=True)
        KTb = [ar.alloc((NTOK_,), BF16) for _ in range(3)]
        QTb = [ar.alloc((NOWN_,), BF16) for _ in range(3)]
        q_i = [0]
        head_no = [0]
        Vb = [ar.alloc((NKT, 257), BF16) for _ in range(2)]
        oTh = ar.alloc((2, NOWN_), BF16)
        dtmp = [ar.alloc((128,), F32) for _ in range(2)]
        fin4 = [ar.alloc((8,), F32) for _ in range(4)]
        ofp4 = [ar.alloc((256,), F32) for _ in range(4)]
        junk = ar.alloc((256,), F32)
        obf4 = [ar.alloc((256,), BF16) for _ in range(4)]
        kq_i = [0]
        v_i = [0]
        pt_i = [0]
        dt_i = [0]
        ss_i = [0]
        acc_i = [0]
        bf_i = [0]
        v_sv = v_s.rearrange("(kt p) c -> p kt c", p=128)

        NPB = 8
        PB = [ar.alloc((512,), BF16) for _ in range(NPB)]
        ewD = ar.alloc((96,), F32)
        bFa = ar.alloc((NT, NKT), F32)
        o0 = ar.alloc((4, 256), F32)
        NG = NT // 4

        def attention_head(is_diff, h):
            nmap = 2 if is_diff else 1
            ncol = 256 if is_diff else 128
            maps = [2 * h, 2 * h + 1] if is_diff else [8 + h]
            vcol = h * 256 if is_diff else 1024 + h * 128
            orow = h * 256 if is_diff else 1024 + h * 128
            KT, QT, kk, kqk = [], [], [], []
            for m in maps:
                i = kq_i[0] % 3
                kq_i[0] += 1
                S.add("sp", lambda e, i=i, m=m: e.dma_start(out=KTb[i], in_=kT_s[m]), writes=["KT%d" % i], dma=True)
                iq = q_i[0] % 3
                q_i[0] += 1
                S.add("sp", lambda e, iq=iq, m=m: e.dma_start(out=QTb[iq], in_=qT_s[m]), writes=["QT%d" % iq], dma=True)
                KT.append(KTb[i]); QT.append(QTb[iq]); kk.append("KT%d" % i); kqk.append("QT%d" % iq)
            vi = v_i[0] % 2
            v_i[0] += 1
            V = Vb[vi]
            kv = "V%d" % vi
            for g0 in range(0, NKT, 13):
                g1 = min(NKT, g0 + 13)
                S.add("sp", lambda e, g0=g0, g1=g1: e.dma_start(out=V[:, g0:g1, 0:ncol], in_=v_sv[:, g0:g1, vcol:vcol + ncol]),
                      writes=[kv], dma=True)
            S.add("dve", lambda e: e.memset(V[:, :, ncol:ncol + 1], 1.0), writes=[kv])
            pool_ok = head_no[0] >= 2
            head_no[0] += 1
            if is_diff:
                S.add("act", lambda e: e.activation(out=ewD, in_=tabs[:, T_TABD + h * 96:T_TABD + (h + 1) * 96], func=AF.Exp),
                      reads=["tabs"], writes=["ew"])
            else:
                for j in range(NT):
                    S.add("dve", lambda e, j=j: e.tensor_scalar(out=bFa[:, j, :], in0=A[:, :, h], scalar1=CB[:, j, h:h + 1], scalar2=60.0,
                                                                 op0=ALU.subtract, op1=ALU.min), reads=["A", "CB"], writes=["ew"])
                    S.add("dve", lambda e, j=j: e.tensor_tensor(out=bFa[:, j, NT + j:NT + j + 1], in0=bFa[:, j, NT + j:NT + j + 1],
                                                                 in1=tabs[:, T_FO:T_FO + 1], op=ALU.add), reads=["ew", "tabs"], writes=["ew"])
                S.add("act", lambda e: e.activation(out=bFa, in_=bFa, func=AF.Exp), reads=["ew"], writes=["ew"])

            def ew_ap(G, kind, i, b0, nb):
                j0 = 4 * G + b0
                if is_diff:
                    if kind == "own":
                        c0 = j0 - i
                    elif kind == "oth":
                        c0 = 32 + j0 - i
                    else:
                        c0 = 64 + j0
                    return ewD[:, c0:c0 + nb]
                kt = i if kind == "own" else (NT + i if kind == "oth" else META)
                return bFa[:, j0:j0 + nb, kt]

            items = []
            for G in range(NG):
                for c in range(nmap):
                    lst = [dict(kt=META, kp=16, kind="meta", i=0, b0=0)]
                    for i in range(4 * G):
                        lst.append(dict(kt=i, kp=128, kind="own", i=i, b0=0))
                        lst.append(dict(kt=NT + i, kp=128, kind="oth", i=i, b0=0))
                    for a in range(4):
                        lst.append(dict(kt=NT + 4 * G + a, kp=128, kind="oth", i=4 * G + a, b0=a))
                        lst.append(dict(kt=4 * G + a, kp=128, kind="own", i=4 * G + a, b0=a, diag=True))
                    if is_diff:
                        slope = 2.0 ** (-8.0 * (h + 1) / 4)
                        keep = []
                        for it in lst:
                            if it["kind"] == "meta" or it.get("diag"):
                                keep.append(it)
                                continue
                            dl = 4 * G + it["b0"] - it["i"]
                            mx = slope * (127 - 256 * dl + (128 if it["kind"] == "oth" else 0))
                            if mx > -120.0:
                                keep.append(it)
                        lst = keep
                    for n_, it in enumerate(lst):
                        it.update(G=G, c=c, first=(n_ == 0), last=(n_ == len(lst) - 1))
                        items.append(it)
                    items.append(dict(T=True, G=G, c=c))
            par = [0]

            def emit_st(n, it):
                if it.get("T"):
                    return
                bank = n % 3
                G, c, kt, kp, b0 = it["G"], it["c"], it["kt"], it["kp"], it["b0"]
                q0 = (4 * G + b0) * 128
                q1 = (4 * G + 4) * 128
                S.add("pe", lambda e: e.matmul(ps[bank][0:kp, b0 * 128:512], lhsT=KT[c][:, kt * 128:kt * 128 + kp], rhs=QT[c][:, q0:q1],
                                               start=True, stop=True), reads=[kk[c], kqk[c]], writes=["ps%d" % bank])

            def emit_rest(n, it):
                if it.get("T"):
                    return
                bank = n % 3
                kbank = "ps%d" % bank
                P = PB[n % NPB]
                kPb = ["PB%d_%d" % (n % NPB, b_) for b_ in range(4)]
                kP = kPb[0]
                G, c, kt, kp, b0, kind = it["G"], it["c"], it["kt"], it["kp"], it["b0"], it["kind"]
                c0 = b0 * 128
                diag = it.get("diag", False)
                if diag:
                    di = dt_i[0] % 2
                    dt_i[0] += 1
                    dtm = dtmp[di]
                    kd = "dt%d" % di
                    mk = tabs[:, T_DG + h * 128:T_DG + (h + 1) * 128] if is_diff else tabs[:, T_FM:T_FM + 128]
                    S.add("dve", lambda e: e.scalar_tensor_tensor(out=dtm, in0=ps[bank][:, c0:c0 + 128], scalar=SCALE, in1=mk,
                                                                  op0=ALU.mult, op1=ALU.add), reads=["tabs"], writes=[kd, kbank])
                    S.add("act", lambda e: e.activation(out=P[:, c0:c0 + 128], in_=dtm, func=AF.Exp), reads=[kd], writes=[kPb[b0]])
                    if c0 + 128 < 512:
                        S.add("act", lambda e: e.activation(out=P[:, c0 + 128:512], in_=ps[bank][:, c0 + 128:512], func=AF.Exp, scale=SCALE),
                              reads=[kbank], writes=kPb[b0 + 1:4])
                else:
                    S.add("act", lambda e: e.activation(out=P[0:kp, c0:512], in_=ps[bank][0:kp, c0:512], func=AF.Exp, scale=SCALE),
                          reads=[kbank], writes=kPb[b0:4])
                sb0 = b0 + 1 if (diag and is_diff) else b0
                nb = 4 - sb0
                if nb > 0:
                    ew = ew_ap(G, kind, it["i"], sb0, nb)[0:kp].unsqueeze(2).broadcast_to([kp, nb, 128])
                    Pv = P[0:kp, sb0 * 128:512].rearrange("p (a b) -> p a b", a=nb)
                    par[0] += 1
                    eng_ = "pool" if (par[0] % 2 == 0 and pool_ok) else "dve"
                    if eng_ == "pool":
                        S.add(eng_, lambda e: e.tensor_tensor(out=Pv, in0=Pv, in1=ew, op=ALU.mult), reads=["ew"], writes=kPb[sb0:4])
                    else:
                        ew2 = ew_ap(G, kind, it["i"], sb0, nb)
                        for bb in range(nb):
                            S.add("dve", lambda e, bb=bb: e.tensor_scalar(out=P[0:kp, (sb0 + bb) * 128:(sb0 + bb + 1) * 128],
                                                                         in0=P[0:kp, (sb0 + bb) * 128:(sb0 + bb + 1) * 128],
                                                                         scalar1=ew2[0:kp, bb:bb + 1], scalar2=None, op0=ALU.mult),
                                  reads=["ew"], writes=[kPb[sb0 + bb]])

            def emit_pv(n, it):
                if it.get("T"):
                    finalize_pass(n, it["G"], it["c"])
                    return
                P = PB[n % NPB]
                kPb = ["PB%d_%d" % (n % NPB, b_) for b_ in range(4)]
                G, c, kt, kp, b0 = it["G"], it["c"], it["kt"], it["kp"], it["b0"]
                diag = it.get("diag", False)
                for b in range(b0, 4):
                    S.add("pe", lambda e, b=b: e.matmul(ps[4 + b][:, 0:ncol + 1], lhsT=P[0:kp, b * 128:(b + 1) * 128], rhs=V[0:kp, kt, 0:ncol + 1],
                                                        start=it["first"], stop=(it["last"] or (b == b0 and diag))),
                          reads=[kPb[b], kv], writes=["ps%d" % (4 + b)])

            def finalize_pass(n, G, c):
                bank = 3
                kbank = "ps%d" % bank
                tb = ps[bank][:, :].bitcast(BF16)
                nblk = ncol // 128
                for b in range(4):
                    a = 4 + b
                    ka = "ps%d" % a
                    fin, ofp, obf = fin4[b], ofp4[b], obf4[b]
                    kf, ko, kob = "fin%d" % b, "ofp%d" % b, "obf%d" % b
                    if is_diff:
                        S.add("dve", lambda e, a=a, fin=fin: e.reciprocal(out=fin[:, 0:1], in_=ps[a][:, 256:257]), reads=[ka], writes=[kf])
                        if c == 0:
                            S.add("dve", lambda e, a=a, fin=fin, b=b: e.tensor_scalar(out=o0[:, b, :], in0=ps[a][:, 0:256], scalar1=fin[:, 0:1],
                                                                                       scalar2=None, op0=ALU.mult), reads=[ka, kf], writes=["o0_%d" % b])
                            continue
                        S.add("dve", lambda e, fin=fin: e.tensor_tensor(out=fin[:, 2:3], in0=fin[:, 0:1], in1=nlam, op=ALU.mult),
                              reads=[kf, "nlam"], writes=[kf])
                        S.add("dve", lambda e, a=a, fin=fin, ofp=ofp, b=b: e.scalar_tensor_tensor(
                            out=ofp, in0=ps[a][:, 0:256], scalar=fin[:, 2:3], in1=o0[:, b, :], op0=ALU.mult, op1=ALU.add),
                            reads=[ka, kf, "o0_%d" % b], writes=[ko])
                    else:
                        S.add("dve", lambda e, a=a, fin=fin: e.reciprocal(out=fin[:, 0:1], in_=ps[a][:, 128:129]), reads=[ka], writes=[kf])
                        S.add("dve", lambda e, a=a, fin=fin, obf=obf: e.tensor_scalar(out=obf[:, 0:128], in0=ps[a][:, 0:128], scalar1=fin[:, 0:1],
                                                                                       scalar2=None, op0=ALU.mult), reads=[ka, kf], writes=[kob])
                if is_diff and c == 0:
                    return
                if is_diff:
                    for b in range(4):
                        fin, ofp = fin4[b], ofp4[b]
                        kf, ko = "fin%d" % b, "ofp%d" % b
                        S.add("act", lambda e, fin=fin, ofp=ofp: e.activation(out=junk, in_=ofp, func=AF.Square, accum_out=fin[:, 3:4]),
                              reads=[ko], writes=["junk", kf])
                        S.add("act", lambda e, fin=fin: e.activation(out=fin[:, 4:5], in_=fin[:, 3:4], func=AF.Sqrt, bias=eps_t[:, 1:2], scale=1.0 / 256),
                              reads=[kf, "eps"], writes=[kf])
                    for b in range(4):
                        fin, ofp, obf = fin4[b], ofp4[b], obf4[b]
                        kf, ko, kob = "fin%d" % b, "ofp%d" % b, "obf%d" % b
                        S.add("dve", lambda e, fin=fin: e.reciprocal(out=fin[:, 5:6], in_=fin[:, 4:5]), reads=[kf], writes=[kf])
                        S.add("dve", lambda e, fin=fin, ofp=ofp, obf=obf: e.scalar_tensor_tensor(out=obf, in0=ofp, scalar=fin[:, 5:6], in1=gsub,
                                                                                                  op0=ALU.mult, op1=ALU.mult),
                              reads=[ko, kf, "cst3"], writes=[kob])
                for b in range(4):
                    obf = obf4[b]
                    for blk in range(nblk):
                        sl_ = b * nblk + blk
                        S.add("pe", lambda e, obf=obf, blk=blk, sl_=sl_: e.transpose(tb[:, sl_ * 128:(sl_ + 1) * 128], obf[:, blk * 128:(blk + 1) * 128], ident),
                              reads=["obf%d" % b, "cst"], writes=[kbank])
                for b in range(4):
                    j = 4 * G + b
                    src = tb[:, b * nblk * 128:(b + 1) * nblk * 128].rearrange("p (a q) -> p a q", a=nblk)
                    S.add("dve", lambda e, src=src, j=j: e.tensor_copy(out=oTh[:, 0:nblk, j * 128:(j + 1) * 128], in_=src),
                          reads=[kbank], writes=["oTh"])

            LA, LP = 2, 6
            for n in range(len(items) + LP):
                if n < len(items):
                    emit_st(n, items[n])
                if 0 <= n - LA < len(items):
                    emit_rest(n - LA, items[n - LA])
                if n - LP >= 0:
                    emit_pv(n - LP, items[n - LP])
            for blk in range(ncol // 128):
                S.add("sp", lambda e, blk=blk: e.dma_start(out=oT_s[orow + blk * 128:orow + (blk + 1) * 128, :], in_=oTh[:, blk, :]),
                      reads=["oTh"], writes=[U("scr")], dma=True)

        if upto >= 2:
            for hd in range(4):
                attention_head(True, hd)
                attention_head(False, 2 * hd)
                attention_head(False, 2 * hd + 1)
        ar.release()
        S.fence(dummy)

        ar.mark()
        xs = ar.alloc((KC, 512), F32)
        uT3 = ar.alloc((KC, 512), BF16)
        sq3 = ar.alloc((KC, 512), BF16)
        aT = ar.alloc((FC, 512), BF16)
        oTc = aT[:, 0:KC, :]
        mrg = sq3
        gA = ar.alloc((512,), BF16)
        gF = ar.alloc((512,), BF16)
        m1 = ar.alloc((512,), F32)
        m2 = ar.alloc((512,), F32)
        sgt = ar.alloc((512,), F32)
        NSL = 4
        sl3 = [ar.alloc((KC * 512,), BF16) for _ in range(NSL)]
        s3 = [0]

        def get_slab(w_ap, r0, nk, c0, ncol):
            i = s3[0] % NSL
            s3[0] += 1
            v = sl3[i][:, 0:nk * ncol].rearrange("p (a b) -> p a b", a=nk)
            idx = wreg[(w_ap.tensor.name, r0, nk, c0, ncol)]
            S.add("sp", lambda e, i=i, idx=idx, n_=nk * ncol: e.dma_start(out=sl3[i][:, 0:n_], in_=wsc[idx][:, 0:n_]),
                  writes=["sl3_%d" % i], dma=True)
            return v, "sl3_%d" % i

        oT_v = oT_s.rearrange("(kc p) t -> p kc t", p=128)
        outT_v = outT.rearrange("(kc p) t -> p kc t", p=128)

        def mm_group(b, lhs_fn, rhs_fn, nk, reads):
            for kc in range(nk):
                S.add("pe", lambda e, kc=kc, l_=lhs_fn(kc), r_=rhs_fn(kc): e.matmul(ps[b][:, :], lhsT=l_, rhs=r_, start=(kc == 0), stop=(kc == nk - 1)),
                      reads=reads, writes=["ps%d" % b])

        for ch in range(NOWN_ // 512 if upto >= 3 else 0):
            a0 = ch * 512
            S.add("sp", lambda e, a0=a0: e.dma_start(out=xs, in_=xT_v[:, :, a0:a0 + 512]), writes=["xs"], dma=True)
            S.add("sp", lambda e, a0=a0: e.dma_start(out=oTc, in_=oT_v[:, :, a0:a0 + 512]), writes=["aT"], dma=True)
            rms_stats(xs, 512, sq3, "xs")
            norm_apply(xs, 512, 0, uT3, "xs", "uT3")
            for sgp in range(4):
                wga, kga = get_slab(w_in, 0, KC, 6152 + sgp * 512, 512)
                wa_, kwa = get_slab(w_a, 0, 8, sgp * 512, 512)
                for ct in range(4):
                    b = next_bank()
                    mm_group(b, lambda kc, ct=ct: wga[:, kc, ct * 128:(ct + 1) * 128], lambda kc: uT3[:, kc, :], KC, ["uT3", kga])
                    S.add("act", lambda e, b=b: e.activation(out=gA, in_=ps[b][:, :], func=AF.Sigmoid), reads=["ps%d" % b], writes=["gA"])
                    b2 = next_bank()
                    mm_group(b2, lambda kc, ct=ct: wa_[:, kc, ct * 128:(ct + 1) * 128], lambda kc: oTc[:, kc, :], 8, ["aT", kwa])
                    S.add("dve", lambda e, b2=b2, ct=ct, sgp=sgp: e.tensor_tensor(out=mrg[:, sgp * 4 + ct, :], in0=ps[b2][:, :], in1=gA, op=ALU.mult),
                          reads=["ps%d" % b2, "gA"], writes=["sq"])
                wgf, kgf = get_slab(w_in, 0, KC, 8200 + sgp * 512, 512)
                wf_, kwf = get_slab(w_f, 0, 8, sgp * 512, 512)
                for ct in range(4):
                    b = next_bank()
                    mm_group(b, lambda kc, ct=ct: wgf[:, kc, ct * 128:(ct + 1) * 128], lambda kc: uT3[:, kc, :], KC, ["uT3", kgf])
                    S.add("act", lambda e, b=b: e.activation(out=gF, in_=ps[b][:, :], func=AF.Sigmoid), reads=["ps%d" % b], writes=["gF"])
                    b2 = next_bank()
                    mm_group(b2, lambda kc, ct=ct: wf_[:, kc, ct * 128:(ct + 1) * 128], lambda kc: oTc[:, 8 + kc, :], 8, ["aT", kwf])
                    S.add("dve", lambda e, b2=b2: e.tensor_tensor(out=m2, in0=ps[b2][:, :], in1=gF, op=ALU.mult),
                          reads=["ps%d" % b2, "gF"], writes=["m2"])
                    S.add("dve", lambda e, ct=ct, sgp=sgp: e.tensor_tensor(out=mrg[:, sgp * 4 + ct, :], in0=mrg[:, sgp * 4 + ct, :], in1=m2, op=ALU.add),
                          reads=["m2", "sq"], writes=["sq"])
            for sgp in range(4):
                wo_, kwo = get_slab(w_o, 0, KC, sgp * 512, 512)
                for ct in range(4):
                    b = next_bank()
                    mm_group(b, lambda kc, ct=ct: wo_[:, kc, ct * 128:(ct + 1) * 128], lambda kc: mrg[:, kc, :], KC, ["sq", kwo])
                    S.add("dve", lambda e, b=b, ct=ct, sgp=sgp: e.tensor_tensor(out=xs[:, sgp * 4 + ct, :], in0=xs[:, sgp * 4 + ct, :], in1=ps[b][:, :], op=ALU.add),
                          reads=["ps%d" % b, "xs"], writes=["xs"])
            rms_stats(xs, 512, sq3, "xs")
            norm_apply(xs, 512, 16, uT3, "xs", "uT3")
            for sgp in range(11):
                wg_, kwg = get_slab(w_g, 0, KC, sgp * 512, 512)
                wu_, kwu = get_slab(w_u, 0, KC, sgp * 512, 512)
                for ct in range(4):
                    b = next_bank()
                    mm_group(b, lambda kc, ct=ct: wg_[:, kc, ct * 128:(ct + 1) * 128], lambda kc: uT3[:, kc, :], KC, ["uT3", kwg])
                    S.add("act", lambda e, b=b: e.activation(out=sgt, in_=ps[b][:, :], func=AF.Silu), reads=["ps%d" % b], writes=["sgt"])
                    b2 = next_bank()
                    mm_group(b2, lambda kc, ct=ct: wu_[:, kc, ct * 128:(ct + 1) * 128], lambda kc: uT3[:, kc, :], KC, ["uT3", kwu])
                    S.add("dve", lambda e, b2=b2, ct=ct, sgp=sgp: e.tensor_tensor(out=aT[:, sgp * 4 + ct, :], in0=ps[b2][:, :], in1=sgt, op=ALU.mult),
                          reads=["ps%d" % b2, "sgt"], writes=["aT"])
            for ct in range(16):
                b = next_bank()
                wd_, kwd = get_slab(w_d, 0, FC, ct * 128, 128)
                for kc in range(FC):
                    S.add("pe", lambda e, kc=kc, b=b, wd_=wd_: e.matmul(ps[b][:, :], lhsT=wd_[:, kc, :], rhs=aT[:, kc, :], start=(kc == 0), stop=(kc == FC - 1)),
                          reads=["aT", kwd], writes=["ps%d" % b])
                S.add("dve", lambda e, b=b, ct=ct: e.tensor_tensor(out=xs[:, ct, :], in0=xs[:, ct, :], in1=ps[b][:, :], op=ALU.add),
                      reads=["ps%d" % b, "xs"], writes=["xs"])
            rms_stats(xs, 512, sq3, "xs")
            norm_apply(xs, 512, 32, xs, "xs", "xs")
            od = S.add("sp", lambda e, a0=a0: e.dma_start(out=outT_v[:, :, a0:a0 + 512], in_=xs), reads=["xs"], dma=True)
            out_dmas.append(od)
        ar.release()
        fz = S.fence(dummy)
        S.finish(out_dmas + [fz])
        S.emit(st)
    return nc


_CACHE = {}


def _tables(half, NT=32):
    t = np.zeros((128, T_END), np.float32)
    p = np.arange(128, dtype=np.float64)
    slopes = [2.0 ** (-8.0 * (i + 1) / 4) for i in range(4)]
    for h in range(4):
        s = slopes[h]
        for dl in range(32):
            t[:, T_TABD + h * 96 + dl] = s * (p - 256 * dl)
            if half == 0:
                t[:, T_TABD + h * 96 + 32 + dl] = NEG if dl == 0 else s * (p - 256 * dl + 128)
            else:
                t[:, T_TABD + h * 96 + 32 + dl] = s * (p - 256 * dl - 128)
            t[:, T_TABD + h * 96 + 64 + dl] = s * (p - 16 - 128 * (2 * dl + half))
        k = p[:, None]
        q = p[None, :]
        vis = (k // 64) <= (q // 64)
        t[:, T_DG + h * 128:T_DG + (h + 1) * 128] = np.where(vis, -s * np.abs(q - k) + s * q, NEG)
    k = p[:, None]
    q = p[None, :]
    t[:, T_FM:T_FM + 128] = np.where(k <= q, 0.0, NEG)
    t[:, T_FO] = 0.0 if half == 1 else NEG
    t[:, T_RM] = (p < 16)
    NKT = 2 * NT + 1
    rank = np.zeros(NKT)
    rank[2 * NT] = 0
    for T in range(2 * NT):
        idx = T // 2 + (0 if T % 2 == half else NT)
        rank[idx] = T + 1
    t[0:NKT, T_PREF:T_PREF + NKT] = (rank[:, None] < rank[None, :])
    t[:, T_TRI:T_TRI + 128] = (k <= q)
    t[:, T_SEL:T_SEL + 128] = (k == 64)
    t[:, T_ID:T_ID + 128] = (k == q)
    return t


def kernel(x, meta, g_mix, w_in, lambda_q1, lambda_k1, lambda_q2, lambda_k2, g_subln, b_f,
           w_branch_a, w_branch_f, w_out, g_ffn, w_gate, w_up, w_down, g_final):
    x = np.asarray(x, np.float32)
    f = lambda a: np.ascontiguousarray(np.asarray(a, np.float32))
    if "nc" not in _CACHE:
        _CACHE["nc"] = build_program()
    nc = _CACHE["nc"]
    w_in0, w_a0, w_f0, w_o0 = f(w_in[0]), f(w_branch_a[0]), f(w_branch_f[0]), f(w_out[0])
    w_g0, w_u0, w_d0 = f(w_gate[0]), f(w_up[0]), f(w_down[0])
    meta = f(meta)
    in_maps = []
    for c in range(8):
        b, half = c // 2, c % 2
        xb = x[b].reshape(64, 128, D)
        toks = np.concatenate([xb[half::2].reshape(NOWN, D), xb[1 - half::2].reshape(NOWN, D), meta,
                               np.zeros((112, D), np.float32)], axis=0)
        t = _tables(half)
        gv = np.concatenate([f(g_mix[0]).reshape(16, 128).T, f(g_ffn[0]).reshape(16, 128).T, f(g_final).reshape(16, 128).T], axis=1)
        t[:, T_G:T_G + 48] = gv
        t[:, T_GSUB:T_GSUB + 256] = f(g_subln[0])[None, :]
        t[:, T_BF:T_BF + 8] = f(b_f[0])[None, :]
        t[:, T_LAM:T_LAM + 512] = np.concatenate([f(lambda_q1[0]), f(lambda_k1[0]), f(lambda_q2[0]), f(lambda_k2[0])])[None, :]
        in_maps.append({"xT": np.ascontiguousarray(toks.T), "tabs": t, "w_in": w_in0, "w_a": w_a0, "w_f": w_f0, "w_o": w_o0,
                        "w_g": w_g0, "w_u": w_u0, "w_d": w_d0})
    res = run_bass_kernel_spmd(nc, in_maps, core_ids=list(range(8)))
    out = np.zeros((4, SEQ, D), np.float32)
    for c in range(8):
        b, half = c // 2, c % 2
        o = np.asarray(res.results[c]["outT"], np.float32).T.reshape(32, 128, D)
        out[b].reshape(64, 128, D)[half::2] = o
    return out
```

```python
import numpy as np
from contextlib import ExitStack
import concourse.bass as bass
import concourse.mybir as mybir
from concourse.bass_utils import run_bass_kernel_spmd

F32 = mybir.dt.float32
BF16 = mybir.dt.bfloat16
AF = mybir.ActivationFunctionType
ALU = mybir.AluOpType

N_DMA_SEMS = 48
N_SW_SEMS = 16


class Sched:
    ENGS = ("pe", "act", "dve", "pool", "sp")

    def __init__(self, nc):
        self.nc = nc
        self.ops = []
        self.last_w = {}
        self.readers = {}
        self.dma_uses = [0] * N_DMA_SEMS
        self.dma_rr = 0
        self.dma_rr_sw = 0
        self.final_deps = []

    def add(self, eng, fn, reads=(), writes=(), dma=False):
        oid = len(self.ops)
        reads = list(reads) + ["__all__"]
        deps = set()
        for k in reads:
            w = self.last_w.get(k)
            if w is not None:
                deps.add(w)
        for k in writes:
            w = self.last_w.get(k)
            if w is not None:
                deps.add(w)
            for r in self.readers.get(k, ()):
                deps.add(r)
        deps.discard(oid)
        best = {}
        keep = []
        bestd = {}
        for d in deps:
            od = self.ops[d]
            if od["dma"]:
                s_ = od["dsem"]
                if s_ not in bestd or od["dval"] > self.ops[bestd[s_]]["dval"]:
                    bestd[s_] = d
            else:
                e = od["eng"]
                if e not in best or d > best[e]:
                    best[e] = d
        keep.extend(bestd.values())
        for e, d in best.items():
            if e == "pe" and eng == "pe" and not dma:
                continue
            keep.append(d)
        op = {"eng": eng, "fn": fn, "deps": keep, "dma": dma, "signal": False,
              "seq": None, "dsem": None, "dval": None}
        if dma:
            if eng == "pool":
                s = N_DMA_SEMS - N_SW_SEMS + self.dma_rr_sw
                self.dma_rr_sw = (self.dma_rr_sw + 1) % N_SW_SEMS
            else:
                s = self.dma_rr
                self.dma_rr = (self.dma_rr + 1) % (N_DMA_SEMS - N_SW_SEMS)
            self.dma_uses[s] += 1
            op["dsem"] = s
            op["dval"] = 16 * self.dma_uses[s]
        for d in keep:
            self.ops[d]["signal"] = True
        self.ops.append(op)
        for k in reads:
            self.readers.setdefault(k, []).append(oid)
        for k in writes:
            self.last_w[k] = oid
            self.readers[k] = []
        return oid

    def fence(self, dummy_ap):
        return self.add("dve", lambda e: e.memset(dummy_ap, 0.0), writes=["__all__"])

    def finish(self, oids):
        self.final_deps = list(oids)
        for d in oids:
            self.ops[d]["signal"] = True

    def emit(self, stack):
        nc = self.nc
        esem = {e: stack.enter_context(nc.semaphore("es_" + e)) for e in self.ENGS}
        dsem = [stack.enter_context(nc.semaphore("ds_%d" % i)) for i in range(N_DMA_SEMS)]
        cnt = {e: 0 for e in self.ENGS}
        for op in self.ops:
            if not op["dma"] and op["signal"]:
                cnt[op["eng"]] += 1
                op["seq"] = cnt[op["eng"]]
        block = stack.enter_context(nc.Block())
        ops = self.ops
        final_deps = self.final_deps

        def stream(ename, eng):
            waited = {}

            def wait_for(d):
                od = ops[d]
                if od["dma"]:
                    key = ("d", od["dsem"])
                    val = od["dval"]
                    sem = dsem[od["dsem"]]
                else:
                    key = ("e", od["eng"])
                    val = od["seq"]
                    sem = esem[od["eng"]]
                if waited.get(key, 0) >= val:
                    return
                waited[key] = val
                eng.wait_ge(sem, val)

            for op in ops:
                if op["eng"] != ename:
                    continue
                for d in op["deps"]:
                    wait_for(d)
                if op["dma"]:
                    prev = op["dval"] - 16
                    key = ("d", op["dsem"])
                    if prev > 0 and waited.get(key, 0) < prev:
                        waited[key] = prev
                        eng.wait_ge(dsem[op["dsem"]], prev)
                    ins = op["fn"](eng)
                    ins.then_inc(dsem[op["dsem"]], 16)
                else:
                    ins = op["fn"](eng)
                    if op["signal"]:
                        ins.then_inc(esem[ename], 1)
            if ename == "sp":
                for d in final_deps:
                    wait_for(d)

        @block.sync
        def _(eng):
            stream("sp", eng)

        @block.scalar
        def _(eng):
            stream("act", eng)

        @block.vector
        def _(eng):
            stream("dve", eng)

        @block.gpsimd
        def _(eng):
            stream("pool", eng)

        @block.tensor
        def _(eng):
            stream("pe", eng)


class Arena:
    def __init__(self, ap_f32, nbytes):
        self.base = ap_f32
        self.nbytes = nbytes
        self.off = 0
        self.marks = []

    def alloc(self, shape_free, dtype):
        esz = 4 if dtype == F32 else 2
        n = int(np.prod(shape_free))
        nb = (n * esz + 31) // 32 * 32
        assert self.off + nb <= self.nbytes, ("arena overflow", self.off, nb, self.nbytes)
        a = self.base[:, self.off // 4:(self.off + nb) // 4]
        if dtype != F32:
            a = a.bitcast(dtype)
        a = a[:, 0:n]
        self.off += nb
        if len(shape_free) == 2:
            a = a.rearrange("p (a b) -> p a b", a=shape_free[0])
        elif len(shape_free) == 3:
            a = a.rearrange("p (a b c) -> p a b c", a=shape_free[0], b=shape_free[1])
        return a

    def mark(self):
        self.marks.append(self.off)

    def release(self):
        self.off = self.marks.pop()

D = 2048
KC = 16
SEQ = 8192
NOWN = 4096
NTOK = 8320
DFF = 5632
FC = 44
NEG = -30000.0
SCALE = 128 ** -0.5
LAMBDA_INIT = 0.2
T_TABD = 0
T_DG = T_TABD + 4 * 96
T_FM = T_DG + 4 * 128
T_FO = T_FM + 128
T_RM = T_FO + 1
T_PREF = T_RM + 1
T_TRI = T_PREF + 65
T_SEL = T_TRI + 128
T_ID = T_SEL + 128
T_G = T_ID + 128
T_GSUB = T_G + 48
T_BF = T_GSUB + 256
T_LAM = T_BF + 8
T_END = T_LAM + 512


def build_program(NT=32, dbg=False, upto=3):
    nc = bass.Bass("TRN2", target_bir_lowering=False)
    NOWN_ = NT * 128
    NKT = 2 * NT + 1
    META = 2 * NT
    NTOK_ = NKT * 128
    SK = "ExternalOutput" if dbg else "Internal"
    xT = nc.dram_tensor("xT", [D, NTOK_], F32, kind="ExternalInput").ap()
    tabs_d = nc.dram_tensor("tabs", [128, T_END], F32, kind="ExternalInput").ap()
    w_in = nc.dram_tensor("w_in", [D, 10248], F32, kind="ExternalInput").ap()
    w_a = nc.dram_tensor("w_a", [1024, D], F32, kind="ExternalInput").ap()
    w_f = nc.dram_tensor("w_f", [1024, D], F32, kind="ExternalInput").ap()
    w_o = nc.dram_tensor("w_o", [D, D], F32, kind="ExternalInput").ap()
    w_g = nc.dram_tensor("w_g", [D, DFF], F32, kind="ExternalInput").ap()
    w_u = nc.dram_tensor("w_u", [D, DFF], F32, kind="ExternalInput").ap()
    w_d = nc.dram_tensor("w_d", [DFF, D], F32, kind="ExternalInput").ap()
    outT = nc.dram_tensor("outT", [D, NOWN_], F32, kind="ExternalOutput").ap()
    kT_s = nc.dram_tensor("kT_s", [16, 128, NTOK_], BF16, kind=SK).ap()
    qT_s = nc.dram_tensor("qT_s", [16, 128, NOWN_], BF16, kind=SK).ap()
    v_s = nc.dram_tensor("v_s", [NTOK_, D], BF16, kind=SK).ap()
    oT_s = nc.dram_tensor("oT_s", [D, NOWN_], BF16, kind=SK).ap()
    wsc = nc.dram_tensor("wsc", [58, 128, 8192], BF16, kind="Internal").ap()

    with ExitStack() as st:
        ARENA_W = 53000
        arena_t = st.enter_context(nc.sbuf_tensor("arena", [128, ARENA_W], F32))
        ar = Arena(arena_t[:, :], ARENA_W * 4)
        ps = [st.enter_context(nc.psum_tensor("ps%d" % i, [128, 512], F32)) for i in range(8)]
        S = Sched(nc)
        uid = [0]

        def U(p):
            uid[0] += 1
            return "%s#%d" % (p, uid[0])

        tabs = ar.alloc((T_END,), F32)
        S.add("sp", lambda e: e.dma_start(out=tabs, in_=tabs_d), writes=["tabs"], dma=True)
        ident = ar.alloc((128,), BF16)
        S.add("dve", lambda e: e.tensor_copy(out=ident, in_=tabs[:, T_ID:T_ID + 128]), reads=["tabs"], writes=["cst"])
        ones_b = ar.alloc((128,), BF16)
        S.add("pool", lambda e: e.memset(ones_b, 1.0), writes=["cst1"])
        ones_f = ar.alloc((128,), F32)
        S.add("pool", lambda e: e.memset(ones_f, 1.0), writes=["cst2"])
        gsub = ar.alloc((256,), F32)
        S.add("dve", lambda e: e.tensor_scalar(out=gsub, in0=tabs[:, T_GSUB:T_GSUB + 256], scalar1=1.0 - LAMBDA_INIT,
                                              scalar2=None, op0=ALU.mult), reads=["tabs"], writes=["cst3"])
        lt = ar.alloc((128,), F32)
        lsum = ar.alloc((2,), F32)
        nlam = ar.alloc((1,), F32)
        for i in range(2):
            a0 = T_LAM + i * 256
            S.add("dve", lambda e, a0=a0: e.tensor_tensor(out=lt, in0=tabs[:, a0:a0 + 128], in1=tabs[:, a0 + 128:a0 + 256],
                                                          op=ALU.mult), reads=["tabs"], writes=["lt"])
            S.add("dve", lambda e, i=i: e.tensor_reduce(out=lsum[:, i:i + 1], in_=lt, axis=mybir.AxisListType.X, op=ALU.add),
                  reads=["lt"], writes=["lsum"])
        S.add("act", lambda e: e.activation(out=lsum, in_=lsum, func=AF.Exp), reads=["lsum"], writes=["lsum"])
        S.add("dve", lambda e: e.tensor_tensor(out=nlam, in0=lsum[:, 1:2], in1=lsum[:, 0:1], op=ALU.subtract),
              reads=["lsum"], writes=["nlam"])
        S.add("dve", lambda e: e.tensor_scalar(out=nlam, in0=nlam, scalar1=-LAMBDA_INIT, scalar2=None, op0=ALU.add),
              reads=["nlam"], writes=["nlam"])
        SPL = ar.alloc((NKT, 8), F32)
        dummy = ar.alloc((8,), F32)
        ev = [0]

        def evac(out, in_, reads, writes):
            ev[0] += 1
            if ev[0] % 2:
                S.add("dve", lambda e: e.tensor_copy(out=out, in_=in_), reads=reads, writes=writes)
            else:
                S.add("act", lambda e: e.copy(out=out, in_=in_), reads=reads, writes=writes)

        bank_rr = [0]

        def next_bank():
            b = 2 + bank_rr[0] % 6
            bank_rr[0] += 1
            return b

        xT_v = xT.rearrange("(kc p) t -> p kc t", p=128)

        def rms_stats(src, n, sq, ksrc, eps=1e-6):
            for q_ in range(4):
                S.add("act", lambda e, q_=q_: e.activation(out=sq[:, 4 * q_:4 * q_ + 4, 0:n], in_=src[:, 4 * q_:4 * q_ + 4, :], func=AF.Square),
                      reads=[ksrc], writes=["sq", "sq_q%d" % q_])
            for kc in range(KC):
                S.add("pe", lambda e, kc=kc: e.matmul(ps[0][0:1, 0:n], lhsT=ones_b[:, 0:1], rhs=sq[:, kc, 0:n],
                                                      start=(kc == 0), stop=(kc == KC - 1)),
                      reads=["sq_q%d" % (kc // 4), "cst1"], writes=["ps0"])
            S.add("act", lambda e: e.activation(out=rs_row[0:1, 0:n], in_=ps[0][0:1, 0:n], func=AF.Sqrt, bias=eps_t[0:1, 0:1],
                                                scale=1.0 / D), reads=["ps0", "eps"], writes=["rsrow"])
            S.add("dve", lambda e: e.reciprocal(out=rs_row[0:1, 0:n], in_=rs_row[0:1, 0:n]), reads=["rsrow"], writes=["rsrow"])
            S.add("pe", lambda e: e.matmul(ps[1][:, 0:n], lhsT=ones_f[0:1, :], rhs=rs_row[0:1, 0:n], start=True, stop=True),
                  reads=["rsrow", "cst2"], writes=["ps1"])

        rs_row = ar.alloc((512,), F32)
        eps_t = ar.alloc((2,), F32)
        S.add("pool", lambda e: e.memset(eps_t[:, 0:1], 1e-6), writes=["eps"])
        S.add("pool", lambda e: e.memset(eps_t[:, 1:2], 1e-5), writes=["eps"])

        def norm_apply(src, n, gofs, dst, ksrc, kdst):
            for kc in range(KC):
                S.add("dve", lambda e, kc=kc: e.scalar_tensor_tensor(
                    out=dst[:, kc, 0:n], in0=src[:, kc, 0:n], scalar=tabs[:, T_G + gofs + kc:T_G + gofs + kc + 1],
                    in1=ps[1][:, 0:n], op0=ALU.mult, op1=ALU.mult), reads=[ksrc, "ps1", "tabs"],
                      writes=[("uT3_%d" % kc) if kdst == "uT3" else kdst])

        def load_slab(dst, w_ap, r0, nk, c0, ncol, key):
            src = w_ap[r0:r0 + nk * 128, c0:c0 + ncol].rearrange("(kc p) c -> p kc c", p=128)
            S.add("pool", lambda e: e.dma_start(out=dst[:, 0:nk, 0:ncol], in_=src), writes=[key], dma=True)

        ar.mark()
        SBT = min(1024, NOWN_)
        uTs = [ar.alloc((KC, SBT), BF16) for _ in range(2)]
        xs2 = [ar.alloc((KC, 512), F32) for _ in range(2)]
        sq = ar.alloc((KC, 512), BF16)
        slabs = [ar.alloc((KC, 512), BF16) for _ in range(2)]
        wz = ar.alloc((KC, 8), BF16)
        stg = [ar.alloc((512,), BF16) for _ in range(6)]
        ztmp = ar.alloc((8,), F32)
        load_slab(wz, w_in, 0, KC, 6144, 8, "wz")
        sbs = [(i * SBT, SBT, True) for i in range(NOWN_ // SBT)] + [(NOWN_ + i * SBT, SBT, False) for i in range(NOWN_ // SBT)] + [(2 * NOWN_, 128, False)]
        xi = [0]
        si = [0]
        gi = [0]
        out_dmas = []
        slab_list = []
        for s in range(2):
            slab_list.append((1024 + s * 512, "KT", s * 4))
        for s in range(2):
            slab_list.append((4096 + s * 512, "KT", 8 + s * 4))
        for s in range(2):
            slab_list.append((2048 + s * 512, "V", s * 512))
        for s in range(2):
            slab_list.append((5120 + s * 512, "V", 1024 + s * 512))
        for s in range(2):
            slab_list.append((0 + s * 512, "QT", s * 4))
        for s in range(2):
            slab_list.append((3072 + s * 512, "QT", 8 + s * 4))
        def norm_stages(sbi):
            t0, nt, own = sbs[sbi]
            uT = uTs[sbi % 2]
            ku = "uT%d" % (sbi % 2)
            nch = (nt + 511) // 512
            st = {}
            for c in range(nch):
                n = min(512, nt - c * 512)
                xs = xs2[xi[0] % 2]
                kx = "xs%d" % (xi[0] % 2)
                xi[0] += 1

                def f_load(xs=xs, n=n, a=t0 + c * 512, kx=kx):
                    S.add("sp", lambda e: e.dma_start(out=xs[:, :, 0:n], in_=xT_v[:, :, a:a + n]), writes=[kx], dma=True)

                def f_sq(xs=xs, n=n, kx=kx):
                    S.add("act", lambda e: e.activation(out=sq[:, :, 0:n], in_=xs[:, :, 0:n], func=AF.Square), reads=[kx], writes=["sq"])

                def f_stats(n=n):
                    for kc in range(KC):
                        S.add("pe", lambda e, kc=kc: e.matmul(ps[0][0:1, 0:n], lhsT=ones_b[:, 0:1], rhs=sq[:, kc, 0:n],
                                                              start=(kc == 0), stop=(kc == KC - 1)), reads=["sq", "cst1"], writes=["ps0"])
                    S.add("act", lambda e: e.activation(out=rs_row[0:1, 0:n], in_=ps[0][0:1, 0:n], func=AF.Sqrt, bias=eps_t[0:1, 0:1],
                                                        scale=1.0 / D), reads=["ps0", "eps"], writes=["rsrow"])
                    S.add("dve", lambda e: e.reciprocal(out=rs_row[0:1, 0:n], in_=rs_row[0:1, 0:n]), reads=["rsrow"], writes=["rsrow"])

                def f_apply(xs=xs, n=n, c=c, kx=kx):
                    S.add("pe", lambda e: e.matmul(ps[1][:, 0:n], lhsT=ones_f[0:1, :], rhs=rs_row[0:1, 0:n], start=True, stop=True),
                          reads=["rsrow", "cst2"], writes=["ps1"])
                    norm_apply(xs, n, 0, uT[:, :, c * 512:c * 512 + n], kx, ku)

                st.setdefault(1, []).append((3, f_load))
                st.setdefault(2 + c, []).append((2, f_sq))
                st.setdefault(3 + c, []).append((1, f_stats))
                st.setdefault(4 + c, []).append((0, f_apply))
            return {k_: [f for _, f in sorted(v_, key=lambda t_: t_[0])] for k_, v_ in st.items()}

        for stage_, fs_ in sorted(norm_stages(0).items()):
            for f_ in fs_:
                f_()
        for sbi, (t0, nt, own) in enumerate(sbs):
            nch = (nt + 511) // 512
            uT = uTs[sbi % 2]
            ku = "uT%d" % (sbi % 2)
            slab_n = [0]
            pend = [{}]
            for tt in range(nt // 128):
                b = next_bank()
                for kc in range(KC):
                    S.add("pe", lambda e, uT=uT, kc=kc, tt=tt, b=b: e.matmul(ps[b][:, 0:8], lhsT=uT[:, kc, tt * 128:(tt + 1) * 128],
                                                                      rhs=wz[:, kc, :], start=(kc == 0), stop=(kc == KC - 1)),
                          reads=[ku, "wz"], writes=["ps%d" % b])
                tile_idx = (t0 // 128 + tt) if t0 < 2 * NOWN_ else META
                S.add("dve", lambda e, b=b: e.tensor_tensor(out=ztmp, in0=ps[b][:, 0:8], in1=tabs[:, T_BF:T_BF + 8], op=ALU.add),
                      reads=["ps%d" % b, "tabs"], writes=["ztmp"])
                S.add("act", lambda e: e.activation(out=ztmp, in_=ztmp, func=AF.Exp, scale=-1.0), reads=["ztmp"], writes=["ztmp"])
                S.add("act", lambda e, ti=tile_idx: e.activation(out=SPL[:, ti, :], in_=ztmp, func=AF.Ln, bias=ones_f[:, 0:1], scale=1.0),
                      reads=["ztmp", "cst2"], writes=["SPL"])
            for (c0, kind, idx) in slab_list:
                if kind == "QT" and not own:
                    continue
                if sbi + 1 < len(sbs):
                    if slab_n[0] == 1:
                        pend[0] = norm_stages(sbi + 1)
                    for f_ in pend[0].get(slab_n[0], []):
                        f_()
                slab_n[0] += 1
                sl = slabs[si[0] % 2]
                ks = "slab%d" % (si[0] % 2)
                si[0] += 1
                load_slab(sl, w_in, 0, KC, c0, 512, ks)
                if kind in ("KT", "QT"):
                    dst_s = kT_s if kind == "KT" else qT_s
                    for ct in range(4):
                        for c in range(nch):
                            n = min(512, nt - c * 512)
                            b = next_bank()
                            for kc in range(KC):
                                S.add("pe", lambda e, uT=uT, kc=kc, ct=ct, c=c, n=n, b=b, sl=sl: e.matmul(
                                    ps[b][:, 0:n], lhsT=sl[:, kc, ct * 128:(ct + 1) * 128], rhs=uT[:, kc, c * 512:c * 512 + n],
                                    start=(kc == 0), stop=(kc == KC - 1)), reads=[ku, ks], writes=["ps%d" % b])
                            sg = stg[gi[0] % 6]
                            kg = "stg%d" % (gi[0] % 6)
                            gi[0] += 1
                            evac(sg[:, 0:n], ps[b][:, 0:n], ["ps%d" % b], [kg])
                            S.add("sp", lambda e, sg=sg, n=n, m=idx + ct, a=t0 + c * 512, dst_s=dst_s: e.dma_start(
                                out=dst_s[m, :, a:a + n], in_=sg[:, 0:n]), reads=[kg], writes=[U("scr")], dma=True)
                else:
                    for tt in range(nt // 128):
                        b = next_bank()
                        for kc in range(KC):
                            S.add("pe", lambda e, uT=uT, kc=kc, tt=tt, b=b, sl=sl: e.matmul(
                                ps[b][:, :], lhsT=uT[:, kc, tt * 128:(tt + 1) * 128], rhs=sl[:, kc, :],
                                start=(kc == 0), stop=(kc == KC - 1)), reads=[ku, ks], writes=["ps%d" % b])
                        sg = stg[gi[0] % 6]
                        kg = "stg%d" % (gi[0] % 6)
                        gi[0] += 1
                        evac(sg, ps[b][:, :], ["ps%d" % b], [kg])
                        S.add("sp", lambda e, sg=sg, a=t0 + tt * 128, idx=idx: e.dma_start(
                            out=v_s[a:a + 128, idx:idx + 512], in_=sg), reads=[kg], writes=[U("scr")], dma=True)
        ar.release()
        S.fence(dummy)

        ar.mark()
        Wc = ar.alloc((NKT, 8), F32)
        A = ar.alloc((NKT, 8), F32)
        tot = ar.alloc((8,), F32)
        totb = ar.alloc((128,), F32)
        CB = ar.alloc((NT, 8), F32)
        trif = tabs[:, T_TRI:T_TRI + 128]
        self64 = tabs[:, T_SEL:T_SEL + 128]
        pref = tabs[0:NKT, T_PREF:T_PREF + NKT]
        S.add("dve", lambda e: e.tensor_scalar(out=SPL[:, META, :], in0=SPL[:, META, :], scalar1=tabs[:, T_RM:T_RM + 1], scalar2=None,
                                              op0=ALU.mult), reads=["SPL", "tabs"], writes=["SPL"])
        SPLf = SPL.rearrange("p a b -> p (a b)")
        Wcf = Wc.rearrange("p a b -> p (a b)")
        for c0_ in range(0, NKT * 8, 512):
            n_ = min(512, NKT * 8 - c0_)
            S.add("pe", lambda e, c0_=c0_, n_=n_: e.matmul(ps[2][:, 0:n_], lhsT=trif, rhs=SPLf[:, c0_:c0_ + n_], start=True, stop=True),
                  reads=["SPL", "tabs"], writes=["ps2"])
            S.add("dve", lambda e, c0_=c0_, n_=n_: e.tensor_copy(out=Wcf[:, c0_:c0_ + n_], in_=ps[2][:, 0:n_]), reads=["ps2"], writes=["Wc"])
        for h in range(8):
            S.add("pe", lambda e, h=h: e.matmul(ps[4][0:NKT, h:h + 1], lhsT=SPL[:, :, h], rhs=ones_f[:, 0:1], start=True, stop=True),
                  reads=["SPL", "cst2"], writes=["ps4"])
        S.add("dve", lambda e: e.tensor_copy(out=tot[0:NKT, :], in_=ps[4][0:NKT, 0:8]), reads=["ps4"], writes=["tot"])
        for h in range(8):
            S.add("dve", lambda e, h=h: e.tensor_scalar(out=totb[0:NKT, :], in0=ones_f[0:NKT, :], scalar1=tot[0:NKT, h:h + 1], scalar2=None,
                                                         op0=ALU.mult), reads=["tot", "cst2"], writes=["totb"])
            S.add("pe", lambda e: e.matmul(ps[5][:, 0:NKT], lhsT=totb[0:NKT, :], rhs=pref, start=True, stop=True),
                  reads=["totb", "tabs"], writes=["ps5"])
            S.add("dve", lambda e, h=h: e.tensor_tensor(out=A[:, :, h], in0=Wc[:, :, h], in1=ps[5][:, 0:NKT], op=ALU.add),
                  reads=["Wc", "ps5"], writes=["A"])
        Af = A.rearrange("p a b -> p (a b)")
        S.add("pe", lambda e: e.matmul(ps[6][:, 0:NT * 8], lhsT=self64, rhs=Af[:, 0:NT * 8], start=True, stop=True), reads=["A", "tabs"], writes=["ps6"])
        S.add("dve", lambda e: e.tensor_copy(out=CB.rearrange("p a b -> p (a b)"), in_=ps[6][:, 0:NT * 8]), reads=["ps6"], writes=["CB"])

        wreg = {}
        wlist = []
        for sgp in range(4):
            wlist += [(w_in, 0, KC, 6152 + sgp * 512, 512), (w_a, 0, 8, sgp * 512, 512),
                      (w_in, 0, KC, 8200 + sgp * 512, 512), (w_f, 0, 8, sgp * 512, 512)]
        for sgp in range(4):
            wlist.append((w_o, 0, KC, sgp * 512, 512))
        for sgp in range(11):
            wlist += [(w_g, 0, KC, sgp * 512, 512), (w_u, 0, KC, sgp * 512, 512)]
        for ct in range(16):
            wlist.append((w_d, 0, FC, ct * 128, 128))
        for idx, (w_ap, r0, nk, c0, ncol) in enumerate(wlist):
            wreg[(w_ap.tensor.name, r0, nk, c0, ncol)] = idx
            src = w_ap[r0:r0 + nk * 128, c0:c0 + ncol].rearrange("(kc p) c -> p kc c", p=128)
            dst = wsc[idx][:, 0:nk * ncol].rearrange("p (a b) -> p a b", a=nk)
            S.add("pool", lambda e, src=src, dst=dst: e.dma_start(out=dst, in_=src), writes=[U("wsc")], dma=True)
        KTb = [ar.alloc((NTOK_,), BF16) for _ in range(3)]
        QTb = [ar.alloc((NOWN_,), BF16) for _ in range(3)]
        q_i = [0]
        head_no = [0]
        Vb = [ar.alloc((NKT, 257), BF16) for _ in range(2)]
        oTh = ar.alloc((2, NOWN_), BF16)
        dtmp = [ar.alloc((128,), F32) for _ in range(2)]
        fin4 = [ar.alloc((8,), F32) for _ in range(4)]
        ofp4 = [ar.alloc((256,), F32) for _ in range(4)]
        junk = ar.alloc((256,), F32)
        obf4 = [ar.alloc((256,), BF16) for _ in range(4)]
        kq_i = [0]
        v_i = [0]
        pt_i = [0]
        dt_i = [0]
        ss_i = [0]
        acc_i = [0]
        bf_i = [0]
        v_sv = v_s.rearrange("(kt p) c -> p kt c", p=128)

        NPB = 8
        PB = [ar.alloc((512,), BF16) for _ in range(NPB)]
        ewD = ar.alloc((96,), F32)
        bFa = ar.alloc((NT, NKT), F32)
        o0 = ar.alloc((4, 256), F32)
        NG = NT // 4

        def attention_head(is_diff, h):
            nmap = 2 if is_diff else 1
            ncol = 256 if is_diff else 128
            maps = [2 * h, 2 * h + 1] if is_diff else [8 + h]
            vcol = h * 256 if is_diff else 1024 + h * 128
            orow = h * 256 if is_diff else 1024 + h * 128
            KT, QT, kk, kqk = [], [], [], []
            for m in maps:
                i = kq_i[0] % 3
                kq_i[0] += 1
                S.add("sp", lambda e, i=i, m=m: e.dma_start(out=KTb[i], in_=kT_s[m]), writes=["KT%d" % i], dma=True)
                iq = q_i[0] % 3
                q_i[0] += 1
                S.add("sp", lambda e, iq=iq, m=m: e.dma_start(out=QTb[iq], in_=qT_s[m]), writes=["QT%d" % iq], dma=True)
                KT.append(KTb[i]); QT.append(QTb[iq]); kk.append("KT%d" % i); kqk.append("QT%d" % iq)
            vi = v_i[0] % 2
            v_i[0] += 1
            V = Vb[vi]
            kv = "V%d" % vi
            for g0 in range(0, NKT, 13):
                g1 = min(NKT, g0 + 13)
                S.add("sp", lambda e, g0=g0, g1=g1: e.dma_start(out=V[:, g0:g1, 0:ncol], in_=v_sv[:, g0:g1, vcol:vcol + ncol]),
                      writes=[kv], dma=True)
            S.add("dve", lambda e: e.memset(V[:, :, ncol:ncol + 1], 1.0), writes=[kv])
            pool_ok = head_no[0] >= 2
            head_no[0] += 1
            if is_diff:
                S.add("act", lambda e: e.activation(out=ewD, in_=tabs[:, T_TABD + h * 96:T_TABD + (h + 1) * 96], func=AF.Exp),
                      reads=["tabs"], writes=["ew"])
            else:
                for j in range(NT):
                    S.add("dve", lambda e, j=j: e.tensor_scalar(out=bFa[:, j, :], in0=A[:, :, h], scalar1=CB[:, j, h:h + 1], scalar2=60.0,
                                                                 op0=ALU.subtract, op1=ALU.min), reads=["A", "CB"], writes=["ew"])
                    S.add("dve", lambda e, j=j: e.tensor_tensor(out=bFa[:, j, NT + j:NT + j + 1], in0=bFa[:, j, NT + j:NT + j + 1],
                                                                 in1=tabs[:, T_FO:T_FO + 1], op=ALU.add), reads=["ew", "tabs"], writes=["ew"])
                S.add("act", lambda e: e.activation(out=bFa, in_=bFa, func=AF.Exp), reads=["ew"], writes=["ew"])

            def ew_ap(G, kind, i, b0, nb):
                j0 = 4 * G + b0
                if is_diff:
                    if kind == "own":
                        c0 = j0 - i
                    elif kind == "oth":
                        c0 = 32 + j0 - i
                    else:
                        c0 = 64 + j0
                    return ewD[:, c0:c0 + nb]
                kt = i if kind == "own" else (NT + i if kind == "oth" else META)
                return bFa[:, j0:j0 + nb, kt]

            items = []
            for G in range(NG):
                for c in range(nmap):
                    lst = [dict(kt=META, kp=16, kind="meta", i=0, b0=0)]
                    for i in range(4 * G):
                        lst.append(dict(kt=i, kp=128, kind="own", i=i, b0=0))
                        lst.append(dict(kt=NT + i, kp=128, kind="oth", i=i, b0=0))
                    for a in range(4):
                        lst.append(dict(kt=NT + 4 * G + a, kp=128, kind="oth", i=4 * G + a, b0=a))
                        lst.append(dict(kt=4 * G + a, kp=128, kind="own", i=4 * G + a, b0=a, diag=True))
                    if is_diff:
                        slope = 2.0 ** (-8.0 * (h + 1) / 4)
                        keep = []
                        for it in lst:
                            if it["kind"] == "meta" or it.get("diag"):
                                keep.append(it)
                                continue
                            dl = 4 * G + it["b0"] - it["i"]
                            mx = slope * (127 - 256 * dl + (128 if it["kind"] == "oth" else 0))
                            if mx > -120.0:
                                keep.append(it)
                        lst = keep
                    for n_, it in enumerate(lst):
                        it.update(G=G, c=c, first=(n_ == 0), last=(n_ == len(lst) - 1))
                        items.append(it)
                    items.append(dict(T=True, G=G, c=c))
            par = [0]

            def emit_st(n, it):
                if it.get("T"):
                    return
                bank = n % 3
                G, c, kt, kp, b0 = it["G"], it["c"], it["kt"], it["kp"], it["b0"]
                q0 = (4 * G + b0) * 128
                q1 = (4 * G + 4) * 128
                S.add("pe", lambda e: e.matmul(ps[bank][0:kp, b0 * 128:512], lhsT=KT[c][:, kt * 128:kt * 128 + kp], rhs=QT[c][:, q0:q1],
                                               start=True, stop=True), reads=[kk[c], kqk[c]], writes=["ps%d" % bank])

            def emit_rest(n, it):
                if it.get("T"):
                    return
                bank = n % 3
                kbank = "ps%d" % bank
                P = PB[n % NPB]
                kPb = ["PB%d_%d" % (n % NPB, b_) for b_ in range(4)]
                kP = kPb[0]
                G, c, kt, kp, b0, kind = it["G"], it["c"], it["kt"], it["kp"], it["b0"], it["kind"]
                c0 = b0 * 128
                diag = it.get("diag", False)
                if diag:
                    di = dt_i[0] % 2
                    dt_i[0] += 1
                    dtm = dtmp[di]
                    kd = "dt%d" % di
                    mk = tabs[:, T_DG + h * 128:T_DG + (h + 1) * 128] if is_diff else tabs[:, T_FM:T_FM + 128]
                    S.add("dve", lambda e: e.scalar_tensor_tensor(out=dtm, in0=ps[bank][:, c0:c0 + 128], scalar=SCALE, in1=mk,
                                                                  op0=ALU.mult, op1=ALU.add), reads=["tabs"], writes=[kd, kbank])
                    S.add("act", lambda e: e.activation(out=P[:, c0:c0 + 128], in_=dtm, func=AF.Exp), reads=[kd], writes=[kPb[b0]])
                    if c0 + 128 < 512:
                        S.add("act", lambda e: e.activation(out=P[:, c0 + 128:512], in_=ps[bank][:, c0 + 128:512], func=AF.Exp, scale=SCALE),
                              reads=[kbank], writes=kPb[b0 + 1:4])
                else:
                    S.add("act", lambda e: e.activation(out=P[0:kp, c0:512], in_=ps[bank][0:kp, c0:512], func=AF.Exp, scale=SCALE),
                          reads=[kbank], writes=kPb[b0:4])
                sb0 = b0 + 1 if (diag and is_diff) else b0
                nb = 4 - sb0
                if nb > 0:
                    ew = ew_ap(G, kind, it["i"], sb0, nb)[0:kp].unsqueeze(2).broadcast_to([kp, nb, 128])
                    Pv = P[0:kp, sb0 * 128:512].rearrange("p (a b) -> p a b", a=nb)
                    par[0] += 1
                    eng_ = "pool" if (par[0] % 2 == 0 and pool_ok) else "dve"
                    if eng_ == "pool":
                        S.add(eng_, lambda e: e.tensor_tensor(out=Pv, in0=Pv, in1=ew, op=ALU.mult), reads=["ew"], writes=kPb[sb0:4])
                    else:
                        ew2 = ew_ap(G, kind, it["i"], sb0, nb)
                        for bb in range(nb):
                            S.add("dve", lambda e, bb=bb: e.tensor_scalar(out=P[0:kp, (sb0 + bb) * 128:(sb0 + bb + 1) * 128],
                                                                         in0=P[0:kp, (sb0 + bb) * 128:(sb0 + bb + 1) * 128],
                                                                         scalar1=ew2[0:kp, bb:bb + 1], scalar2=None, op0=ALU.mult),
                                  reads=["ew"], writes=[kPb[sb0 + bb]])

            def emit_pv(n, it):
                if it.get("T"):
                    finalize_pass(n, it["G"], it["c"])
                    return
                P = PB[n % NPB]
                kPb = ["PB%d_%d" % (n % NPB, b_) for b_ in range(4)]
                G, c, kt, kp, b0 = it["G"], it["c"], it["kt"], it["kp"], it["b0"]
                diag = it.get("diag", False)
                for b in range(b0, 4):
                    S.add("pe", lambda e, b=b: e.matmul(ps[4 + b][:, 0:ncol + 1], lhsT=P[0:kp, b * 128:(b + 1) * 128], rhs=V[0:kp, kt, 0:ncol + 1],
                                                        start=it["first"], stop=(it["last"] or (b == b0 and diag))),
                          reads=[kPb[b], kv], writes=["ps%d" % (4 + b)])

            def finalize_pass(n, G, c):
                bank = 3
                kbank = "ps%d" % bank
                tb = ps[bank][:, :].bitcast(BF16)
                nblk = ncol // 128
                for b in range(4):
                    a = 4 + b
                    ka = "ps%d" % a
                    fin, ofp, obf = fin4[b], ofp4[b], obf4[b]
                    kf, ko, kob = "fin%d" % b, "ofp%d" % b, "obf%d" % b
                    if is_diff:
                        S.add("dve", lambda e, a=a, fin=fin: e.reciprocal(out=fin[:, 0:1], in_=ps[a][:, 256:257]), reads=[ka], writes=[kf])
                        if c == 0:
                            S.add("dve", lambda e, a=a, fin=fin, b=b: e.tensor_scalar(out=o0[:, b, :], in0=ps[a][:, 0:256], scalar1=fin[:, 0:1],
                                                                                       scalar2=None, op0=ALU.mult), reads=[ka, kf], writes=["o0_%d" % b])
                            continue
                        S.add("dve", lambda e, fin=fin: e.tensor_tensor(out=fin[:, 2:3], in0=fin[:, 0:1], in1=nlam, op=ALU.mult),
                              reads=[kf, "nlam"], writes=[kf])
                        S.add("dve", lambda e, a=a, fin=fin, ofp=ofp, b=b: e.scalar_tensor_tensor(
                            out=ofp, in0=ps[a][:, 0:256], scalar=fin[:, 2:3], in1=o0[:, b, :], op0=ALU.mult, op1=ALU.add),
                            reads=[ka, kf, "o0_%d" % b], writes=[ko])
                    else:
                        S.add("dve", lambda e, a=a, fin=fin: e.reciprocal(out=fin[:, 0:1], in_=ps[a][:, 128:129]), reads=[ka], writes=[kf])
                        S.add("dve", lambda e, a=a, fin=fin, obf=obf: e.tensor_scalar(out=obf[:, 0:128], in0=ps[a][:, 0:128], scalar1=fin[:, 0:1],
                                                                                       scalar2=None, op0=ALU.mult), reads=[ka, kf], writes=[kob])
                if is_diff and c == 0:
                    return
                if is_diff:
                    for b in range(4):
                        fin, ofp = fin4[b], ofp4[b]
                        kf, ko = "fin%d" % b, "ofp%d" % b
                        S.add("act", lambda e, fin=fin, ofp=ofp: e.activation(out=junk, in_=ofp, func=AF.Square, accum_out=fin[:, 3:4]),
                              reads=[ko], writes=["junk", kf])
                        S.add("act", lambda e, fin=fin: e.activation(out=fin[:, 4:5], in_=fin[:, 3:4], func=AF.Sqrt, bias=eps_t[:, 1:2], scale=1.0 / 256),
                              reads=[kf, "eps"], writes=[kf])
                    for b in range(4):
                        fin, ofp, obf = fin4[b], ofp4[b], obf4[b]
                        kf, ko, kob = "fin%d" % b, "ofp%d" % b, "obf%d" % b
                        S.add("dve", lambda e, fin=fin: e.reciprocal(out=fin[:, 5:6], in_=fin[:, 4:5]), reads=[kf], writes=[kf])
                        S.add("dve", lambda e, fin=fin, ofp=ofp, obf=obf: e.scalar_tensor_tensor(out=obf, in0=ofp, scalar=fin[:, 5:6], in1=gsub,
                                                                                                  op0=ALU.mult, op1=ALU.mult),
                              reads=[ko, kf, "cst3"], writes=[kob])
                for b in range(4):
                    obf = obf4[b]
                    for blk in range(nblk):
                        sl_ = b * nblk + blk
                        S.add("pe", lambda e, obf=obf, blk=blk, sl_=sl_: e.transpose(tb[:, sl_ * 128:(sl_ + 1) * 128], obf[:, blk * 128:(blk + 1) * 128], ident),
                              reads=["obf%d" % b, "cst"], writes=[kbank])
                for b in range(4):
                    j = 4 * G + b
                    src = tb[:, b * nblk * 128:(b + 1) * nblk * 128].rearrange("p (a q) -> p a q", a=nblk)
                    S.add("dve", lambda e, src=src, j=j: e.tensor_copy(out=oTh[:, 0:nblk, j * 128:(j + 1) * 128], in_=src),
                          reads=[kbank], writes=["oTh"])

            LA, LP = 2, 6
            for n in range(len(items) + LP):
                if n < len(items):
                    emit_st(n, items[n])
                if 0 <= n - LA < len(items):
                    emit_rest(n - LA, items[n - LA])
                if n - LP >= 0:
                    emit_pv(n - LP, items[n - LP])
            for blk in range(ncol // 128):
                S.add("sp", lambda e, blk=blk: e.dma_start(out=oT_s[orow + blk * 128:orow + (blk + 1) * 128, :], in_=oTh[:, blk, :]),
                      reads=["oTh"], writes=[U("scr")], dma=True)

        if upto >= 2:
            for hd in range(4):
                attention_head(True, hd)
                attention_head(False, 2 * hd)
                attention_head(False, 2 * hd + 1)
        ar.release()
        S.fence(dummy)

        ar.mark()
        xs = ar.alloc((KC, 512), F32)
        uT3 = ar.alloc((KC, 512), BF16)
        sq3 = ar.alloc((KC, 512), BF16)
        aT = ar.alloc((FC, 512), BF16)
        oTc = aT[:, 0:KC, :]
        mrg = sq3
        gA = ar.alloc((512,), BF16)
        gF = ar.alloc((512,), BF16)
        m1 = ar.alloc((512,), F32)
        m2 = ar.alloc((512,), F32)
        sgt = ar.alloc((512,), F32)
        NSL = 4
        sl3 = [ar.alloc((KC * 512,), BF16) for _ in range(NSL)]
        s3 = [0]

        def get_slab(w_ap, r0, nk, c0, ncol):
            i = s3[0] % NSL
            s3[0] += 1
            v = sl3[i][:, 0:nk * ncol].rearrange("p (a b) -> p a b", a=nk)
            idx = wreg[(w_ap.tensor.name, r0, nk, c0, ncol)]
            S.add("sp", lambda e, i=i, idx=idx, n_=nk * ncol: e.dma_start(out=sl3[i][:, 0:n_], in_=wsc[idx][:, 0:n_]),
                  writes=["sl3_%d" % i], dma=True)
            return v, "sl3_%d" % i

        oT_v = oT_s.rearrange("(kc p) t -> p kc t", p=128)
        outT_v = outT.rearrange("(kc p) t -> p kc t", p=128)

        def mm_group(b, lhs_fn, rhs_fn, nk, reads):
            for kc in range(nk):
                S.add("pe", lambda e, kc=kc, l_=lhs_fn(kc), r_=rhs_fn(kc): e.matmul(ps[b][:, :], lhsT=l_, rhs=r_, start=(kc == 0), stop=(kc == nk - 1)),
                      reads=[("uT3_%d" % kc) if r_k == "uT3" else r_k for r_k in reads], writes=["ps%d" % b])

        for ch in range(NOWN_ // 512 if upto >= 3 else 0):
            a0 = ch * 512
            S.add("sp", lambda e, a0=a0: e.dma_start(out=xs, in_=xT_v[:, :, a0:a0 + 512]), writes=["xs"], dma=True)
            S.add("sp", lambda e, a0=a0: e.dma_start(out=oTc, in_=oT_v[:, :, a0:a0 + 512]), writes=["aT"], dma=True)
            rms_stats(xs, 512, sq3, "xs")
            norm_apply(xs, 512, 0, uT3, "xs", "uT3")
            for sgp in range(4):
                wga, kga = get_slab(w_in, 0, KC, 6152 + sgp * 512, 512)
                wa_, kwa = get_slab(w_a, 0, 8, sgp * 512, 512)
                for ct in range(4):
                    b = next_bank()
                    mm_group(b, lambda kc, ct=ct: wga[:, kc, ct * 128:(ct + 1) * 128], lambda kc: uT3[:, kc, :], KC, ["uT3", kga])
                    S.add("act", lambda e, b=b: e.activation(out=gA, in_=ps[b][:, :], func=AF.Sigmoid), reads=["ps%d" % b], writes=["gA"])
                    b2 = next_bank()
                    mm_group(b2, lambda kc, ct=ct: wa_[:, kc, ct * 128:(ct + 1) * 128], lambda kc: oTc[:, kc, :], 8, ["aT", kwa])
                    S.add("dve", lambda e, b2=b2, ct=ct, sgp=sgp: e.tensor_tensor(out=mrg[:, sgp * 4 + ct, :], in0=ps[b2][:, :], in1=gA, op=ALU.mult),
                          reads=["ps%d" % b2, "gA"], writes=["sq"])
                wgf, kgf = get_slab(w_in, 0, KC, 8200 + sgp * 512, 512)
                wf_, kwf = get_slab(w_f, 0, 8, sgp * 512, 512)
                for ct in range(4):
                    b = next_bank()
                    mm_group(b, lambda kc, ct=ct: wgf[:, kc, ct * 128:(ct + 1) * 128], lambda kc: uT3[:, kc, :], KC, ["uT3", kgf])
                    S.add("act", lambda e, b=b: e.activation(out=gF, in_=ps[b][:, :], func=AF.Sigmoid), reads=["ps%d" % b], writes=["gF"])
                    b2 = next_bank()
                    mm_group(b2, lambda kc, ct=ct: wf_[:, kc, ct * 128:(ct + 1) * 128], lambda kc: oTc[:, 8 + kc, :], 8, ["aT", kwf])
                    S.add("dve", lambda e, b2=b2: e.tensor_tensor(out=m2, in0=ps[b2][:, :], in1=gF, op=ALU.mult),
                          reads=["ps%d" % b2, "gF"], writes=["m2"])
                    S.add("dve", lambda e, ct=ct, sgp=sgp: e.tensor_tensor(out=mrg[:, sgp * 4 + ct, :], in0=mrg[:, sgp * 4 + ct, :], in1=m2, op=ALU.add),
                          reads=["m2", "sq"], writes=["sq"])
            for sgp in range(4):
                wo_, kwo = get_slab(w_o, 0, KC, sgp * 512, 512)
                for ct in range(4):
                    b = next_bank()
                    mm_group(b, lambda kc, ct=ct: wo_[:, kc, ct * 128:(ct + 1) * 128], lambda kc: mrg[:, kc, :], KC, ["sq", kwo])
                    S.add("dve", lambda e, b=b, ct=ct, sgp=sgp: e.tensor_tensor(out=xs[:, sgp * 4 + ct, :], in0=xs[:, sgp * 4 + ct, :], in1=ps[b][:, :], op=ALU.add),
                          reads=["ps%d" % b, "xs"], writes=["xs"])
            rms_stats(xs, 512, sq3, "xs")
            norm_apply(xs, 512, 16, uT3, "xs", "uT3")
            for sgp in range(11):
                wg_, kwg = get_slab(w_g, 0, KC, sgp * 512, 512)
                wu_, kwu = get_slab(w_u, 0, KC, sgp * 512, 512)
                for ct in range(4):
                    b = next_bank()
                    mm_group(b, lambda kc, ct=ct: wg_[:, kc, ct * 128:(ct + 1) * 128], lambda kc: uT3[:, kc, :], KC, ["uT3", kwg])
                    S.add("act", lambda e, b=b: e.activation(out=sgt, in_=ps[b][:, :], func=AF.Silu), reads=["ps%d" % b], writes=["sgt"])
                    b2 = next_bank()
                    mm_group(b2, lambda kc, ct=ct: wu_[:, kc, ct * 128:(ct + 1) * 128], lambda kc: uT3[:, kc, :], KC, ["uT3", kwu])
                    S.add("dve", lambda e, b2=b2, ct=ct, sgp=sgp: e.tensor_tensor(out=aT[:, sgp * 4 + ct, :], in0=ps[b2][:, :], in1=sgt, op=ALU.mult),
                          reads=["ps%d" % b2, "sgt"], writes=["aT"])
            for ct in range(16):
                b = next_bank()
                wd_, kwd = get_slab(w_d, 0, FC, ct * 128, 128)
                for kc in range(FC):
                    S.add("pe", lambda e, kc=kc, b=b, wd_=wd_: e.matmul(ps[b][:, :], lhsT=wd_[:, kc, :], rhs=aT[:, kc, :], start=(kc == 0), stop=(kc == FC - 1)),
                          reads=["aT", kwd], writes=["ps%d" % b])
                S.add("dve", lambda e, b=b, ct=ct: e.tensor_tensor(out=xs[:, ct, :], in0=xs[:, ct, :], in1=ps[b][:, :], op=ALU.add),
                      reads=["ps%d" % b, "xs"], writes=["xs"])
            rms_stats(xs, 512, sq3, "xs")
            norm_apply(xs, 512, 32, xs, "xs", "xs")
            od = S.add("sp", lambda e, a0=a0: e.dma_start(out=outT_v[:, :, a0:a0 + 512], in_=xs), reads=["xs"], dma=True)
            out_dmas.append(od)
        ar.release()
        fz = S.fence(dummy)
        S.finish(out_dmas + [fz])
        S.emit(st)
    return nc


_CACHE = {}


def _tables(half, NT=32):
    t = np.zeros((128, T_END), np.float32)
    p = np.arange(128, dtype=np.float64)
    slopes = [2.0 ** (-8.0 * (i + 1) / 4) for i in range(4)]
    for h in range(4):
        s = slopes[h]
        for dl in range(32):
            t[:, T_TABD + h * 96 + dl] = s * (p - 256 * dl)
            if half == 0:
                t[:, T_TABD + h * 96 + 32 + dl] = NEG if dl == 0 else s * (p - 256 * dl + 128)
            else:
                t[:, T_TABD + h * 96 + 32 + dl] = s * (p - 256 * dl - 128)
            t[:, T_TABD + h * 96 + 64 + dl] = s * (p - 16 - 128 * (2 * dl + half))
        k = p[:, None]
        q = p[None, :]
        vis = (k // 64) <= (q // 64)
        t[:, T_DG + h * 128:T_DG + (h + 1) * 128] = np.where(vis, -s * np.abs(q - k) + s * q, NEG)
    k = p[:, None]
    q = p[None, :]
    t[:, T_FM:T_FM + 128] = np.where(k <= q, 0.0, NEG)
    t[:, T_FO] = 0.0 if half == 1 else NEG
    t[:, T_RM] = (p < 16)
    NKT = 2 * NT + 1
    rank = np.zeros(NKT)
    rank[2 * NT] = 0
    for T in range(2 * NT):
        idx = T // 2 + (0 if T % 2 == half else NT)
        rank[idx] = T + 1
    t[0:NKT, T_PREF:T_PREF + NKT] = (rank[:, None] < rank[None, :])
    t[:, T_TRI:T_TRI + 128] = (k <= q)
    t[:, T_SEL:T_SEL + 128] = (k == 64)
    t[:, T_ID:T_ID + 128] = (k == q)
    return t


def kernel(x, meta, g_mix, w_in, lambda_q1, lambda_k1, lambda_q2, lambda_k2, g_subln, b_f,
           w_branch_a, w_branch_f, w_out, g_ffn, w_gate, w_up, w_down, g_final):
    x = np.asarray(x, np.float32)
    f = lambda a: np.ascontiguousarray(np.asarray(a, np.float32))
    if "nc" not in _CACHE:
        _CACHE["nc"] = build_program()
    nc = _CACHE["nc"]
    w_in0, w_a0, w_f0, w_o0 = f(w_in[0]), f(w_branch_a[0]), f(w_branch_f[0]), f(w_out[0])
    w_g0, w_u0, w_d0 = f(w_gate[0]), f(w_up[0]), f(w_down[0])
    meta = f(meta)
    in_maps = []
    for c in range(8):
        b, half = c // 2, c % 2
        xb = x[b].reshape(64, 128, D)
        toks = np.concatenate([xb[half::2].reshape(NOWN, D), xb[1 - half::2].reshape(NOWN, D), meta,
                               np.zeros((112, D), np.float32)], axis=0)
        t = _tables(half)
        gv = np.concatenate([f(g_mix[0]).reshape(16, 128).T, f(g_ffn[0]).reshape(16, 128).T, f(g_final).reshape(16, 128).T], axis=1)
        t[:, T_G:T_G + 48] = gv
        t[:, T_GSUB:T_GSUB + 256] = f(g_subln[0])[None, :]
        t[:, T_BF:T_BF + 8] = f(b_f[0])[None, :]
        t[:, T_LAM:T_LAM + 512] = np.concatenate([f(lambda_q1[0]), f(lambda_k1[0]), f(lambda_q2[0]), f(lambda_k2[0])])[None, :]
        in_maps.append({"xT": np.ascontiguousarray(toks.T), "tabs": t, "w_in": w_in0, "w_a": w_a0, "w_f": w_f0, "w_o": w_o0,
                        "w_g": w_g0, "w_u": w_u0, "w_d": w_d0})
    res = run_bass_kernel_spmd(nc, in_maps, core_ids=list(range(8)))
    out = np.zeros((4, SEQ, D), np.float32)
    for c in range(8):
        b, half = c // 2, c % 2
        o = np.asarray(res.results[c]["outT"], np.float32).T.reshape(32, 128, D)
        out[b].reshape(64, 128, D)[half::2] = o
    return out
```
